# Optimizing a Trainium2 kernel written in Bass

```python
import math
import jax, jax.numpy as jnp
from jax import lax
import numpy as np

D_MODEL = 1024
BATCH = 4
SEQ = 4096
DEPTH = 2

POOL_WINDOWS = (2, 4, 8, 16)
N_POOL_GROUPS = len(POOL_WINDOWS)
POOL_GROUP_DIM = D_MODEL // 8
POOL_DIM = N_POOL_GROUPS * POOL_GROUP_DIM

N_HEADS = D_MODEL // 128
QK_NOPE = 64
QK_ROPE = 32
V_DIM = 64
Q_LORA = 384
KV_LORA = 256
ROPE_THETA = 10000.0
ATTN_DIM = N_HEADS * V_DIM
Q_BLOCK = 128

D_FF = 4 * D_MODEL

N_MOD = 6
EPS = 1e-6

IN_SPLITS = (POOL_DIM, Q_LORA, KV_LORA, QK_ROPE, D_MODEL, D_MODEL)
D_IN = sum(IN_SPLITS)

kernel_name = "hybrid_pool_mla_gated_adaln"


def rms_norm(x, g):
    xf = x.astype(jnp.float32)
    y = xf * lax.rsqrt(jnp.mean(xf * xf, axis=-1, keepdims=True) + EPS)
    return y.astype(x.dtype) * g


def apply_rope(x, cos, sin):
    half = x.shape[-1] // 2
    x1, x2 = x[..., :half], x[..., half:]
    return jnp.concatenate([x1 * cos - x2 * sin, x2 * cos + x1 * sin], axis=-1)


def pool_mixer(u, w_pool, pool_scale):
    B, S, _ = u.shape
    t = jnp.arange(S)
    outs = []
    for g, w in enumerate(POOL_WINDOWS):
        ug = u[..., g * POOL_GROUP_DIM:(g + 1) * POOL_GROUP_DIM].astype(jnp.float32)
        cs = jnp.cumsum(ug, axis=1)
        lag = jnp.pad(cs, ((0, 0), (w, 0), (0, 0)))[:, :S]
        cnt = jnp.minimum(t + 1, w).astype(jnp.float32)[None, :, None]
        outs.append(((cs - lag) / cnt - ug).astype(u.dtype))
    p = jnp.stack(outs, axis=2)
    y = jnp.einsum('bsgc,gcd->bsgd', p, w_pool).reshape(B, S, POOL_DIM)
    return y * pool_scale


def mla(c_q_raw, c_kv_raw, k_rope_raw, q_norm_g, w_uq, kv_norm_g, w_uk, w_uv, cos, sin):
    B, S, _ = c_q_raw.shape
    c_q = rms_norm(c_q_raw, q_norm_g)
    q = jnp.einsum('bsr,rhd->bshd', c_q, w_uq)
    q_nope = q[..., :QK_NOPE]
    q_rope = apply_rope(q[..., QK_NOPE:], cos[:, :, None, :], sin[:, :, None, :])
    c_kv = rms_norm(c_kv_raw, kv_norm_g)
    k_nope = jnp.einsum('bsr,rhd->bhsd', c_kv, w_uk)
    v = jnp.einsum('bsr,rhd->bhsd', c_kv, w_uv)
    k_rope = apply_rope(k_rope_raw, cos, sin)

    nb = S // Q_BLOCK
    def to_blocks(a):
        d = a.shape[-1]
        return a.reshape(B, nb, Q_BLOCK, N_HEADS, d).transpose(1, 0, 3, 2, 4)
    qn_b = to_blocks(q_nope)
    qr_b = to_blocks(q_rope)
    starts = jnp.arange(nb, dtype=jnp.int32) * Q_BLOCK
    key_pos = jnp.arange(S, dtype=jnp.int32)
    scale = 1.0 / math.sqrt(QK_NOPE + QK_ROPE)
    neg = jnp.finfo(jnp.float32).min

    def attend(args):
        qn, qr, start = args
        s = (jnp.einsum('bhqd,bhkd->bhqk', qn, k_nope)
             + jnp.einsum('bhqd,bkd->bhqk', qr, k_rope)).astype(jnp.float32) * scale
        q_pos = start + jnp.arange(Q_BLOCK, dtype=jnp.int32)
        mask = q_pos[:, None] >= key_pos[None, :]
        p = jax.nn.softmax(jnp.where(mask, s, neg), axis=-1)
        return jnp.einsum('bhqk,bhkd->bhqd', p.astype(v.dtype), v)

    o = lax.map(attend, (qn_b, qr_b, starts))
    return o.transpose(1, 0, 3, 2, 4).reshape(B, S, ATTN_DIM)


def setup_inputs(seed: int = 0) -> dict:
    key = jax.random.key(seed)
    ks = jax.random.split(key, 24)
    f32 = jnp.float32

    def dense(k, shape, fan_in, mult=1.0):
        return jax.random.normal(k, shape, f32) * (mult * fan_in ** -0.5)

    def gain(k, shape):
        return 1.0 + 0.02 * jax.random.normal(k, shape, f32)

    x = jax.random.normal(ks[0], (BATCH, SEQ, D_MODEL), f32)
    c = jax.random.normal(ks[1], (BATCH, D_MODEL), f32)
    offsets = jax.random.randint(ks[2], (BATCH, 1), 0, 1024, dtype=jnp.int32)
    positions = offsets + jnp.arange(SEQ, dtype=jnp.int32)[None, :]
    return {
        "x": x,
        "c": c,
        "positions": positions,
        "ln1_g": gain(ks[3], (DEPTH, D_MODEL)),
        "ln2_g": gain(ks[4], (DEPTH, D_MODEL)),
        "w_ada": dense(ks[5], (DEPTH, D_MODEL, N_MOD * D_MODEL), D_MODEL, 0.5),
        "b_ada": 0.01 * jax.random.normal(ks[6], (DEPTH, N_MOD * D_MODEL), f32),
        "w_in": dense(ks[7], (DEPTH, D_MODEL, D_IN), D_MODEL),
        "q_norm_g": gain(ks[8], (DEPTH, Q_LORA)),
        "w_uq": dense(ks[9], (DEPTH, Q_LORA, N_HEADS, QK_NOPE + QK_ROPE), Q_LORA),
        "kv_norm_g": gain(ks[10], (DEPTH, KV_LORA)),
        "w_uk": dense(ks[11], (DEPTH, KV_LORA, N_HEADS, QK_NOPE), KV_LORA),
        "w_uv": dense(ks[12], (DEPTH, KV_LORA, N_HEADS, V_DIM), KV_LORA),
        "w_pool": dense(ks[13], (DEPTH, N_POOL_GROUPS, POOL_GROUP_DIM, POOL_GROUP_DIM), POOL_GROUP_DIM),
        "pool_scale": gain(ks[14], (DEPTH, POOL_DIM)),
        "p_pool": dense(ks[15], (DEPTH, POOL_DIM, D_MODEL), POOL_DIM),
        "p_attn": dense(ks[16], (DEPTH, ATTN_DIM, D_MODEL), ATTN_DIM),
        "w_out": dense(ks[17], (DEPTH, D_MODEL, D_MODEL), D_MODEL),
        "w_ff1": dense(ks[18], (DEPTH, D_MODEL, D_FF), D_MODEL),
        "w_ff2": dense(ks[19], (DEPTH, D_FF, D_MODEL), D_FF),
        "final_g": gain(ks[20], (D_MODEL,)),
    }


def reference(x, c, positions, ln1_g, ln2_g, w_ada, b_ada, w_in, q_norm_g, w_uq,
              kv_norm_g, w_uk, w_uv, w_pool, pool_scale, p_pool, p_attn, w_out,
              w_ff1, w_ff2, final_g):
    inv_freq = ROPE_THETA ** (-jnp.arange(0, QK_ROPE, 2, dtype=jnp.float32) / QK_ROPE)
    ang = positions.astype(jnp.float32)[..., None] * inv_freq
    cos = jnp.cos(ang).astype(x.dtype)
    sin = jnp.sin(ang).astype(x.dtype)
    c_act = jax.nn.silu(c)
    cuts = np.cumsum(IN_SPLITS)[:-1].tolist()

    for l in range(DEPTH):
        mod = c_act @ w_ada[l] + b_ada[l]
        sh1, sc1, g1, sh2, sc2, g2 = [m[:, None, :] for m in jnp.split(mod, N_MOD, axis=-1)]

        h = rms_norm(x, ln1_g[l]) * (1.0 + sc1) + sh1
        z = h @ w_in[l]
        u_pool, c_q_raw, c_kv_raw, k_rope_raw, gz_a, gz_b = jnp.split(z, cuts, axis=-1)
        y_a = pool_mixer(u_pool, w_pool[l], pool_scale[l]) @ p_pool[l]
        y_b = mla(c_q_raw, c_kv_raw, k_rope_raw, q_norm_g[l], w_uq[l], kv_norm_g[l],
                  w_uk[l], w_uv[l], cos, sin) @ p_attn[l]
        merged = jax.nn.sigmoid(gz_a) * y_a + jax.nn.sigmoid(gz_b) * y_b
        x = x + g1 * (merged @ w_out[l])

        h2 = rms_norm(x, ln2_g[l]) * (1.0 + sc2) + sh2
        x = x + g2 * (jnp.square(jax.nn.relu(h2 @ w_ff1[l])) @ w_ff2[l])

    return rms_norm(x, final_g)
```

```python
import math
import numpy as np
import ml_dtypes
import concourse.bass as bass
import concourse.mybir as mybir
from concourse.bass_utils import run_bass_kernel_spmd

F32 = mybir.dt.float32
BF16 = mybir.dt.bfloat16
I32 = mybir.dt.int32
U8 = mybir.dt.uint8
ALU = mybir.AluOpType
AF = mybir.ActivationFunctionType

D = 1024
NT = 2048
CTX = 4096
C = 512
NCH = NT // C
GC = 256
DEPTH = 2
NH = 8
EPS = 1e-6
SCALE = 1.0 / math.sqrt(96.0)
FFG = 8
ESZ = {F32: 4, BF16: 2, I32: 4, U8: 1}

SV_LN1, SV_LN2, SV_BADA, SV_QNG, SV_KVG, SV_PSC, SV_FING = 0, 16, 32, 128, 134, 138, 146
SV_INVF, SV_FLAG, SV_INVCNT, SV_C = 154, 155, 156, 220
NV = 228
ARENA = 207 * 1024


class View:
    __slots__ = ("ap", "space", "iv")

    def __init__(self, ap, space, iv):
        self.ap = ap
        self.space = space
        self.iv = iv


class T:
    def __init__(self, base_ap, space, off, shape, dt):
        self.shape = tuple(shape)
        self.dt = dt
        self.space = space
        self.off = off
        es = ESZ[dt]
        self.es = es
        free = self.shape[1:]
        n = int(np.prod(free))
        self.nbytes = n * es
        strides = []
        s = 1
        for d in reversed(free):
            strides.append(s)
            s *= d
        self.strides = list(reversed(strides))
        if space == "sb":
            ap = base_ap[:, off:off + n * es]
            if dt != U8:
                ap = ap.bitcast(dt)
        else:
            ap = base_ap[:, :]
        if len(free) == 2:
            ap = ap.rearrange("p (a b) -> p a b", a=free[0])
        elif len(free) == 3:
            ap = ap.rearrange("p (a b c) -> p a b c", a=free[0], b=free[1])
        self.full = ap

    def v(self, *idx, p=None):
        free = self.shape[1:]
        idx = list(idx) + [slice(None)] * (len(free) - len(idx))
        lo = 0
        hi = 0
        key = []
        for i, d, st in zip(idx, free, self.strides):
            if isinstance(i, int):
                a, b = i, i + 1
                key.append(i)
            else:
                a = 0 if i.start is None else i.start
                b = d if i.stop is None else i.stop
                key.append(slice(a, b))
            assert 0 <= a < b <= d, (idx, self.shape)
            lo += a * st
            hi += (b - 1) * st
        p0, p1 = (0, self.shape[0]) if p is None else p
        assert p1 <= self.shape[0]
        ap = self.full[(slice(p0, p1),) + tuple(key)]
        return View(ap, self.space, (p0, p1, self.off + lo * self.es, self.off + (hi + 1) * self.es))


class Sched:
    EPOCH = 12000

    def __init__(self, nc, es):
        self.nc = nc
        self.es = es
        self.eng = {"pe": nc.tensor, "act": nc.scalar, "dve": nc.vector, "pool": nc.gpsimd, "sp": nc.sync}
        self.streams = {e: [] for e in self.eng}
        self.cnt = {e: 0 for e in self.eng}
        self.psems = {e: [] for e in self.eng}
        self.waited = {e: {} for e in self.eng}
        self.recs = {}
        self.dsem = {}
        self.pbank = {}
        self.nsem = 0

    def new_sem(self, name):
        self.nsem += 1
        return self.es.enter_context(self.nc.semaphore(name))

    def _ptok(self, e):
        self.cnt[e] += 1
        n = self.cnt[e]
        ei = (n - 1) // self.EPOCH
        while len(self.psems[e]) <= ei:
            self.psems[e].append(self.new_sem(f"p_{e}_{len(self.psems[e])}"))
        return ("p", e, n)

    def _tok_wait(self, tok):
        if tok[0] == "p":
            _, e, n = tok
            ei = (n - 1) // self.EPOCH
            return self.psems[e][ei], n - ei * self.EPOCH
        _, key, val = tok
        return self.dsem[key][0], val

    @staticmethod
    def _ov(a, b):
        return a[0] < b[1] and b[0] < a[1] and a[2] < b[3] and b[2] < a[3]

    @staticmethod
    def _cov(a, b):
        return a[0] <= b[0] and a[1] >= b[1] and a[2] <= b[2] and a[3] >= b[3]

    def _deps(self, reads, writes, e=None):
        deps = []
        for v in reads:
            if v.space[0] == "ps":
                st = self.pbank.get(v.space)
                if st is not None and st[0] != e:
                    deps.append(st[1])
                continue
            for r in self.recs.get(v.space, ()):
                if r[1] == "W" and self._ov(r[0], v.iv):
                    deps.append(r[2])
        for v in writes:
            if v.space[0] == "ps":
                st = self.pbank.get(v.space)
                if st is not None and st[0] != e:
                    deps.append(st[1])
                continue
            for r in self.recs.get(v.space, ()):
                if self._ov(r[0], v.iv):
                    deps.append(r[2])
        return deps

    def _record(self, reads, writes, tok):
        ps = [v for v in list(reads) + list(writes) if v.space[0] == "ps"]
        for v in ps:
            self.pbank[v.space] = (tok[1], tok)
        reads = [v for v in reads if v.space[0] != "ps"]
        writes = [v for v in writes if v.space[0] != "ps"]
        for v in writes:
            lst = self.recs.setdefault(v.space, [])
            lst[:] = [r for r in lst if not self._cov(v.iv, r[0])]
            lst.append([v.iv, "W", tok])
        for v in reads:
            lst = self.recs.setdefault(v.space, [])
            lst[:] = [r for r in lst if not (r[1] == "R" and r[2][0] == "p" and tok[0] == "p"
                                             and r[2][1] == tok[1] and self._cov(v.iv, r[0]))]
            lst.append([v.iv, "R", tok])

    def _waits(self, e, deps):
        best = {}
        for tok in deps:
            if tok[0] == "p":
                if tok[1] == e and e in ("pe", "sp"):
                    continue
                k = ("p", tok[1])
                val = tok[2]
            else:
                k = ("d", tok[1])
                val = tok[2]
            if self.waited[e].get(k, 0) >= val:
                continue
            if best.get(k, (0, None))[0] < val:
                best[k] = (val, tok)
        out = []
        for k, (val, tok) in best.items():
            self.waited[e][k] = val
            out.append(self._tok_wait(tok))
        return out

    def op(self, e, fn, reads=(), writes=(), sig=True):
        deps = self._deps(reads, writes, e)
        waits = self._waits(e, deps)
        if sig:
            tok = self._ptok(e)
            sem, val = self._tok_wait(tok)
            self._record(reads, writes, tok)
        else:
            tok = None
            sem = None
        self.streams[e].append((waits, fn, sem, False))
        return tok

    def dma(self, q, fns, key, reads=(), writes=()):
        deps = self._deps(reads, writes)
        waits = self._waits(q, deps)
        if key not in self.dsem:
            self.dsem[key] = [self.new_sem("d_" + key), 0]
        ds = self.dsem[key]
        ds[1] += 16 * len(fns)
        tok = ("d", key, ds[1])
        self._record(reads, writes, tok)
        first = True
        for fn in fns:
            self.streams[q].append((waits if first else [], fn, ds[0], True))
            first = False
        return tok

    def custom(self, e, fn, reads=(), writes=(), key=None, inc=1):
        deps = self._deps(reads, writes)
        waits = self._waits(e, deps)
        if key not in self.dsem:
            self.dsem[key] = [self.new_sem("c_" + key), 0]
        ds = self.dsem[key]
        ds[1] += inc
        tok = ("d", key, ds[1])
        self._record(reads, writes, tok)
        self.streams[e].append((waits, fn, ds[0], "cc"))
        return tok

    def wait_tok(self, e, tok):
        ws = self._waits(e, [tok])
        if ws:
            self.streams[e].append((ws, None, None, False))

    def emit(self, block):
        def run(e, eng):
            for waits, fn, sem, kind in self.streams[e]:
                for s, v in waits:
                    eng.wait_ge(s, v)
                if fn is None:
                    continue
                inst = fn(eng)
                if sem is not None:
                    if kind is True:
                        inst.then_inc(sem, 16)
                    elif kind == "cc":
                        inst.then_inc(sem)
                    else:
                        inst.then_inc(sem, 1)

        @block.tensor
        def _(eng):
            run("pe", eng)

        @block.scalar
        def _(eng):
            run("act", eng)

        @block.vector
        def _(eng):
            run("dve", eng)

        @block.gpsimd
        def _(eng):
            run("pool", eng)

        @block.sync
        def _(eng):
            run("sp", eng)


def build_nc(debug=False):
    from contextlib import ExitStack
    nc = bass.Bass("TRN2", target_bir_lowering=False)
    dr = {}

    def din(name, shape, dt):
        dr[name] = nc.dram_tensor(name, list(shape), dt, kind="ExternalInput").ap()
        return dr[name]

    xT_d = din("xT", [D, NT], F32)
    pos_d = din("pos", [1, NT], I32)
    sv_d = din("smallv", [128, NV], F32)
    tri_d = din("tri", [128, 128], BF16)
    wada_d = din("w_ada", [DEPTH, D, 6 * D], F32)
    win_d = din("w_in", [DEPTH, D, 3232], F32)
    wuq_d = din("w_uq", [DEPTH, 384, 768], F32)
    wuk_d = din("w_uk", [DEPTH, 256, 512], F32)
    wuv_d = din("w_uv", [DEPTH, 256, 512], F32)
    wpool_d = din("w_pool", [DEPTH, 4, 128, 128], F32)
    ppool_d = din("p_pool", [DEPTH, 512, D], F32)
    pattn_d = din("p_attn", [DEPTH, 512, D], F32)
    wout_d = din("w_out", [DEPTH, D, D], F32)
    wff1_d = din("w_ff1", [DEPTH, D, 4 * D], F32)
    wff2_d = din("w_ff2", [DEPTH, 4 * D, D], F32)
    out_d = nc.dram_tensor("outT", [D, NT], F32, kind="ExternalOutput").ap()

    exs = [nc.dram_tensor(f"exs{l}", [288, NT], BF16) for l in range(DEPTH)]
    exd = [nc.dram_tensor(f"exd{l}", [576, NT], BF16) for l in range(DEPTH)]
    hls = [nc.dram_tensor(f"hls{l}", [128, 64], F32) for l in range(DEPTH)]
    hld = [nc.dram_tensor(f"hld{l}", [256, 64], F32) for l in range(DEPTH)]
    tabs = nc.dram_tensor("tabs", [256, NT], F32)
    if debug:
        dbg_q = nc.dram_tensor("dbg_q", [128, NH * NT], BF16, kind="ExternalOutput").ap()
        dbg_ckv = nc.dram_tensor("dbg_ckv", [128, 2 * CTX], BF16, kind="ExternalOutput").ap()
        dbg_kr = nc.dram_tensor("dbg_kr", [128, CTX], BF16, kind="ExternalOutput").ap()
        dbg_ao = nc.dram_tensor("dbg_ao", [128, 4 * NT], BF16, kind="ExternalOutput").ap()
        dbg_cos = nc.dram_tensor("dbg_cos", [128, 2 * NT], F32, kind="ExternalOutput").ap()
        dbg_pat = nc.dram_tensor("dbg_pat", [128, 4 * D], BF16, kind="ExternalOutput").ap()
        dbg_mrg = nc.dram_tensor("dbg_mrg", [128, 8 * GC], BF16, kind="ExternalOutput").ap()

    with ExitStack() as es:
        arena = es.enter_context(nc.sbuf_tensor("arena", [128, ARENA], U8))
        banks = [es.enter_context(nc.psum_tensor(f"bank{i}", [128, 512], F32)) for i in range(8)]
        S = Sched(nc, es)
        block = es.enter_context(nc.Block())

        def sb(off, shape, dt):
            t = T(arena, "sb", off, shape, dt)
            assert off + t.nbytes <= ARENA, (off, shape)
            return t

        PB = [T(banks[i], ("ps", i), 0, [128, 512], F32) for i in range(8)]

        def dview(handle, name):
            return View(None, ("dram", name), (0, 1, 0, 1))

        xT = sb(0, [128, 8, NT], F32)
        o = 65536
        smallv = sb(o, [128, NV], F32); o += NV * 4
        modT = sb(o, [128, DEPTH, 48], F32); o += DEPTH * 48 * 4
        avec = sb(o, [128, DEPTH, 2, 8], F32); o += DEPTH * 16 * 4
        cact = sb(o, [128, 8], BF16); o += 16
        ones_b = sb(o, [128, 128], BF16); o += 256
        tri = sb(o, [128, 128], BF16); o += 256
        epsT = sb(o, [128, 1], F32); o += 4
        zeroT = sb(o, [128, 1], F32); o += 4
        ones_f = sb(o, [128, 128], F32); o += 512
        sel_f = sb(o, [128, 128], F32); o += 512
        flag16 = sb(o, [128, 16], BF16); o += 32
        halo_sb = sb(o, [128, 4, 16], F32); o += 256
        halo_in = sb(o, [128, 4, 16], F32); o += 256
        assert o <= 69632, o
        P0 = 69632
        R_AO = P0
        R_CTX = R_AO + 16384
        R_Q = R_CTX + 24576
        R_X = R_Q + 32768
        assert R_X == 143360

        cosT = sb(R_AO, [128, NT], F32)
        sinT = sb(R_AO + 8192, [128, NT], F32)
        attn_o = sb(R_AO, [128, 4, NT], BF16)
        ctx_ckv = sb(R_CTX, [128, 2, CTX], BF16)
        ctx_kr = sb(R_CTX + 16384, [128, CTX], BF16)
        qT_all = sb(R_Q, [128, NH, NT], BF16)

        def svc(col, n=1):
            return smallv.v(slice(col, col + n))

        bank_rr = [0]

        def next_bank(pool=(0, 1, 2, 3, 4, 5, 6, 7)):
            b = pool[bank_rr[0] % len(pool)]
            bank_rr[0] += 1
            return b

        def mm_group(out, pairs, extra_reads=()):
            n = len(pairs)
            allreads = [v for pr in pairs for v in pr] + list(extra_reads)
            for i, (l, r) in enumerate(pairs):
                last = i == n - 1
                fn = (lambda l=l, r=r, i=i, last=last: lambda e: e.matmul(out.ap, l.ap, r.ap, start=(i == 0), stop=last))()
                if last:
                    S.op("pe", fn, reads=allreads, writes=[out])
                else:
                    if i == 0:
                        S.op("pe", fn, reads=allreads, writes=[out], sig=False)
                    else:
                        S.op("pe", fn, sig=False)

        def act(out, in_, func, scale=1.0, bias=None, extra_reads=()):
            rd = [in_] + list(extra_reads)
            kw = {}
            if bias is not None:
                kw["bias"] = bias.ap
                rd.append(bias)
            if isinstance(scale, View):
                rd.append(scale)
                sc = scale.ap
            else:
                sc = scale
            S.op("act", lambda e: e.activation(out=out.ap, in_=in_.ap, func=func, scale=sc, **kw),
                 reads=rd, writes=[out])

        def tt(eng, out, a, b, op):
            S.op(eng, lambda e: e.tensor_tensor(out=out.ap, in0=a.ap, in1=b.ap, op=op), reads=[a, b], writes=[out])

        def ts(eng, out, a, s1, op0, s2=None, op1=None):
            rd = [a]
            s1v = s1.ap if isinstance(s1, View) else s1
            s2v = s2.ap if isinstance(s2, View) else s2
            if isinstance(s1, View):
                rd.append(s1)
            if isinstance(s2, View):
                rd.append(s2)
            if op1 is None:
                S.op(eng, lambda e: e.tensor_scalar(out=out.ap, in0=a.ap, scalar1=s1v, scalar2=None, op0=op0),
                     reads=rd, writes=[out])
            else:
                S.op(eng, lambda e: e.tensor_scalar(out=out.ap, in0=a.ap, scalar1=s1v, scalar2=s2v, op0=op0, op1=op1),
                     reads=rd, writes=[out])

        def stt(out, a, s, b, op0, op1):
            rd = [a, b]
            sv = s.ap if isinstance(s, View) else s
            if isinstance(s, View):
                rd.append(s)
            S.op("dve", lambda e: e.scalar_tensor_tensor(out=out.ap, in0=a.ap, scalar=sv, in1=b.ap, op0=op0, op1=op1),
                 reads=rd, writes=[out])

        def copy(eng, out, in_):
            if eng == "act":
                S.op("act", lambda e: e.copy(out=out.ap, in_=in_.ap), reads=[in_], writes=[out])
            else:
                S.op(eng, lambda e: e.tensor_copy(out=out.ap, in_=in_.ap), reads=[in_], writes=[out])

        def memset(eng, out, val):
            S.op(eng, lambda e: e.memset(out.ap, val), writes=[out])

        def wload(dst, src_ap, key):
            S.dma("pool", [lambda e: e.dma_start(out=dst.ap, in_=src_ap)], key, writes=[dst])

        def rstd_from(stat_ps, out, n, tmp):
            act(tmp, stat_ps, AF.Ln, scale=1.0 / n, bias=epsT.v())
            act(out, tmp, AF.Exp, scale=-0.5)

        S.dma("sp", [lambda e: e.dma_start(out=smallv.v().ap, in_=sv_d)], "smallv", writes=[smallv.v()])
        S.dma("sp", [lambda e: e.dma_start(out=tri.v().ap, in_=tri_d)], "tri", writes=[tri.v()])
        xTv = xT_d.rearrange("(k p) t -> p k t", p=128)
        for c in range(NCH):
            dst = xT.v(slice(None), slice(c * C, (c + 1) * C))
            S.dma("sp", [(lambda c=c, dst=dst: lambda e: e.dma_start(out=dst.ap, in_=xTv[:, :, c * C:(c + 1) * C]))()],
                  f"x{c}", writes=[dst])
        posi = sb(R_CTX, [128, NT], I32)
        S.dma("sp", [lambda e: e.dma_start(out=posi.v().ap, in_=pos_d.partition_broadcast(128)[:, 0, :])],
              "pos", writes=[posi.v()])

        Wkr0 = sb(R_X + 4096, [128, 8, 96], BF16)
        Wkrr0 = sb(R_X + 4096 + 1536, [128, 8, 96], BF16)
        memset("dve", Wkr0.v(), 0.0)
        memset("dve", Wkrr0.v(), 0.0)
        memset("dve", ones_b.v(), 1.0)
        memset("dve", epsT.v(), EPS)
        memset("dve", zeroT.v(), 0.0)
        memset("dve", ones_f.v(), 1.0)
        memset("dve", sel_f.v(), 0.0)
        memset("dve", sel_f.v(slice(64, 128)), 1.0)
        memset("dve", flag16.v(), 1.0)
        ts("dve", flag16.v(), flag16.v(), svc(SV_FLAG), ALU.mult)

        act(cact.v(), svc(SV_C, 8), AF.Silu)

        PI = math.pi
        ang = sb(R_Q, [128, NT], F32)
        t1 = sb(R_Q + 8192, [128, NT], F32)
        t2 = sb(R_Q + 16384, [128, NT], F32)
        ki = sb(R_Q + 24576, [128, NT], I32)
        copy("dve", ang.v(), posi.v())
        ts("dve", ang.v(), ang.v(), svc(SV_INVF), ALU.mult)
        ts("dve", t1.v(), ang.v(), 1.0 / (2 * PI), ALU.mult)
        copy("dve", ki.v(), t1.v())
        copy("dve", t1.v(), ki.v())
        C1 = 6.28125
        C2 = 2 * PI - C1
        stt(t2.v(), t1.v(), -C1, ang.v(), ALU.mult, ALU.add)
        stt(t2.v(), t1.v(), -C2, t2.v(), ALU.mult, ALU.add)

        def wrap(r, tmp):
            ts("dve", tmp, r, PI, ALU.is_gt)
            stt(r, tmp, -2 * PI, r, ALU.mult, ALU.add)
            ts("dve", tmp, r, -PI, ALU.is_lt)
            stt(r, tmp, 2 * PI, r, ALU.mult, ALU.add)

        wrap(t2.v(), t1.v())
        act(sinT.v(), t2.v(), AF.Sin)
        ts("dve", t2.v(), t2.v(), PI / 2, ALU.add)
        wrap(t2.v(), t1.v())
        act(cosT.v(), t2.v(), AF.Sin)
        tabs_v = dview(tabs, "tabs")
        if debug:
            S.dma("sp", [lambda e: e.dma_start(out=dbg_cos[:, 0:NT], in_=cosT.v().ap),
                         lambda e: e.dma_start(out=dbg_cos[:, NT:2 * NT], in_=sinT.v().ap)],
                  "dbg0", reads=[cosT.v(), sinT.v()], writes=[View(None, ("dram", "dbg0"), (0, 1, 0, 1))])
        S.dma("sp", [lambda e: e.dma_start(out=tabs.ap()[0:128, :], in_=cosT.v().ap),
                     lambda e: e.dma_start(out=tabs.ap()[128:256, :], in_=sinT.v().ap)],
              "tabs_st", reads=[cosT.v(), sinT.v()], writes=[tabs_v])

        wada_b = [sb(R_CTX + 8192, [128, 8, D], BF16), sb(R_X + 30720, [128, 8, D], BF16)]
        for j in range(2):
            wb = wada_b[j]
            src = wada_d[0].rearrange("(k p) n -> p k n", p=128)[:, :, j * D:(j + 1) * D]
            wload(wb.v(), src, f"wada{j}")
            bk = PB[next_bank()]
            for m in range(8):
                mm_group(bk.v(slice(m, m + 1)),
                         [(wb.v(k, slice(m * 128, (m + 1) * 128)), cact.v(slice(k, k + 1))) for k in range(8)])
            tt("dve", modT.v(0, slice(j * 8, (j + 1) * 8)), bk.v(slice(0, 8)), svc(SV_BADA + j * 8, 8), ALU.add)
        stt(avec.v(0, 0), modT.v(0, slice(8, 16)), 1.0, svc(SV_LN1, 8), ALU.add, ALU.mult)
        wpc = sb(R_X + 51200, [128, 8, 512], BF16)
        mod_pieces = [(0, q) for q in range(4, 12)] + [(1, q) for q in range(12)]

        def mod_piece_step(lq, banks):
            ll, q = lq

            def f():
                src = wada_d[ll].rearrange("(k p) n -> p k n", p=128)[:, :, q * 512:(q + 1) * 512]
                wload(wpc.v(), src, "wpc")
                bk = PB[next_bank(banks)]
                for m in range(4):
                    mm_group(bk.v(slice(m, m + 1)),
                             [(wpc.v(k, slice(m * 128, (m + 1) * 128)), cact.v(slice(k, k + 1))) for k in range(8)])
                tt("dve", modT.v(ll, slice(q * 4, (q + 1) * 4)), bk.v(slice(0, 4)),
                   svc(SV_BADA + ll * 48 + q * 4, 4), ALU.add)
                if q == 3:
                    stt(avec.v(ll, 0), modT.v(ll, slice(8, 16)), 1.0, svc(SV_LN1 + ll * 8, 8), ALU.add, ALU.mult)
                if q == 9:
                    stt(avec.v(ll, 1), modT.v(ll, slice(32, 40)), 1.0, svc(SV_LN2 + ll * 8, 8), ALU.add, ALU.mult)
            return f

        rstd1_all = sb(ARENA - 8192, [128, NT], F32)
        sq0 = sb(R_X + 49152, [128, 8, C], BF16)
        ln0 = sb(R_X + 57344, [128, C], F32)
        for c in range(NCH):
            cs = slice(c * C, (c + 1) * C)
            for k in range(8):
                act(sq0.v(k), xT.v(k, cs), AF.Square)
            bk = PB[next_bank()]
            mm_group(bk.v(), [(ones_b.v(), sq0.v(k)) for k in range(8)])
            act(ln0.v(), bk.v(), AF.Ln, scale=1.0 / D, bias=epsT.v())
            act(rstd1_all.v(cs), ln0.v(), AF.Exp, scale=-0.5)
        def stats_all(rstd_all, sq, lntmp):
            for c in range(NCH):
                cs = slice(c * C, (c + 1) * C)
                for k in range(8):
                    act(sq.v(k), xT.v(k, cs), AF.Square)
                bk = PB[next_bank()]
                mm_group(bk.v(), [(ones_b.v(), sq.v(k)) for k in range(8)])
                rstd_from(bk.v(), rstd_all.v(cs), D, lntmp)

        def h_from(l, which, xv_k, rstd, hv_k, tmpA, tmpB):
            boff = 0 if which == 0 else 24
            for k in range(8):
                tmp = tmpA if k % 2 == 0 else tmpB
                tt("dve", tmp, xv_k(k), rstd, ALU.mult)
                act(hv_k(k), tmp, AF.Identity, scale=avec.v(l, which, slice(k, k + 1)),
                    bias=modT.v(l, slice(boff + k, boff + k + 1)))

        for l in range(DEPTH):
            o = R_X
            Wkv = sb(o, [128, 8, 256], BF16); o += 4096
            Wkr = sb(o, [128, 8, 96], BF16); o += 1536
            Wkrr = sb(o, [128, 8, 96], BF16); o += 1536
            Wq = sb(o, [128, 8, 384], BF16); o += 6144
            wuq = sb(o, [128, 3, 768], BF16); o += 4608
            wuqr = sb(o, [128, 3, 768], BF16); o += 4608
            Wu = sb(o, [128, 8, 512], BF16); o += 8192
            hT = sb(o, [128, 8, C], BF16); o += 8192
            sq = sb(o, [128, 8, C], BF16); o += 8192
            tmpA = sb(o, [128, C], F32); o += 2048
            tmpB = sb(o, [128, C], F32); o += 2048
            rstdq = sb(o, [128, C], F32); o += 2048
            rstdkv = rstdq
            cqn = sb(o, [128, 3, C], BF16); o += 3072
            sql = sb(o, [128, 3, C], BF16); o += 3072
            assert o <= ARENA - 8192, o
            hT2 = [hT, sq]
            rstd1_all = sb(ARENA - 8192, [128, NT], F32)

            winv = win_d[l].rearrange("(k p) n -> p k n", p=128)
            wload(Wkv.v(), winv[:, :, 896:1152], "Wkv")
            if l == 0:
                wload(Wkr.v(slice(None), slice(64, 96)), winv[:, :, 1152:1184], "Wkr")
            wload(Wq.v(), winv[:, :, 512:896], "Wq")
            wload(wuq.v(), wuq_d[l].rearrange("(j p) n -> p j n", p=128), "wuq")
            wload(Wu.v(), winv[:, :, 0:512], "Wu")
            if l > 0:
                memset("dve", Wkr.v(), 0.0)
                memset("dve", Wkrr.v(), 0.0)
                wload(Wkr.v(slice(None), slice(64, 96)), winv[:, :, 1152:1184], "Wkr")
            ts("dve", Wkrr.v(slice(None), slice(64, 80)), Wkr.v(slice(None), slice(80, 96)), -1.0, ALU.mult)
            copy("dve", Wkrr.v(slice(None), slice(80, 96)), Wkr.v(slice(None), slice(64, 80)))
            wuq4 = wuq.full.rearrange("p j (h d) -> p j h d", h=NH)
            wuqr4 = wuqr.full.rearrange("p j (h d) -> p j h d", h=NH)
            memset("dve", wuqr.v(), 0.0)
            S.op("dve", lambda e: e.tensor_scalar(out=wuqr4[:, :, :, 64:80], in0=wuq4[:, :, :, 80:96], scalar1=-1.0,
                                                    scalar2=None, op0=ALU.mult),
                 reads=[wuq.v()], writes=[wuqr.v()])
            S.op("dve", lambda e: e.tensor_copy(out=wuqr4[:, :, :, 80:96], in_=wuq4[:, :, :, 64:80]),
                 reads=[wuq.v()], writes=[wuqr.v()])
            if l > 0:
                S.dma("sp", [lambda e: e.dma_start(out=cosT.v().ap, in_=tabs.ap()[0:128, :]),
                             lambda e: e.dma_start(out=sinT.v().ap, in_=tabs.ap()[128:256, :])],
                      "tabs_ld", reads=[tabs_v], writes=[cosT.v(), sinT.v()])

            for c in range(NCH):
                cs = slice(c * C, (c + 1) * C)
                ls = slice(NT + c * C, NT + (c + 1) * C)
                hT = hT2[c % 2]
                if c == 0:
                    h_from(l, 0, lambda k: xT.v(k, cs), rstd1_all.v(cs), lambda k: hT.v(k), tmpA.v(), tmpB.v())
                bkv = [PB[next_bank()] for _ in range(2)]
                for j in range(2):
                    mm_group(bkv[j].v(), [(Wkv.v(k, slice(j * 128, (j + 1) * 128)), hT.v(k)) for k in range(8)])
                    act(sql.v(j), bkv[j].v(), AF.Square)
                bs = PB[next_bank()]
                mm_group(bs.v(), [(ones_b.v(), sql.v(j)) for j in range(2)])
                rstd_from(bs.v(), rstdkv.v(), 256, tmpA.v())
                for j in range(2):
                    stt(ctx_ckv.v(j, ls), bkv[j].v(), svc(SV_KVG + l * 2 + j), rstdkv.v(), ALU.mult, ALU.mult)
                bkr = PB[next_bank()]
                bkrr = PB[next_bank()]
                mm_group(bkr.v(p=(0, 96)), [(Wkr.v(k), hT.v(k)) for k in range(8)])
                mm_group(bkrr.v(p=(0, 96)), [(Wkrr.v(k), hT.v(k)) for k in range(8)])
                R = (64, 96)
                tt("dve", tmpA.v(p=R), bkr.v(p=R), cosT.v(cs, p=R), ALU.mult)
                tt("dve", tmpB.v(p=R), bkrr.v(p=R), sinT.v(cs, p=R), ALU.mult)
                tt("dve", ctx_kr.v(ls, p=R), tmpA.v(p=R), tmpB.v(p=R), ALU.add)
                bq = [PB[next_bank()] for _ in range(3)]
                for j in range(3):
                    mm_group(bq[j].v(), [(Wq.v(k, slice(j * 128, (j + 1) * 128)), hT.v(k)) for k in range(8)])
                    act(sql.v(j), bq[j].v(), AF.Square)
                bs = PB[next_bank()]
                mm_group(bs.v(), [(ones_b.v(), sql.v(j)) for j in range(3)])
                rstd_from(bs.v(), rstdq.v(), 384, tmpA.v())
                ts("dve", rstdq.v(), rstdq.v(), SCALE, ALU.mult)
                for j in range(3):
                    stt(cqn.v(j), bq[j].v(), svc(SV_QNG + l * 3 + j), rstdq.v(), ALU.mult, ALU.mult)
                Q = (0, 96)
                if c == NCH - 1:
                    bh = PB[next_bank()]
                    for g in range(4):
                        mm_group(bh.v(slice(g * 16, (g + 1) * 16)),
                                 [(Wu.v(k, slice(g * 128, (g + 1) * 128)), hT.v(k, slice(C - 16, C))) for k in range(8)])
                    S.op("dve", lambda e, bh=bh: e.tensor_copy(
                        out=halo_sb.full, in_=bh.full[:, 0:64].rearrange("p (g t) -> p g t", g=4)),
                        reads=[bh.v(slice(0, 64))], writes=[halo_sb.v()])

                if c + 1 < NCH:
                    cs2 = slice((c + 1) * C, (c + 2) * C)
                    hTn = hT2[(c + 1) % 2]
                    h_from(l, 0, lambda k: xT.v(k, cs2), rstd1_all.v(cs2), lambda k: hTn.v(k), tmpA.v(), tmpB.v())
                for h in range(NH):
                    ba = PB[next_bank()]
                    bb = PB[next_bank()]
                    hs = slice(h * 96, (h + 1) * 96)
                    mm_group(ba.v(p=Q), [(wuq.v(j, hs), cqn.v(j)) for j in range(3)])
                    mm_group(bb.v(p=Q), [(wuqr.v(j, hs), cqn.v(j)) for j in range(3)])
                    tt("dve", tmpA.v(p=Q), ba.v(p=Q), cosT.v(cs, p=Q), ALU.mult)
                    tt("dve", tmpB.v(p=Q), bb.v(p=Q), sinT.v(cs, p=Q), ALU.mult)
                    tt("dve", qT_all.v(h, cs, p=Q), tmpA.v(p=Q), tmpB.v(p=Q), ALU.add)
            wuk = sb(R_X, [128, 2, 512], BF16)
            wuv = sb(R_X + 2048, [128, 2, 512], BF16)
            wload(wuk.v(), wuk_d[l].rearrange("(j p) n -> p j n", p=128), "wuk")
            wload(wuv.v(), wuv_d[l].rearrange("(j p) n -> p j n", p=128), "wuv")
            exs_v = dview(exs[l], f"exs{l}")
            exd_v = dview(exd[l], f"exd{l}")
            hls_v = dview(hls[l], f"hls{l}")
            hld_v = dview(hld[l], f"hld{l}")
            own = slice(NT, CTX)
            S.dma("sp", [lambda e, l=l: e.dma_start(out=exs[l].ap()[0:128, :], in_=ctx_ckv.v(0, own).ap),
                         lambda e, l=l: e.dma_start(out=exs[l].ap()[128:256, :], in_=ctx_ckv.v(1, own).ap),
                         lambda e, l=l: e.dma_start(out=exs[l].ap()[256:288, :], in_=ctx_kr.v(own, p=(64, 96)).ap)],
                  f"exst{l}", reads=[ctx_ckv.v(0, own), ctx_ckv.v(1, own), ctx_kr.v(own, p=(64, 96))],
                  writes=[exs_v])
            S.dma("sp", [lambda e, l=l: e.dma_start(out=hls[l].ap(), in_=halo_sb.full.rearrange("p g t -> p (g t)"))],
                  f"hlst{l}", reads=[halo_sb.v()], writes=[hls_v])
            RG = [[0, 1], [2, 3], [4, 5], [6, 7]]
            cc_tok = S.custom("pool", lambda e, l=l: e.collective_compute(
                "AllGather", ALU.bypass, replica_groups=RG, ins=[exs[l].ap().opt()], outs=[exd[l].ap().opt()]),
                reads=[exs_v], writes=[exd_v], key=f"cc{l}")
            S.wait_tok("pool", cc_tok)
            S.custom("pool", lambda e, l=l: e.collective_compute(
                "AllGather", ALU.bypass, replica_groups=RG, ins=[hls[l].ap().opt()], outs=[hld[l].ap().opt()]),
                reads=[hls_v], writes=[hld_v], key=f"cch{l}")
            rem = slice(0, NT)
            for j in range(2):
                S.dma("sp", [(lambda l=l, j=j: lambda e: e.dma_start(out=ctx_ckv.v(j, rem).ap,
                                                                      in_=exd[l].ap()[j * 128:(j + 1) * 128, :]))()],
                      f"exld{l}_{j}", reads=[exd_v], writes=[ctx_ckv.v(j, rem)])
            S.dma("sp", [lambda e, l=l: e.dma_start(out=ctx_kr.v(rem, p=(64, 96)).ap, in_=exd[l].ap()[256:288, :])],
                  f"exldk{l}", reads=[exd_v], writes=[ctx_kr.v(rem, p=(64, 96))])
            S.dma("sp", [lambda e, l=l: e.dma_start(out=halo_in.full.rearrange("p g t -> p (g t)"), in_=hld[l].ap()[0:128, :])],
                  f"exldh{l}", reads=[hld_v], writes=[halo_in.v()])

            if debug and l == 0:
                S.dma("sp", [lambda e: e.dma_start(out=dbg_q, in_=qT_all.full.rearrange("p h t -> p (h t)")),
                             lambda e: e.dma_start(out=dbg_ckv, in_=ctx_ckv.full.rearrange("p j t -> p (j t)")),
                             lambda e: e.dma_start(out=dbg_kr, in_=ctx_kr.full)],
                      "dbg1", reads=[qT_all.v(), ctx_ckv.v(), ctx_kr.v()], writes=[View(None, ("dram", "dbg1"), (0, 1, 0, 1))])
            o = R_X
            o += 4096
            KT = [sb(o + i * 8192, [128, CTX], BF16) for i in range(2)]; o += 16384
            VV = [sb(o + i * 8192, [128, 32, 128], BF16) for i in range(2)]; o += 16384
            NPT = 6
            LA = 4
            PT = [sb(o + i * 1024, [128, C], BF16) for i in range(NPT)]; o += NPT * 1024
            Osb = [sb(o + i * 2048, [128, C], F32) for i in range(2)]; o += 4096
            rdn = [sb(o + i * 2048, [128, C], F32) for i in range(2)]; o += 4096
            assert o <= ARENA
            o = R_CTX
            Wu2 = sb(o, [128, 8, 512], BF16); o += 8192
            Wga = sb(o, [128, 8, D], BF16); o += 16384
            Wgb = sb(o, [128, 8, D], BF16); o += 16384
            wpl = sb(o, [128, 4, 128], BF16); o += 1024
            ppl = sb(o, [128, 4, D], BF16); o += 8192
            pat = sb(o, [128, 4, D], BF16); o += 8192
            wo = sb(o, [128, 8, D], BF16); o += 16384
            G_o = o
            S_BANKS = (0, 1, 2, 3, 4, 7)

            def build_steps(h):
                lower = h < 4
                kt_t = KT[h % 2]
                v_t = VV[h % 2]
                voff = 0 if lower else 64
                vcol = 64 if lower else 0

                def kstep(cc):
                    def f():
                        bk = PB[next_bank(S_BANKS)]
                        ks = slice(cc * C, (cc + 1) * C)
                        mm_group(bk.v(p=(0, 64)),
                                 [(wuk.v(j, slice(h * 64, (h + 1) * 64)), ctx_ckv.v(j, ks)) for j in range(2)])
                        copy("dve", kt_t.v(ks, p=(0, 64)), bk.v(p=(0, 64)))
                    return f

                def krope(part):
                    def f():
                        copy("dve", kt_t.v(part, p=(64, 96)), ctx_kr.v(part, p=(64, 96)))
                    return f

                def vinit():
                    memset("dve", v_t.v(), 0.0)
                    S.op("dve", lambda e: e.tensor_copy(out=v_t.full[:, 0:16, vcol], in_=flag16.full),
                         reads=[flag16.v()], writes=[v_t.v(slice(0, 16))])
                    S.op("dve", lambda e: e.memset(v_t.full[:, 16:32, vcol], 1.0), writes=[v_t.v(slice(16, 32))])

                def vstep(kb):
                    def f():
                        bk = PB[next_bank(S_BANKS)]
                        for i in range(8):
                            kt = kb * 8 + i
                            mm_group(bk.v(slice(i * 64, (i + 1) * 64)),
                                     [(ctx_ckv.v(j, slice(kt * 128, (kt + 1) * 128)), wuv.v(j, slice(h * 64, (h + 1) * 64)))
                                      for j in range(2)])
                        src = bk.full.rearrange("p (i d) -> p i d", i=8)
                        dstap = v_t.full[:, kb * 8:(kb + 1) * 8, voff:voff + 64]
                        if kb < 2:
                            S.op("dve", lambda e: e.tensor_scalar(out=dstap, in0=src, scalar1=svc(SV_FLAG).ap,
                                                                    scalar2=None, op0=ALU.mult),
                                 reads=[bk.v(), svc(SV_FLAG)], writes=[v_t.v(slice(kb * 8, (kb + 1) * 8))])
                        else:
                            S.op("dve", lambda e: e.tensor_copy(out=dstap, in_=src),
                                 reads=[bk.v()], writes=[v_t.v(slice(kb * 8, (kb + 1) * 8))])
                    return f

                return ([vinit] + [kstep(cc) for cc in (4, 5, 6, 7)] + [krope(slice(NT, CTX)), vstep(2), vstep(3)]
                        + [kstep(cc) for cc in (0, 1, 2, 3)] + [krope(slice(0, NT)), vstep(0), vstep(1)])

            for st in build_steps(0):
                st()
            pti = 0
            oi = 0
            items = []
            step_at = {}
            for h in range(NH):
                base = len(items)
                for c in range(NCH):
                    tiles = [(kt, 0) for kt in range(16 + 4 * c)] + [(16 + 4 * c + i, i) for i in range(4)]
                    for ti, (kt, i0) in enumerate(tiles):
                        items.append((h, c, kt, i0, ti, len(tiles)))
                n_h = len(items) - base
                nxt = build_steps(h + 1) if h + 1 < NH else []
                if nxt:
                    gap = n_h // (len(nxt) + 1)
                    for si, st in enumerate(nxt):
                        step_at.setdefault(base + (si + 1) * gap, []).append(st)
                if l == 0:
                    for at in (3, 38, 73):
                        if mod_pieces:
                            step_at.setdefault(base + at, []).append(mod_piece_step(mod_pieces.pop(0), S_BANKS))
                if h == 4:
                    step_at.setdefault(base + 12, []).append(
                        lambda: wload(Wgb.v(), winv[:, :, 2208:3232], "Wgb"))
                if h == 5:
                    step_at.setdefault(base + 12, []).append(
                        lambda: wload(wpl.v(), wpool_d[l].rearrange("g c d -> c g d"), "wpl"))
                if h == 7:
                    step_at.setdefault(base + 12, []).append(
                        lambda: (wload(Wu2.v(), winv[:, :, 0:512], "Wu2"),
                                 wload(ppl.v(), ppool_d[l].rearrange("(g p) n -> p g n", p=128), "ppl"),
                                 wload(Wga.v(), winv[:, :, 1184:2208], "Wga")))
            n_items = len(items)
            deferred = []
            pend = {}
            obs = {}
            for idx in range(n_items + LA):
                if idx < n_items:
                    h, c, kt, i0, ti, nt_ = items[idx]
                    q0 = i0 * 128
                    n = C - q0
                    sbk = PB[next_bank(S_BANKS)]
                    sv_ = sbk.v(slice(0, n))
                    qv = qT_all.v(h, slice(c * C + q0, (c + 1) * C), p=(0, 96))
                    kv = KT[h % 2].v(slice(kt * 128, (kt + 1) * 128), p=(0, 96))
                    mm_group(sv_, [(kv, qv)])
                    pt = PT[pti % NPT]
                    pti += 1
                    pv = pt.v(slice(0, n))
                    act(pv, sv_, AF.Exp)
                    if kt >= 16 + 4 * c:
                        tt("dve", pt.v(slice(0, 128)), pt.v(slice(0, 128)), tri.v(), ALU.mult)
                    pend[idx] = pv
                    if ti == 0:
                        obs[(h, c)] = (PB[5 + oi % 2], Osb[oi % 2], rdn[oi % 2])
                        oi += 1
                j = idx - LA
                if j >= 0:
                    h, c, kt, i0, ti, nt_ = items[j]
                    lower = h < 4
                    q0 = i0 * 128
                    ob, osb, rden = obs.pop((h, c)) if ti == nt_ - 1 else obs[(h, c)]
                    ov = ob.v(slice(q0, C))
                    vv = VV[h % 2].v(kt)
                    pv = pend.pop(j)
                    S.op("pe", (lambda ov=ov, vv=vv, pv=pv, ti=ti, nt_=nt_: lambda e: e.matmul(
                        ov.ap, vv.ap, pv.ap, start=(ti == 0), stop=(ti == nt_ - 1)))(),
                        reads=[vv, pv], writes=[ov])
                    if ti == nt_ - 1:
                        copy("dve", osb.v(), ob.v())
                        rp = (64, 65) if lower else (0, 1)
                        S.op("dve", (lambda osb=osb, rden=rden, rp=rp: lambda e: e.reciprocal(
                            out=rden.v(p=rp).ap, in_=osb.v(p=rp).ap))(),
                            reads=[osb.v(p=rp)], writes=[rden.v(p=rp)])

                        def fin(h=h, c=c, osb=osb, rden=rden, rp=rp, lower=lower):
                            bcb = PB[next_bank(S_BANKS)]
                            mm_group(bcb.v(), [((ones_f if lower else sel_f).v(p=rp), rden.v(p=rp))])
                            rows = (0, 64) if lower else (64, 128)
                            tt("dve", attn_o.v(h % 4, slice(c * C, (c + 1) * C), p=rows), osb.v(p=rows),
                               bcb.v(p=rows), ALU.mult)
                        deferred.append((idx + 16, fin))
                for st in step_at.get(idx, []):
                    st()
                while deferred and deferred[0][0] <= idx:
                    deferred.pop(0)[1]()
            for _, fn in deferred:
                fn()

            if debug and l == 0:
                S.dma("sp", [lambda e: e.dma_start(out=dbg_ao, in_=attn_o.full.rearrange("p j t -> p (j t)"))],
                      "dbg2", reads=[attn_o.v()], writes=[View(None, ("dram", "dbg2"), (0, 1, 0, 1))])
            o = G_o
            hTg2 = [sb(o + i * 4096, [128, 8, GC], BF16) for i in range(2)]; o += 8192
            gA = sb(o, [128, GC], F32); o += 1024
            gB = sb(o, [128, GC], F32); o += 1024
            hA = sb(o, [128, GC], F32); o += 1024
            hB = sb(o, [128, GC], F32); o += 1024
            uT = sb(o, [128, 4, 16 + GC], F32); o += 4 * (16 + GC) * 4
            wA = sb(o, [128, 4, 16 + GC], F32); o += 4 * (16 + GC) * 4
            wB = sb(o, [128, 4, 16 + GC], F32); o += 4 * (16 + GC) * 4
            pTt = sb(o, [128, 4, GC], BF16); o += 4 * GC * 2
            yTt = sb(o, [128, 4, GC], BF16); o += 4 * GC * 2
            mrg = sb(o, [128, 8, GC], BF16); o += 8 * GC * 2
            sgA = sb(o, [128, GC], F32); o += 1024
            sgB = sb(o, [128, GC], F32); o += 1024
            pav = pattn_d[l].rearrange("(hh j v) n -> hh v j n", hh=2, j=4)
            S.dma("pool", [lambda e, pav=pav: e.dma_start(out=pat.v(p=(0, 64)).ap, in_=pav[0]),
                           lambda e, pav=pav: e.dma_start(out=pat.v(p=(64, 128)).ap, in_=pav[1])], "pat", writes=[pat.v()])
            wload(wo.v(), wout_d[l].rearrange("(k p) n -> p k n", p=128), "wo")
            yT2 = [yTt, sb(o, [128, 4, GC], BF16)]; o += 4 * GC * 2
            sqe = sb(o, [128, 8, GC], BF16); o += 8 * GC * 2
            assert o <= ARENA - 8192, o
            ts("dve", uT.v(slice(None), slice(0, 16)), halo_in.v(), svc(SV_FLAG), ALU.mult)
            NG = NT // GC
            L_ = 16 + GC

            def gsl(gc):
                return slice(gc * GC, (gc + 1) * GC)

            def P0(gc):
                hTg = hTg2[gc % 2]
                gs = gsl(gc)
                h_from(l, 0, lambda k: xT.v(k, gs), rstd1_all.v(gs), lambda k: hTg.v(k), hA.v(), hB.v())

            def P1(gc):
                hTg = hTg2[gc % 2]
                for half in range(2):
                    bk = PB[next_bank()]
                    for gg in range(2):
                        g = half * 2 + gg
                        mm_group(bk.v(slice(gg * GC, (gg + 1) * GC)),
                                 [(Wu2.v(k, slice(g * 128, (g + 1) * 128)), hTg.v(k)) for k in range(8)])
                    S.op("act", lambda e, bk=bk, half=half: e.copy(
                        out=uT.full[:, half * 2:half * 2 + 2, 16:16 + GC],
                        in_=bk.full.rearrange("p (g t) -> p g t", g=2)),
                        reads=[bk.v()], writes=[uT.v(slice(half * 2, half * 2 + 2))])

            def P2a(gc):
                E = "pool"
                tt(E, wA.v(slice(0, 4), slice(1, L_)), uT.v(slice(0, 4), slice(1, L_)), uT.v(slice(0, 4), slice(0, L_ - 1)), ALU.add)
                tt(E, wB.v(slice(1, 4), slice(3, L_)), wA.v(slice(1, 4), slice(3, L_)), wA.v(slice(1, 4), slice(1, L_ - 2)), ALU.add)
                tt(E, wA.v(slice(2, 4), slice(7, L_)), wB.v(slice(2, 4), slice(7, L_)), wB.v(slice(2, 4), slice(3, L_ - 4)), ALU.add)
                tt(E, wB.v(slice(3, 4), slice(15, L_)), wA.v(slice(3, 4), slice(15, L_)), wA.v(slice(3, 4), slice(7, L_ - 8)), ALU.add)

            def P2a2(gc):
                for g in range(4):
                    w = 2 << g
                    cur = wA if g % 2 == 0 else wB
                    stt(pTt.v(g), cur.v(g, slice(16, L_)), 1.0 / w, uT.v(g, slice(16, L_)), ALU.mult, ALU.subtract)
                    if gc == 0:
                        tt("dve", gA.v(slice(0, 16)), cur.v(g, slice(16, 32)), svc(SV_INVCNT + g * 16, 16), ALU.mult)
                        tt("dve", pTt.v(g, slice(0, 16)), gA.v(slice(0, 16)), uT.v(g, slice(16, 32)), ALU.subtract)
                S.op("dve", lambda e: e.tensor_copy(out=uT.full[:, :, 0:16], in_=uT.full[:, :, GC:GC + 16]),
                     reads=[uT.v()], writes=[uT.v()])

            def P2b(gc):
                yT = yT2[gc % 2]
                bk = PB[next_bank()]
                bk2 = PB[next_bank()]
                for g in range(4):
                    dst = (bk if g < 2 else bk2).v(slice((g % 2) * GC, (g % 2 + 1) * GC))
                    mm_group(dst, [(wpl.v(g), pTt.v(g))])
                for g in range(4):
                    dst = (bk if g < 2 else bk2).v(slice((g % 2) * GC, (g % 2 + 1) * GC))
                    ts("dve", yT.v(g), dst, svc(SV_PSC + l * 4 + g), ALU.mult)

            def M1(gc, mrange):
                hTg = hTg2[gc % 2]
                yT = yT2[gc % 2]
                gs = gsl(gc)
                for m in mrange:
                    ms = slice(m * 128, (m + 1) * 128)
                    b1 = PB[next_bank()]
                    b2 = PB[next_bank()]
                    ya = b1.v(slice(0, GC))
                    ga = b1.v(slice(GC, 2 * GC))
                    yb = b2.v(slice(0, GC))
                    gb = b2.v(slice(GC, 2 * GC))
                    mm_group(ya, [(ppl.v(g, ms), yT.v(g)) for g in range(4)])
                    mm_group(ga, [(Wga.v(k, ms), hTg.v(k)) for k in range(8)])
                    mm_group(yb, [(pat.v(j, ms), attn_o.v(j, gs)) for j in range(4)])
                    mm_group(gb, [(Wgb.v(k, ms), hTg.v(k)) for k in range(8)])
                    act(sgA.v(), ga, AF.Sigmoid)
                    act(sgB.v(), gb, AF.Sigmoid)
                    tt("dve", gA.v(), sgA.v(), ya, ALU.mult)
                    tt("dve", gB.v(), sgB.v(), yb, ALU.mult)
                    tt("dve", mrg.v(m), gA.v(), gB.v(), ALU.add)

            def M2(gc):
                gs = gsl(gc)
                for m in range(8):
                    ms = slice(m * 128, (m + 1) * 128)
                    bk = PB[next_bank()]
                    ov = bk.v(slice(0, GC))
                    mm_group(ov, [(wo.v(k, ms), mrg.v(k)) for k in range(8)])
                    stt(xT.v(m, gs), ov, modT.v(l, slice(16 + m, 17 + m)), xT.v(m, gs), ALU.mult, ALU.add)

            def FsA(gc):
                gs = gsl(gc)
                for k in range(8):
                    act(sqe.v(k), xT.v(k, gs), AF.Square)

            def FsB(gc):
                gs = gsl(gc)
                bk = PB[next_bank()]
                st = bk.v(slice(0, GC))
                mm_group(st, [(ones_b.v(), sqe.v(k)) for k in range(8)])
                rstd_from(st, rstd1_all.v(gs), D, sgA.v())

            P0(0)
            P1(0)
            P2a(0)
            P2a2(0)
            P2b(0)
            for gc in range(NG):
                M1(gc, range(0, 2))
                if gc + 1 < NG:
                    P0(gc + 1)
                if gc >= 1:
                    FsA(gc - 1)
                M1(gc, range(2, 4))
                if gc + 1 < NG:
                    P1(gc + 1)
                    P2a(gc + 1)
                M1(gc, range(4, 6))
                if gc >= 1:
                    FsB(gc - 1)
                M1(gc, range(6, 8))
                if gc + 1 < NG:
                    P2a2(gc + 1)
                M2(gc)
                if gc + 1 < NG:
                    P2b(gc + 1)
            FsA(NG - 1)
            FsB(NG - 1)

            h2 = sb(R_AO, [128, 8, NT], BF16)
            o = R_AO + 32768
            W1 = [sb(o + i * 16384, [128, 8, 512], BF16) for i in range(2)]
            W2 = [sb(o + 8192 + i * 16384, [128, 4, D], BF16) for i in range(2)]
            o += 32768
            hid = [sb(o + i * 4096, [128, 4, C], BF16) for i in range(2)]; o += 8192
            assert o <= R_X
            o = R_X + 32768
            sqf = sb(o, [128, 8, C], BF16); o += 8192
            fA = sb(o, [128, C], F32); o += 2048
            fB = sb(o, [128, C], F32); o += 2048
            frs = sb(o, [128, NT], F32); o += 8192
            rl = [sb(o + i * 2048, [128, C], F32) for i in range(2)]; o += 4096
            assert o <= ARENA
            w1v = wff1_d[l].rearrange("(k p) n -> p k n", p=128)
            w2v = wff2_d[l].rearrange("(j p) n -> p j n", p=128)

            def ffload(g):
                wload(W1[g % 2].v(), w1v[:, :, g * 512:(g + 1) * 512], f"W1_{g % 2}")
                wload(W2[g % 2].v(), w2v[:, g * 4:(g + 1) * 4, :], f"W2_{g % 2}")

            for c in range(NCH):
                cs = slice(c * C, (c + 1) * C)
                h_from(l, 1, lambda k: xT.v(k, cs), rstd1_all.v(cs), lambda k: h2.v(k, cs), fA.v(), fB.v())
            ffload(0)
            ri = [0]

            def ff_up(g, c):
                w1 = W1[g % 2]
                cs = slice(c * C, (c + 1) * C)
                hd = hid[(g * NCH + c) % 2]
                for hc in range(4):
                    bk = PB[next_bank()]
                    mm_group(bk.v(), [(w1.v(k, slice(hc * 128, (hc + 1) * 128)), h2.v(k, cs)) for k in range(8)])
                    r = rl[ri[0] % 2]
                    ri[0] += 1
                    act(r.v(), bk.v(), AF.Relu)
                    tt("dve", hd.v(hc), r.v(), r.v(), ALU.mult)

            def ff_down(g, c):
                w2 = W2[g % 2]
                cs = slice(c * C, (c + 1) * C)
                hd = hid[(g * NCH + c) % 2]
                for m in range(8):
                    bk = PB[next_bank()]
                    mm_group(bk.v(), [(w2.v(hc, slice(m * 128, (m + 1) * 128)), hd.v(hc)) for hc in range(4)])
                    stt(xT.v(m, cs), bk.v(), modT.v(l, slice(40 + m, 41 + m)), xT.v(m, cs), ALU.mult, ALU.add)

            def early_stats(c):
                cs = slice(c * C, (c + 1) * C)
                for k in range(8):
                    act(sqf.v(k), xT.v(k, cs), AF.Square)
                bk = PB[next_bank()]
                mm_group(bk.v(), [(ones_b.v(), sqf.v(k)) for k in range(8)])
                rstd_from(bk.v(), rstd1_all.v(cs), D, fA.v())

            seq = [(g, c) for g in range(FFG) for c in range(NCH)]
            loaded = {0}
            ff_up(*seq[0])
            for i, (g, c) in enumerate(seq):
                if c == 0 and g + 1 < FFG and (g + 1) not in loaded:
                    ffload(g + 1)
                    loaded.add(g + 1)
                if i + 1 < len(seq):
                    ff_up(*seq[i + 1])
                ff_down(g, c)
                if g == FFG - 1 and c >= 1:
                    early_stats(c - 1)
            early_stats(NCH - 1)

        o = R_AO
        osq = sb(o, [128, 8, C], BF16); o += 8192
        ors = sb(o, [128, C], F32); o += 2048
        otmp = sb(o, [128, C], F32); o += 2048
        obuf = [sb(o + i * 16384, [128, 8, C], F32) for i in range(2)]; o += 32768
        outv = out_d.rearrange("(k p) t -> p k t", p=128)
        out_tok = None
        out_toks = []
        for c in range(NCH):
            cs = slice(c * C, (c + 1) * C)
            ob_ = obuf[c % 2]
            for k in range(8):
                stt(ob_.v(k), xT.v(k, cs), svc(SV_FING + k), rstd1_all.v(cs), ALU.mult, ALU.mult)
            out_tok = S.dma("sp", [(lambda c=c, ob_=ob_: lambda e: e.dma_start(out=outv[:, :, c * C:(c + 1) * C], in_=ob_.v().ap))()],
                            f"outst{c}", reads=[ob_.v()], writes=[View(None, ("dram", "out"), (0, 1, c, c + 1))])
            out_toks.append(out_tok)
        for tk in out_toks:
            S.wait_tok("sp", tk)
        S.emit(block)
    return nc


_NC_CACHE = {}


def _inv_freq():
    return (np.float32(10000.0) ** (-np.arange(0, 32, 2, dtype=np.float32) / np.float32(32))).astype(np.float32)


def kernel(x, c, positions, ln1_g, ln2_g, w_ada, b_ada, w_in, q_norm_g, w_uq, kv_norm_g, w_uk, w_uv,
           w_pool, pool_scale, p_pool, p_attn, w_out, w_ff1, w_ff2, final_g):
    f32 = np.float32
    x = np.asarray(x, f32)
    c = np.asarray(c, f32)
    positions = np.asarray(positions, np.int32)
    B, S_, _ = x.shape
    dbg = _NC_CACHE.get("debug", False)
    if "nc" not in _NC_CACHE:
        _NC_CACHE["nc"] = build_nc(debug=dbg)
    nc = _NC_CACHE["nc"]

    def fm(v, n):
        return np.asarray(v, f32).reshape(n, 128).T

    tri = (np.arange(128)[None, :] >= np.arange(128)[:, None]).astype(ml_dtypes.bfloat16)
    invf = _inv_freq()
    shared = {
        "tri": tri,
        "w_ada": np.ascontiguousarray(np.asarray(w_ada, f32)),
        "w_in": np.ascontiguousarray(np.asarray(w_in, f32)),
        "w_uq": np.ascontiguousarray(np.asarray(w_uq, f32).reshape(DEPTH, 384, 768)),
        "w_uk": np.ascontiguousarray(np.asarray(w_uk, f32).reshape(DEPTH, 256, 512)),
        "w_uv": np.ascontiguousarray(np.asarray(w_uv, f32).reshape(DEPTH, 256, 512)),
        "w_pool": np.ascontiguousarray(np.asarray(w_pool, f32)),
        "p_pool": np.ascontiguousarray(np.asarray(p_pool, f32)),
        "p_attn": np.ascontiguousarray(np.asarray(p_attn, f32)),
        "w_out": np.ascontiguousarray(np.asarray(w_out, f32)),
        "w_ff1": np.ascontiguousarray(np.asarray(w_ff1, f32)),
        "w_ff2": np.ascontiguousarray(np.asarray(w_ff2, f32)),
    }
    in_maps = []
    for core in range(8):
        b, half = core // 2, core % 2
        t0 = half * NT
        sv = np.zeros((128, NV), f32)
        for l in range(DEPTH):
            sv[:, SV_LN1 + l * 8:SV_LN1 + l * 8 + 8] = fm(ln1_g[l], 8)
            sv[:, SV_LN2 + l * 8:SV_LN2 + l * 8 + 8] = fm(ln2_g[l], 8)
            sv[:, SV_BADA + l * 48:SV_BADA + l * 48 + 48] = fm(b_ada[l], 48)
            sv[:, SV_QNG + l * 3:SV_QNG + l * 3 + 3] = fm(q_norm_g[l], 3)
            sv[:, SV_KVG + l * 2:SV_KVG + l * 2 + 2] = fm(kv_norm_g[l], 2)
            sv[:, SV_PSC + l * 4:SV_PSC + l * 4 + 4] = fm(pool_scale[l], 4)
        sv[:, SV_FING:SV_FING + 8] = fm(final_g, 8)
        sv[64:96, SV_INVF] = np.tile(invf, 2)
        sv[:, SV_FLAG] = float(half)
        for g in range(4):
            w = 2 << g
            tpos = t0 + np.arange(16)
            cnt = np.minimum(tpos + 1, w).astype(f32)
            sv[:, SV_INVCNT + g * 16:SV_INVCNT + g * 16 + 16] = (f32(1.0) / cnt)[None, :]
        sv[:, SV_C:SV_C + 8] = fm(c[b], 8)
        m = dict(shared)
        m["xT"] = np.ascontiguousarray(x[b, t0:t0 + NT, :].T)
        m["pos"] = np.ascontiguousarray(positions[b, t0:t0 + NT].reshape(1, NT))
        m["smallv"] = sv
        in_maps.append(m)
    res = run_bass_kernel_spmd(nc, in_maps, core_ids=list(range(8)))
    _NC_CACHE["res"] = res.results if dbg else None
    out = np.empty((B, S_, D), f32)
    for core in range(8):
        b, half = core // 2, core % 2
        t0 = half * NT
        out[b, t0:t0 + NT, :] = np.asarray(res.results[core]["outT"], f32).T
    return out
```

```python
import math
import numpy as np
import ml_dtypes
import concourse.bass as bass
import concourse.mybir as mybir
from concourse.bass_utils import run_bass_kernel_spmd

F32 = mybir.dt.float32
BF16 = mybir.dt.bfloat16
I32 = mybir.dt.int32
U8 = mybir.dt.uint8
ALU = mybir.AluOpType
AF = mybir.ActivationFunctionType

D = 1024
NT = 2048
CTX = 4096
C = 512
NCH = NT // C
GC = 256
DEPTH = 2
NH = 8
EPS = 1e-6
SCALE = 1.0 / math.sqrt(96.0)
FFG = 8
ESZ = {F32: 4, BF16: 2, I32: 4, U8: 1}

SV_LN1, SV_LN2, SV_BADA, SV_QNG, SV_KVG, SV_PSC, SV_FING = 0, 16, 32, 128, 134, 138, 146
SV_INVF, SV_FLAG, SV_INVCNT, SV_C = 154, 155, 156, 220
NV = 228
ARENA = 207 * 1024


class View:
    __slots__ = ("ap", "space", "iv")

    def __init__(self, ap, space, iv):
        self.ap = ap
        self.space = space
        self.iv = iv


class T:
    def __init__(self, base_ap, space, off, shape, dt):
        self.shape = tuple(shape)
        self.dt = dt
        self.space = space
        self.off = off
        es = ESZ[dt]
        self.es = es
        free = self.shape[1:]
        n = int(np.prod(free))
        self.nbytes = n * es
        strides = []
        s = 1
        for d in reversed(free):
            strides.append(s)
            s *= d
        self.strides = list(reversed(strides))
        if space == "sb":
            ap = base_ap[:, off:off + n * es]
            if dt != U8:
                ap = ap.bitcast(dt)
        else:
            ap = base_ap[:, :]
        if len(free) == 2:
            ap = ap.rearrange("p (a b) -> p a b", a=free[0])
        elif len(free) == 3:
            ap = ap.rearrange("p (a b c) -> p a b c", a=free[0], b=free[1])
        self.full = ap

    def v(self, *idx, p=None):
        free = self.shape[1:]
        idx = list(idx) + [slice(None)] * (len(free) - len(idx))
        lo = 0
        hi = 0
        key = []
        for i, d, st in zip(idx, free, self.strides):
            if isinstance(i, int):
                a, b = i, i + 1
                key.append(i)
            else:
                a = 0 if i.start is None else i.start
                b = d if i.stop is None else i.stop
                key.append(slice(a, b))
            assert 0 <= a < b <= d, (idx, self.shape)
            lo += a * st
            hi += (b - 1) * st
        p0, p1 = (0, self.shape[0]) if p is None else p
        assert p1 <= self.shape[0]
        ap = self.full[(slice(p0, p1),) + tuple(key)]
        return View(ap, self.space, (p0, p1, self.off + lo * self.es, self.off + (hi + 1) * self.es))


class Sched:
    EPOCH = 12000

    def __init__(self, nc, es):
        self.nc = nc
        self.es = es
        self.eng = {"pe": nc.tensor, "act": nc.scalar, "dve": nc.vector, "pool": nc.gpsimd, "sp": nc.sync}
        self.streams = {e: [] for e in self.eng}
        self.cnt = {e: 0 for e in self.eng}
        self.psems = {e: [] for e in self.eng}
        self.waited = {e: {} for e in self.eng}
        self.recs = {}
        self.dsem = {}
        self.pbank = {}
        self.nsem = 0

    def new_sem(self, name):
        self.nsem += 1
        return self.es.enter_context(self.nc.semaphore(name))

    def _ptok(self, e):
        self.cnt[e] += 1
        n = self.cnt[e]
        ei = (n - 1) // self.EPOCH
        while len(self.psems[e]) <= ei:
            self.psems[e].append(self.new_sem(f"p_{e}_{len(self.psems[e])}"))
        return ("p", e, n)

    def _tok_wait(self, tok):
        if tok[0] == "p":
            _, e, n = tok
            ei = (n - 1) // self.EPOCH
            return self.psems[e][ei], n - ei * self.EPOCH
        _, key, val = tok
        return self.dsem[key][0], val

    @staticmethod
    def _ov(a, b):
        return a[0] < b[1] and b[0] < a[1] and a[2] < b[3] and b[2] < a[3]

    @staticmethod
    def _cov(a, b):
        return a[0] <= b[0] and a[1] >= b[1] and a[2] <= b[2] and a[3] >= b[3]

    def _deps(self, reads, writes, e=None):
        deps = []
        for v in reads:
            if v.space[0] == "ps":
                st = self.pbank.get(v.space)
                if st is not None and st[0] != e:
                    deps.append(st[1])
                continue
            for r in self.recs.get(v.space, ()):
                if r[1] == "W" and self._ov(r[0], v.iv):
                    deps.append(r[2])
        for v in writes:
            if v.space[0] == "ps":
                st = self.pbank.get(v.space)
                if st is not None and st[0] != e:
                    deps.append(st[1])
                continue
            for r in self.recs.get(v.space, ()):
                if self._ov(r[0], v.iv):
                    deps.append(r[2])
        return deps

    def _record(self, reads, writes, tok):
        ps = [v for v in list(reads) + list(writes) if v.space[0] == "ps"]
        for v in ps:
            self.pbank[v.space] = (tok[1], tok)
        reads = [v for v in reads if v.space[0] != "ps"]
        writes = [v for v in writes if v.space[0] != "ps"]
        for v in writes:
            lst = self.recs.setdefault(v.space, [])
            lst[:] = [r for r in lst if not self._cov(v.iv, r[0])]
            lst.append([v.iv, "W", tok])
        for v in reads:
            lst = self.recs.setdefault(v.space, [])
            lst[:] = [r for r in lst if not (r[1] == "R" and r[2][0] == "p" and tok[0] == "p"
                                             and r[2][1] == tok[1] and self._cov(v.iv, r[0]))]
            lst.append([v.iv, "R", tok])

    def _waits(self, e, deps):
        best = {}
        for tok in deps:
            if tok[0] == "p":
                if tok[1] == e and e in ("pe", "sp"):
                    continue
                k = ("p", tok[1])
                val = tok[2]
            else:
                k = ("d", tok[1])
                val = tok[2]
            if self.waited[e].get(k, 0) >= val:
                continue
            if best.get(k, (0, None))[0] < val:
                best[k] = (val, tok)
        out = []
        for k, (val, tok) in best.items():
            self.waited[e][k] = val
            out.append(self._tok_wait(tok))
        return out

    def op(self, e, fn, reads=(), writes=(), sig=True):
        deps = self._deps(reads, writes, e)
        waits = self._waits(e, deps)
        if sig:
            tok = self._ptok(e)
            sem, val = self._tok_wait(tok)
            self._record(reads, writes, tok)
        else:
            tok = None
            sem = None
        self.streams[e].append((waits, fn, sem, False))
        return tok

    def dma(self, q, fns, key, reads=(), writes=()):
        deps = self._deps(reads, writes)
        waits = self._waits(q, deps)
        if key not in self.dsem:
            self.dsem[key] = [self.new_sem("d_" + key), 0]
        ds = self.dsem[key]
        ds[1] += 16 * len(fns)
        tok = ("d", key, ds[1])
        self._record(reads, writes, tok)
        first = True
        for fn in fns:
            self.streams[q].append((waits if first else [], fn, ds[0], True))
            first = False
        return tok

    def custom(self, e, fn, reads=(), writes=(), key=None, inc=1):
        deps = self._deps(reads, writes)
        waits = self._waits(e, deps)
        if key not in self.dsem:
            self.dsem[key] = [self.new_sem("c_" + key), 0]
        ds = self.dsem[key]
        ds[1] += inc
        tok = ("d", key, ds[1])
        self._record(reads, writes, tok)
        self.streams[e].append((waits, fn, ds[0], "cc"))
        return tok

    def wait_tok(self, e, tok):
        ws = self._waits(e, [tok])
        if ws:
            self.streams[e].append((ws, None, None, False))

    def emit(self, block):
        def run(e, eng):
            for waits, fn, sem, kind in self.streams[e]:
                for s, v in waits:
                    eng.wait_ge(s, v)
                if fn is None:
                    continue
                inst = fn(eng)
                if sem is not None:
                    if kind is True:
                        inst.then_inc(sem, 16)
                    elif kind == "cc":
                        inst.then_inc(sem)
                    else:
                        inst.then_inc(sem, 1)

        @block.tensor
        def _(eng):
            run("pe", eng)

        @block.scalar
        def _(eng):
            run("act", eng)

        @block.vector
        def _(eng):
            run("dve", eng)

        @block.gpsimd
        def _(eng):
            run("pool", eng)

        @block.sync
        def _(eng):
            run("sp", eng)


def build_nc(debug=False):
    from contextlib import ExitStack
    nc = bass.Bass("TRN2", target_bir_lowering=False)
    dr = {}

    def din(name, shape, dt):
        dr[name] = nc.dram_tensor(name, list(shape), dt, kind="ExternalInput").ap()
        return dr[name]

    xT_d = din("xT", [D, NT], F32)
    pos_d = din("pos", [1, NT], I32)
    sv_d = din("smallv", [128, NV], F32)
    tri_d = din("tri", [128, 128], BF16)
    wada_d = din("w_ada", [DEPTH, D, 6 * D], F32)
    win_d = din("w_in", [DEPTH, D, 3232], F32)
    wuq_d = din("w_uq", [DEPTH, 384, 768], F32)
    wuk_d = din("w_uk", [DEPTH, 256, 512], F32)
    wuv_d = din("w_uv", [DEPTH, 256, 512], F32)
    wpool_d = din("w_pool", [DEPTH, 4, 128, 128], F32)
    ppool_d = din("p_pool", [DEPTH, 512, D], F32)
    pattn_d = din("p_attn", [DEPTH, 512, D], F32)
    wout_d = din("w_out", [DEPTH, D, D], F32)
    wff1_d = din("w_ff1", [DEPTH, D, 4 * D], F32)
    wff2_d = din("w_ff2", [DEPTH, 4 * D, D], F32)
    out_d = nc.dram_tensor("outT", [D, NT], F32, kind="ExternalOutput").ap()

    exs = [nc.dram_tensor(f"exs{l}", [288, NT], BF16) for l in range(DEPTH)]
    exd = [nc.dram_tensor(f"exd{l}", [576, NT], BF16) for l in range(DEPTH)]
    hls = [nc.dram_tensor(f"hls{l}", [128, 64], F32) for l in range(DEPTH)]
    hld = [nc.dram_tensor(f"hld{l}", [256, 64], F32) for l in range(DEPTH)]
    tabs = nc.dram_tensor("tabs", [256, NT], F32)
    if debug:
        dbg_q = nc.dram_tensor("dbg_q", [128, NH * NT], BF16, kind="ExternalOutput").ap()
        dbg_ckv = nc.dram_tensor("dbg_ckv", [128, 2 * CTX], BF16, kind="ExternalOutput").ap()
        dbg_kr = nc.dram_tensor("dbg_kr", [128, CTX], BF16, kind="ExternalOutput").ap()
        dbg_ao = nc.dram_tensor("dbg_ao", [128, 4 * NT], BF16, kind="ExternalOutput").ap()
        dbg_cos = nc.dram_tensor("dbg_cos", [128, 2 * NT], F32, kind="ExternalOutput").ap()
        dbg_pat = nc.dram_tensor("dbg_pat", [128, 4 * D], BF16, kind="ExternalOutput").ap()
        dbg_mrg = nc.dram_tensor("dbg_mrg", [128, 8 * GC], BF16, kind="ExternalOutput").ap()

    with ExitStack() as es:
        arena = es.enter_context(nc.sbuf_tensor("arena", [128, ARENA], U8))
        banks = [es.enter_context(nc.psum_tensor(f"bank{i}", [128, 512], F32)) for i in range(8)]
        S = Sched(nc, es)
        block = es.enter_context(nc.Block())

        def sb(off, shape, dt):
            t = T(arena, "sb", off, shape, dt)
            assert off + t.nbytes <= ARENA, (off, shape)
            return t

        PB = [T(banks[i], ("ps", i), 0, [128, 512], F32) for i in range(8)]

        def dview(handle, name):
            return View(None, ("dram", name), (0, 1, 0, 1))

        xT = sb(0, [128, 8, NT], F32)
        o = 65536
        smallv = sb(o, [128, NV], F32); o += NV * 4
        modT = sb(o, [128, DEPTH, 48], F32); o += DEPTH * 48 * 4
        avec = sb(o, [128, DEPTH, 2, 8], F32); o += DEPTH * 16 * 4
        cact = sb(o, [128, 8], BF16); o += 16
        ones_b = sb(o, [128, 128], BF16); o += 256
        tri = sb(o, [128, 128], BF16); o += 256
        epsT = sb(o, [128, 1], F32); o += 4
        zeroT = sb(o, [128, 1], F32); o += 4
        ones_f = sb(o, [128, 128], F32); o += 512
        sel_f = sb(o, [128, 128], F32); o += 512
        flag16 = sb(o, [128, 16], BF16); o += 32
        halo_sb = sb(o, [128, 4, 16], F32); o += 256
        halo_in = sb(o, [128, 4, 16], F32); o += 256
        assert o <= 69632, o
        P0 = 69632
        R_AO = P0
        R_CTX = R_AO + 16384
        R_Q = R_CTX + 24576
        R_X = R_Q + 32768
        assert R_X == 143360

        cosT = sb(R_AO, [128, NT], F32)
        sinT = sb(R_AO + 8192, [128, NT], F32)
        attn_o = sb(R_AO, [128, 4, NT], BF16)
        ctx_ckv = sb(R_CTX, [128, 2, CTX], BF16)
        ctx_kr = sb(R_CTX + 16384, [128, CTX], BF16)
        qT_all = sb(R_Q, [128, NH, NT], BF16)

        def svc(col, n=1):
            return smallv.v(slice(col, col + n))

        bank_rr = [0]

        def next_bank(pool=(0, 1, 2, 3, 4, 5, 6, 7)):
            b = pool[bank_rr[0] % len(pool)]
            bank_rr[0] += 1
            return b

        def mm_group(out, pairs, extra_reads=()):
            n = len(pairs)
            allreads = [v for pr in pairs for v in pr] + list(extra_reads)
            for i, (l, r) in enumerate(pairs):
                last = i == n - 1
                fn = (lambda l=l, r=r, i=i, last=last: lambda e: e.matmul(out.ap, l.ap, r.ap, start=(i == 0), stop=last))()
                if last:
                    S.op("pe", fn, reads=allreads, writes=[out])
                else:
                    if i == 0:
                        S.op("pe", fn, reads=allreads, writes=[out], sig=False)
                    else:
                        S.op("pe", fn, sig=False)

        def act(out, in_, func, scale=1.0, bias=None, extra_reads=()):
            rd = [in_] + list(extra_reads)
            kw = {}
            if bias is not None:
                kw["bias"] = bias.ap
                rd.append(bias)
            if isinstance(scale, View):
                rd.append(scale)
                sc = scale.ap
            else:
                sc = scale
            S.op("act", lambda e: e.activation(out=out.ap, in_=in_.ap, func=func, scale=sc, **kw),
                 reads=rd, writes=[out])

        def tt(eng, out, a, b, op):
            S.op(eng, lambda e: e.tensor_tensor(out=out.ap, in0=a.ap, in1=b.ap, op=op), reads=[a, b], writes=[out])

        def ts(eng, out, a, s1, op0, s2=None, op1=None):
            rd = [a]
            s1v = s1.ap if isinstance(s1, View) else s1
            s2v = s2.ap if isinstance(s2, View) else s2
            if isinstance(s1, View):
                rd.append(s1)
            if isinstance(s2, View):
                rd.append(s2)
            if op1 is None:
                S.op(eng, lambda e: e.tensor_scalar(out=out.ap, in0=a.ap, scalar1=s1v, scalar2=None, op0=op0),
                     reads=rd, writes=[out])
            else:
                S.op(eng, lambda e: e.tensor_scalar(out=out.ap, in0=a.ap, scalar1=s1v, scalar2=s2v, op0=op0, op1=op1),
                     reads=rd, writes=[out])

        def stt(out, a, s, b, op0, op1):
            rd = [a, b]
            sv = s.ap if isinstance(s, View) else s
            if isinstance(s, View):
                rd.append(s)
            S.op("dve", lambda e: e.scalar_tensor_tensor(out=out.ap, in0=a.ap, scalar=sv, in1=b.ap, op0=op0, op1=op1),
                 reads=rd, writes=[out])

        def copy(eng, out, in_):
            if eng == "act":
                S.op("act", lambda e: e.copy(out=out.ap, in_=in_.ap), reads=[in_], writes=[out])
            else:
                S.op(eng, lambda e: e.tensor_copy(out=out.ap, in_=in_.ap), reads=[in_], writes=[out])

        def memset(eng, out, val):
            S.op(eng, lambda e: e.memset(out.ap, val), writes=[out])

        def wload(dst, src_ap, key):
            S.dma("pool", [lambda e: e.dma_start(out=dst.ap, in_=src_ap)], key, writes=[dst])

        def rstd_from(stat_ps, out, n, tmp):
            act(tmp, stat_ps, AF.Ln, scale=1.0 / n, bias=epsT.v())
            act(out, tmp, AF.Exp, scale=-0.5)

        S.dma("sp", [lambda e: e.dma_start(out=smallv.v().ap, in_=sv_d)], "smallv", writes=[smallv.v()])
        S.dma("sp", [lambda e: e.dma_start(out=tri.v().ap, in_=tri_d)], "tri", writes=[tri.v()])
        posi = sb(R_CTX, [128, NT], I32)
        S.dma("sp", [lambda e: e.dma_start(out=posi.v().ap, in_=pos_d.partition_broadcast(128)[:, 0, :])],
              "pos", writes=[posi.v()])
        xTv = xT_d.rearrange("(k p) t -> p k t", p=128)
        for c in range(NCH):
            dst = xT.v(slice(None), slice(c * C, (c + 1) * C))
            S.dma("sp", [(lambda c=c, dst=dst: lambda e: e.dma_start(out=dst.ap, in_=xTv[:, :, c * C:(c + 1) * C]))()],
                  f"x{c}", writes=[dst])

        Wkr0 = sb(R_X + 4096, [128, 8, 96], BF16)
        Wkrr0 = sb(R_X + 4096 + 1536, [128, 8, 96], BF16)
        memset("dve", Wkr0.v(), 0.0)
        memset("dve", Wkrr0.v(), 0.0)
        memset("dve", ones_b.v(), 1.0)
        memset("dve", epsT.v(), EPS)
        memset("dve", zeroT.v(), 0.0)
        memset("dve", ones_f.v(), 1.0)
        memset("dve", sel_f.v(), 0.0)
        memset("dve", sel_f.v(slice(64, 128)), 1.0)
        memset("dve", flag16.v(), 1.0)
        ts("dve", flag16.v(), flag16.v(), svc(SV_FLAG), ALU.mult)

        act(cact.v(), svc(SV_C, 8), AF.Silu)

        PI = math.pi
        ang = sb(R_Q, [128, NT], F32)
        t1 = sb(R_Q + 8192, [128, NT], F32)
        t2 = sb(R_Q + 16384, [128, NT], F32)
        ki = sb(R_Q + 24576, [128, NT], I32)
        copy("dve", ang.v(), posi.v())
        ts("dve", ang.v(), ang.v(), svc(SV_INVF), ALU.mult)
        ts("dve", t1.v(), ang.v(), 1.0 / (2 * PI), ALU.mult)
        copy("dve", ki.v(), t1.v())
        copy("dve", t1.v(), ki.v())
        C1 = 6.28125
        C2 = 2 * PI - C1
        stt(t2.v(), t1.v(), -C1, ang.v(), ALU.mult, ALU.add)
        stt(t2.v(), t1.v(), -C2, t2.v(), ALU.mult, ALU.add)

        def wrap(r, tmp):
            ts("dve", tmp, r, PI, ALU.is_gt)
            stt(r, tmp, -2 * PI, r, ALU.mult, ALU.add)
            ts("dve", tmp, r, -PI, ALU.is_lt)
            stt(r, tmp, 2 * PI, r, ALU.mult, ALU.add)

        wrap(t2.v(), t1.v())
        act(sinT.v(), t2.v(), AF.Sin)
        ts("dve", t2.v(), t2.v(), PI / 2, ALU.add)
        wrap(t2.v(), t1.v())
        act(cosT.v(), t2.v(), AF.Sin)
        tabs_v = dview(tabs, "tabs")
        if debug:
            S.dma("sp", [lambda e: e.dma_start(out=dbg_cos[:, 0:NT], in_=cosT.v().ap),
                         lambda e: e.dma_start(out=dbg_cos[:, NT:2 * NT], in_=sinT.v().ap)],
                  "dbg0", reads=[cosT.v(), sinT.v()], writes=[View(None, ("dram", "dbg0"), (0, 1, 0, 1))])
        S.dma("sp", [lambda e: e.dma_start(out=tabs.ap()[0:128, :], in_=cosT.v().ap),
                     lambda e: e.dma_start(out=tabs.ap()[128:256, :], in_=sinT.v().ap)],
              "tabs_st", reads=[cosT.v(), sinT.v()], writes=[tabs_v])

        wada_b = [sb(R_CTX + 8192, [128, 8, D], BF16), sb(R_X + 30720, [128, 8, D], BF16)]
        for j in range(2):
            wb = wada_b[j]
            src = wada_d[0].rearrange("(k p) n -> p k n", p=128)[:, :, j * D:(j + 1) * D]
            wload(wb.v(), src, f"wada{j}")
            bk = PB[next_bank()]
            for m in range(8):
                mm_group(bk.v(slice(m, m + 1)),
                         [(wb.v(k, slice(m * 128, (m + 1) * 128)), cact.v(slice(k, k + 1))) for k in range(8)])
            tt("dve", modT.v(0, slice(j * 8, (j + 1) * 8)), bk.v(slice(0, 8)), svc(SV_BADA + j * 8, 8), ALU.add)
        stt(avec.v(0, 0), modT.v(0, slice(8, 16)), 1.0, svc(SV_LN1, 8), ALU.add, ALU.mult)
        wpc = sb(R_X + 51200, [128, 8, 512], BF16)
        mod_pieces = [(0, q) for q in range(4, 12)] + [(1, q) for q in range(12)]

        def mod_piece_step(lq, banks):
            ll, q = lq

            def f():
                src = wada_d[ll].rearrange("(k p) n -> p k n", p=128)[:, :, q * 512:(q + 1) * 512]
                wload(wpc.v(), src, "wpc")
                bk = PB[next_bank(banks)]
                for m in range(4):
                    mm_group(bk.v(slice(m, m + 1)),
                             [(wpc.v(k, slice(m * 128, (m + 1) * 128)), cact.v(slice(k, k + 1))) for k in range(8)])
                tt("dve", modT.v(ll, slice(q * 4, (q + 1) * 4)), bk.v(slice(0, 4)),
                   svc(SV_BADA + ll * 48 + q * 4, 4), ALU.add)
                if q == 3:
                    stt(avec.v(ll, 0), modT.v(ll, slice(8, 16)), 1.0, svc(SV_LN1 + ll * 8, 8), ALU.add, ALU.mult)
                if q == 9:
                    stt(avec.v(ll, 1), modT.v(ll, slice(32, 40)), 1.0, svc(SV_LN2 + ll * 8, 8), ALU.add, ALU.mult)
            return f

        rstd1_all = sb(ARENA - 8192, [128, NT], F32)
        sq0 = sb(R_X + 49152, [128, 8, C], BF16)
        ln0 = sb(R_X + 57344, [128, C], F32)
        for c in range(NCH):
            cs = slice(c * C, (c + 1) * C)
            for k in range(8):
                act(sq0.v(k), xT.v(k, cs), AF.Square)
            bk = PB[next_bank()]
            mm_group(bk.v(), [(ones_b.v(), sq0.v(k)) for k in range(8)])
            act(ln0.v(), bk.v(), AF.Ln, scale=1.0 / D, bias=epsT.v())
            act(rstd1_all.v(cs), ln0.v(), AF.Exp, scale=-0.5)
        def stats_all(rstd_all, sq, lntmp):
            for c in range(NCH):
                cs = slice(c * C, (c + 1) * C)
                for k in range(8):
                    act(sq.v(k), xT.v(k, cs), AF.Square)
                bk = PB[next_bank()]
                mm_group(bk.v(), [(ones_b.v(), sq.v(k)) for k in range(8)])
                rstd_from(bk.v(), rstd_all.v(cs), D, lntmp)

        def h_from(l, which, xv_k, rstd, hv_k, tmpA, tmpB):
            boff = 0 if which == 0 else 24
            for k in range(8):
                tmp = tmpA if k % 2 == 0 else tmpB
                tt("dve", tmp, xv_k(k), rstd, ALU.mult)
                act(hv_k(k), tmp, AF.Identity, scale=avec.v(l, which, slice(k, k + 1)),
                    bias=modT.v(l, slice(boff + k, boff + k + 1)))

        for l in range(DEPTH):
            o = R_X
            Wkv = sb(o, [128, 8, 256], BF16); o += 4096
            Wkr = sb(o, [128, 8, 96], BF16); o += 1536
            Wkrr = sb(o, [128, 8, 96], BF16); o += 1536
            Wq = sb(o, [128, 8, 384], BF16); o += 6144
            wuq = sb(o, [128, 3, 768], BF16); o += 4608
            wuqr = sb(o, [128, 3, 768], BF16); o += 4608
            Wu = sb(o, [128, 8, 512], BF16); o += 8192
            hT = sb(o, [128, 8, C], BF16); o += 8192
            sq = sb(o, [128, 8, C], BF16); o += 8192
            tmpA = sb(o, [128, C], F32); o += 2048
            tmpB = sb(o, [128, C], F32); o += 2048
            rstdq = sb(o, [128, C], F32); o += 2048
            rstdkv = rstdq
            cqn = sb(o, [128, 3, C], BF16); o += 3072
            sql = sb(o, [128, 3, C], BF16); o += 3072
            assert o <= ARENA - 8192, o
            hT2 = [hT, sq]
            rstd1_all = sb(ARENA - 8192, [128, NT], F32)

            winv = win_d[l].rearrange("(k p) n -> p k n", p=128)
            wload(Wkv.v(), winv[:, :, 896:1152], "Wkv")
            if l == 0:
                wload(Wkr.v(slice(None), slice(64, 96)), winv[:, :, 1152:1184], "Wkr")
            wload(Wq.v(), winv[:, :, 512:896], "Wq")
            wload(wuq.v(), wuq_d[l].rearrange("(j p) n -> p j n", p=128), "wuq")
            wload(Wu.v(), winv[:, :, 0:512], "Wu")
            if l > 0:
                memset("dve", Wkr.v(), 0.0)
                memset("dve", Wkrr.v(), 0.0)
                wload(Wkr.v(slice(None), slice(64, 96)), winv[:, :, 1152:1184], "Wkr")
            ts("dve", Wkrr.v(slice(None), slice(64, 80)), Wkr.v(slice(None), slice(80, 96)), -1.0, ALU.mult)
            copy("dve", Wkrr.v(slice(None), slice(80, 96)), Wkr.v(slice(None), slice(64, 80)))
            wuq4 = wuq.full.rearrange("p j (h d) -> p j h d", h=NH)
            wuqr4 = wuqr.full.rearrange("p j (h d) -> p j h d", h=NH)
            memset("dve", wuqr.v(), 0.0)
            S.op("dve", lambda e: e.tensor_scalar(out=wuqr4[:, :, :, 64:80], in0=wuq4[:, :, :, 80:96], scalar1=-1.0,
                                                    scalar2=None, op0=ALU.mult),
                 reads=[wuq.v()], writes=[wuqr.v()])
            S.op("dve", lambda e: e.tensor_copy(out=wuqr4[:, :, :, 80:96], in_=wuq4[:, :, :, 64:80]),
                 reads=[wuq.v()], writes=[wuqr.v()])
            if l > 0:
                S.dma("sp", [lambda e: e.dma_start(out=cosT.v().ap, in_=tabs.ap()[0:128, :]),
                             lambda e: e.dma_start(out=sinT.v().ap, in_=tabs.ap()[128:256, :])],
                      "tabs_ld", reads=[tabs_v], writes=[cosT.v(), sinT.v()])

            for c in range(NCH):
                cs = slice(c * C, (c + 1) * C)
                ls = slice(NT + c * C, NT + (c + 1) * C)
                hT = hT2[c % 2]
                if c == 0:
                    h_from(l, 0, lambda k: xT.v(k, cs), rstd1_all.v(cs), lambda k: hT.v(k), tmpA.v(), tmpB.v())
                bkv = [PB[next_bank()] for _ in range(2)]
                for j in range(2):
                    mm_group(bkv[j].v(), [(Wkv.v(k, slice(j * 128, (j + 1) * 128)), hT.v(k)) for k in range(8)])
                    act(sql.v(j), bkv[j].v(), AF.Square)
                bs = PB[next_bank()]
                mm_group(bs.v(), [(ones_b.v(), sql.v(j)) for j in range(2)])
                rstd_from(bs.v(), rstdkv.v(), 256, tmpA.v())
                for j in range(2):
                    stt(ctx_ckv.v(j, ls), bkv[j].v(), svc(SV_KVG + l * 2 + j), rstdkv.v(), ALU.mult, ALU.mult)
                bkr = PB[next_bank()]
                bkrr = PB[next_bank()]
                mm_group(bkr.v(p=(0, 96)), [(Wkr.v(k), hT.v(k)) for k in range(8)])
                mm_group(bkrr.v(p=(0, 96)), [(Wkrr.v(k), hT.v(k)) for k in range(8)])
                R = (64, 96)
                tt("dve", tmpA.v(p=R), bkr.v(p=R), cosT.v(cs, p=R), ALU.mult)
                tt("dve", tmpB.v(p=R), bkrr.v(p=R), sinT.v(cs, p=R), ALU.mult)
                tt("dve", ctx_kr.v(ls, p=R), tmpA.v(p=R), tmpB.v(p=R), ALU.add)
                bq = [PB[next_bank()] for _ in range(3)]
                for j in range(3):
                    mm_group(bq[j].v(), [(Wq.v(k, slice(j * 128, (j + 1) * 128)), hT.v(k)) for k in range(8)])
                    act(sql.v(j), bq[j].v(), AF.Square)
                bs = PB[next_bank()]
                mm_group(bs.v(), [(ones_b.v(), sql.v(j)) for j in range(3)])
                rstd_from(bs.v(), rstdq.v(), 384, tmpA.v())
                ts("dve", rstdq.v(), rstdq.v(), SCALE, ALU.mult)
                for j in range(3):
                    stt(cqn.v(j), bq[j].v(), svc(SV_QNG + l * 3 + j), rstdq.v(), ALU.mult, ALU.mult)
                Q = (0, 96)
                if c == NCH - 1:
                    bh = PB[next_bank()]
                    for g in range(4):
                        mm_group(bh.v(slice(g * 16, (g + 1) * 16)),
                                 [(Wu.v(k, slice(g * 128, (g + 1) * 128)), hT.v(k, slice(C - 16, C))) for k in range(8)])
                    S.op("dve", lambda e, bh=bh: e.tensor_copy(
                        out=halo_sb.full, in_=bh.full[:, 0:64].rearrange("p (g t) -> p g t", g=4)),
                        reads=[bh.v(slice(0, 64))], writes=[halo_sb.v()])

                if c + 1 < NCH:
                    cs2 = slice((c + 1) * C, (c + 2) * C)
                    hTn = hT2[(c + 1) % 2]
                    h_from(l, 0, lambda k: xT.v(k, cs2), rstd1_all.v(cs2), lambda k: hTn.v(k), tmpA.v(), tmpB.v())
                for h in range(NH):
                    ba = PB[next_bank()]
                    bb = PB[next_bank()]
                    hs = slice(h * 96, (h + 1) * 96)
                    mm_group(ba.v(p=Q), [(wuq.v(j, hs), cqn.v(j)) for j in range(3)])
                    mm_group(bb.v(p=Q), [(wuqr.v(j, hs), cqn.v(j)) for j in range(3)])
                    tt("dve", tmpA.v(p=Q), ba.v(p=Q), cosT.v(cs, p=Q), ALU.mult)
                    tt("dve", tmpB.v(p=Q), bb.v(p=Q), sinT.v(cs, p=Q), ALU.mult)
                    tt("dve", qT_all.v(h, cs, p=Q), tmpA.v(p=Q), tmpB.v(p=Q), ALU.add)
            wuk = sb(R_X, [128, 2, 512], BF16)
            wuv = sb(R_X + 2048, [128, 2, 512], BF16)
            wload(wuk.v(), wuk_d[l].rearrange("(j p) n -> p j n", p=128), "wuk")
            wload(wuv.v(), wuv_d[l].rearrange("(j p) n -> p j n", p=128), "wuv")
            exs_v = dview(exs[l], f"exs{l}")
            exd_v = dview(exd[l], f"exd{l}")
            hls_v = dview(hls[l], f"hls{l}")
            hld_v = dview(hld[l], f"hld{l}")
            own = slice(NT, CTX)
            S.dma("sp", [lambda e, l=l: e.dma_start(out=exs[l].ap()[0:128, :], in_=ctx_ckv.v(0, own).ap),
                         lambda e, l=l: e.dma_start(out=exs[l].ap()[128:256, :], in_=ctx_ckv.v(1, own).ap),
                         lambda e, l=l: e.dma_start(out=exs[l].ap()[256:288, :], in_=ctx_kr.v(own, p=(64, 96)).ap)],
                  f"exst{l}", reads=[ctx_ckv.v(0, own), ctx_ckv.v(1, own), ctx_kr.v(own, p=(64, 96))],
                  writes=[exs_v])
            S.dma("sp", [lambda e, l=l: e.dma_start(out=hls[l].ap(), in_=halo_sb.full.rearrange("p g t -> p (g t)"))],
                  f"hlst{l}", reads=[halo_sb.v()], writes=[hls_v])
            RG = [[0, 1], [2, 3], [4, 5], [6, 7]]
            cc_tok = S.custom("pool", lambda e, l=l: e.collective_compute(
                "AllGather", ALU.bypass, replica_groups=RG, ins=[exs[l].ap().opt()], outs=[exd[l].ap().opt()]),
                reads=[exs_v], writes=[exd_v], key=f"cc{l}")
            S.wait_tok("pool", cc_tok)
            S.custom("pool", lambda e, l=l: e.collective_compute(
                "AllGather", ALU.bypass, replica_groups=RG, ins=[hls[l].ap().opt()], outs=[hld[l].ap().opt()]),
                reads=[hls_v], writes=[hld_v], key=f"cch{l}")
            rem = slice(0, NT)
            for j in range(2):
                S.dma("sp", [(lambda l=l, j=j: lambda e: e.dma_start(out=ctx_ckv.v(j, rem).ap,
                                                                      in_=exd[l].ap()[j * 128:(j + 1) * 128, :]))()],
                      f"exld{l}_{j}", reads=[exd_v], writes=[ctx_ckv.v(j, rem)])
            S.dma("sp", [lambda e, l=l: e.dma_start(out=ctx_kr.v(rem, p=(64, 96)).ap, in_=exd[l].ap()[256:288, :])],
                  f"exldk{l}", reads=[exd_v], writes=[ctx_kr.v(rem, p=(64, 96))])
            S.dma("sp", [lambda e, l=l: e.dma_start(out=halo_in.full.rearrange("p g t -> p (g t)"), in_=hld[l].ap()[0:128, :])],
                  f"exldh{l}", reads=[hld_v], writes=[halo_in.v()])

            if debug and l == 0:
                S.dma("sp", [lambda e: e.dma_start(out=dbg_q, in_=qT_all.full.rearrange("p h t -> p (h t)")),
                             lambda e: e.dma_start(out=dbg_ckv, in_=ctx_ckv.full.rearrange("p j t -> p (j t)")),
                             lambda e: e.dma_start(out=dbg_kr, in_=ctx_kr.full)],
                      "dbg1", reads=[qT_all.v(), ctx_ckv.v(), ctx_kr.v()], writes=[View(None, ("dram", "dbg1"), (0, 1, 0, 1))])
            o = R_X
            o += 4096
            KT = [sb(o + i * 8192, [128, CTX], BF16) for i in range(2)]; o += 16384
            VV = [sb(o + i * 8192, [128, 32, 128], BF16) for i in range(2)]; o += 16384
            NPT = 6
            LA = 4
            PT = [sb(o + i * 1024, [128, C], BF16) for i in range(NPT)]; o += NPT * 1024
            Osb = [sb(o + i * 2048, [128, C], F32) for i in range(2)]; o += 4096
            rdn = [sb(o + i * 2048, [128, C], F32) for i in range(2)]; o += 4096
            assert o <= ARENA
            o = R_CTX
            Wu2 = sb(o, [128, 8, 512], BF16); o += 8192
            Wga = sb(o, [128, 8, D], BF16); o += 16384
            Wgb = sb(o, [128, 8, D], BF16); o += 16384
            wpl = sb(o, [128, 4, 128], BF16); o += 1024
            ppl = sb(o, [128, 4, D], BF16); o += 8192
            pat = sb(o, [128, 4, D], BF16); o += 8192
            wo = sb(o, [128, 8, D], BF16); o += 16384
            G_o = o
            S_BANKS = (0, 1, 2, 3, 4, 7)

            def build_steps(h):
                lower = h < 4
                kt_t = KT[h % 2]
                v_t = VV[h % 2]
                voff = 0 if lower else 64
                vcol = 64 if lower else 0

                def kstep(cc):
                    def f():
                        bk = PB[next_bank(S_BANKS)]
                        ks = slice(cc * C, (cc + 1) * C)
                        mm_group(bk.v(p=(0, 64)),
                                 [(wuk.v(j, slice(h * 64, (h + 1) * 64)), ctx_ckv.v(j, ks)) for j in range(2)])
                        copy("dve", kt_t.v(ks, p=(0, 64)), bk.v(p=(0, 64)))
                    return f

                def krope(part):
                    def f():
                        copy("dve", kt_t.v(part, p=(64, 96)), ctx_kr.v(part, p=(64, 96)))
                    return f

                def vinit():
                    memset("dve", v_t.v(), 0.0)
                    S.op("dve", lambda e: e.tensor_copy(out=v_t.full[:, 0:16, vcol], in_=flag16.full),
                         reads=[flag16.v()], writes=[v_t.v(slice(0, 16))])
                    S.op("dve", lambda e: e.memset(v_t.full[:, 16:32, vcol], 1.0), writes=[v_t.v(slice(16, 32))])

                def vstep(kb):
                    def f():
                        bk = PB[next_bank(S_BANKS)]
                        for i in range(8):
                            kt = kb * 8 + i
                            mm_group(bk.v(slice(i * 64, (i + 1) * 64)),
                                     [(ctx_ckv.v(j, slice(kt * 128, (kt + 1) * 128)), wuv.v(j, slice(h * 64, (h + 1) * 64)))
                                      for j in range(2)])
                        src = bk.full.rearrange("p (i d) -> p i d", i=8)
                        dstap = v_t.full[:, kb * 8:(kb + 1) * 8, voff:voff + 64]
                        if kb < 2:
                            S.op("dve", lambda e: e.tensor_scalar(out=dstap, in0=src, scalar1=svc(SV_FLAG).ap,
                                                                    scalar2=None, op0=ALU.mult),
                                 reads=[bk.v(), svc(SV_FLAG)], writes=[v_t.v(slice(kb * 8, (kb + 1) * 8))])
                        else:
                            S.op("dve", lambda e: e.tensor_copy(out=dstap, in_=src),
                                 reads=[bk.v()], writes=[v_t.v(slice(kb * 8, (kb + 1) * 8))])
                    return f

                return ([vinit] + [kstep(cc) for cc in (4, 5, 6, 7)] + [krope(slice(NT, CTX)), vstep(2), vstep(3)]
                        + [kstep(cc) for cc in (0, 1, 2, 3)] + [krope(slice(0, NT)), vstep(0), vstep(1)])

            for st in build_steps(0):
                st()
            pti = 0
            oi = 0
            items = []
            step_at = {}
            for h in range(NH):
                base = len(items)
                for c in range(NCH):
                    tiles = [(kt, 0) for kt in range(16 + 4 * c)] + [(16 + 4 * c + i, i) for i in range(4)]
                    for ti, (kt, i0) in enumerate(tiles):
                        items.append((h, c, kt, i0, ti, len(tiles)))
                n_h = len(items) - base
                nxt = build_steps(h + 1) if h + 1 < NH else []
                if nxt:
                    gap = n_h // (len(nxt) + 1)
                    for si, st in enumerate(nxt):
                        step_at.setdefault(base + (si + 1) * gap, []).append(st)
                if l == 0:
                    for at in (3, 38, 73):
                        if mod_pieces:
                            step_at.setdefault(base + at, []).append(mod_piece_step(mod_pieces.pop(0), S_BANKS))
                if h == 4:
                    step_at.setdefault(base + 12, []).append(
                        lambda: wload(Wgb.v(), winv[:, :, 2208:3232], "Wgb"))
                if h == 5:
                    step_at.setdefault(base + 12, []).append(
                        lambda: wload(wpl.v(), wpool_d[l].rearrange("g c d -> c g d"), "wpl"))
                if h == 7:
                    step_at.setdefault(base + 12, []).append(
                        lambda: (wload(Wu2.v(), winv[:, :, 0:512], "Wu2"),
                                 wload(ppl.v(), ppool_d[l].rearrange("(g p) n -> p g n", p=128), "ppl"),
                                 wload(Wga.v(), winv[:, :, 1184:2208], "Wga")))
            n_items = len(items)
            deferred = []
            pend = {}
            obs = {}
            for idx in range(n_items + LA):
                if idx < n_items:
                    h, c, kt, i0, ti, nt_ = items[idx]
                    q0 = i0 * 128
                    n = C - q0
                    sbk = PB[next_bank(S_BANKS)]
                    sv_ = sbk.v(slice(0, n))
                    qv = qT_all.v(h, slice(c * C + q0, (c + 1) * C), p=(0, 96))
                    kv = KT[h % 2].v(slice(kt * 128, (kt + 1) * 128), p=(0, 96))
                    mm_group(sv_, [(kv, qv)])
                    pt = PT[pti % NPT]
                    pti += 1
                    pv = pt.v(slice(0, n))
                    act(pv, sv_, AF.Exp)
                    if kt >= 16 + 4 * c:
                        tt("dve", pt.v(slice(0, 128)), pt.v(slice(0, 128)), tri.v(), ALU.mult)
                    pend[idx] = pv
                    if ti == 0:
                        obs[(h, c)] = (PB[5 + oi % 2], Osb[oi % 2], rdn[oi % 2])
                        oi += 1
                j = idx - LA
                if j >= 0:
                    h, c, kt, i0, ti, nt_ = items[j]
                    lower = h < 4
                    q0 = i0 * 128
                    ob, osb, rden = obs.pop((h, c)) if ti == nt_ - 1 else obs[(h, c)]
                    ov = ob.v(slice(q0, C))
                    vv = VV[h % 2].v(kt)
                    pv = pend.pop(j)
                    S.op("pe", (lambda ov=ov, vv=vv, pv=pv, ti=ti, nt_=nt_: lambda e: e.matmul(
                        ov.ap, vv.ap, pv.ap, start=(ti == 0), stop=(ti == nt_ - 1)))(),
                        reads=[vv, pv], writes=[ov])
                    if ti == nt_ - 1:
                        copy("dve", osb.v(), ob.v())
                        rp = (64, 65) if lower else (0, 1)
                        S.op("dve", (lambda osb=osb, rden=rden, rp=rp: lambda e: e.reciprocal(
                            out=rden.v(p=rp).ap, in_=osb.v(p=rp).ap))(),
                            reads=[osb.v(p=rp)], writes=[rden.v(p=rp)])

                        def fin(h=h, c=c, osb=osb, rden=rden, rp=rp, lower=lower):
                            bcb = PB[next_bank(S_BANKS)]
                            mm_group(bcb.v(), [((ones_f if lower else sel_f).v(p=rp), rden.v(p=rp))])
                            rows = (0, 64) if lower else (64, 128)
                            tt("dve", attn_o.v(h % 4, slice(c * C, (c + 1) * C), p=rows), osb.v(p=rows),
                               bcb.v(p=rows), ALU.mult)
                        deferred.append((idx + 16, fin))
                for st in step_at.get(idx, []):
                    st()
                while deferred and deferred[0][0] <= idx:
                    deferred.pop(0)[1]()
            for _, fn in deferred:
                fn()

            if debug and l == 0:
                S.dma("sp", [lambda e: e.dma_start(out=dbg_ao, in_=attn_o.full.rearrange("p j t -> p (j t)"))],
                      "dbg2", reads=[attn_o.v()], writes=[View(None, ("dram", "dbg2"), (0, 1, 0, 1))])
            o = G_o
            hTg2 = [sb(o + i * 4096, [128, 8, GC], BF16) for i in range(2)]; o += 8192
            gA = sb(o, [128, GC], F32); o += 1024
            gB = sb(o, [128, GC], F32); o += 1024
            hA = sb(o, [128, GC], F32); o += 1024
            hB = sb(o, [128, GC], F32); o += 1024
            uT = sb(o, [128, 4, 16 + GC], F32); o += 4 * (16 + GC) * 4
            wA = sb(o, [128, 4, 16 + GC], F32); o += 4 * (16 + GC) * 4
            wB = sb(o, [128, 4, 16 + GC], F32); o += 4 * (16 + GC) * 4
            pTt = sb(o, [128, 4, GC], BF16); o += 4 * GC * 2
            yTt = sb(o, [128, 4, GC], BF16); o += 4 * GC * 2
            mrg = sb(o, [128, 8, GC], BF16); o += 8 * GC * 2
            sgA = sb(o, [128, GC], F32); o += 1024
            sgB = sb(o, [128, GC], F32); o += 1024
            pav = pattn_d[l].rearrange("(hh j v) n -> hh v j n", hh=2, j=4)
            S.dma("pool", [lambda e, pav=pav: e.dma_start(out=pat.v(p=(0, 64)).ap, in_=pav[0]),
                           lambda e, pav=pav: e.dma_start(out=pat.v(p=(64, 128)).ap, in_=pav[1])], "pat", writes=[pat.v()])
            wload(wo.v(), wout_d[l].rearrange("(k p) n -> p k n", p=128), "wo")
            yT2 = [yTt, sb(o, [128, 4, GC], BF16)]; o += 4 * GC * 2
            sqe = sb(o, [128, 8, GC], BF16); o += 8 * GC * 2
            assert o <= ARENA - 8192, o
            ts("dve", uT.v(slice(None), slice(0, 16)), halo_in.v(), svc(SV_FLAG), ALU.mult)
            NG = NT // GC
            L_ = 16 + GC

            def gsl(gc):
                return slice(gc * GC, (gc + 1) * GC)

            def P0(gc):
                hTg = hTg2[gc % 2]
                gs = gsl(gc)
                h_from(l, 0, lambda k: xT.v(k, gs), rstd1_all.v(gs), lambda k: hTg.v(k), hA.v(), hB.v())

            def P1(gc):
                hTg = hTg2[gc % 2]
                for half in range(2):
                    bk = PB[next_bank()]
                    for gg in range(2):
                        g = half * 2 + gg
                        mm_group(bk.v(slice(gg * GC, (gg + 1) * GC)),
                                 [(Wu2.v(k, slice(g * 128, (g + 1) * 128)), hTg.v(k)) for k in range(8)])
                    S.op("act", lambda e, bk=bk, half=half: e.copy(
                        out=uT.full[:, half * 2:half * 2 + 2, 16:16 + GC],
                        in_=bk.full.rearrange("p (g t) -> p g t", g=2)),
                        reads=[bk.v()], writes=[uT.v(slice(half * 2, half * 2 + 2))])

            def P2a(gc):
                E = "pool"
                tt(E, wA.v(slice(0, 4), slice(1, L_)), uT.v(slice(0, 4), slice(1, L_)), uT.v(slice(0, 4), slice(0, L_ - 1)), ALU.add)
                tt(E, wB.v(slice(1, 4), slice(3, L_)), wA.v(slice(1, 4), slice(3, L_)), wA.v(slice(1, 4), slice(1, L_ - 2)), ALU.add)
                tt(E, wA.v(slice(2, 4), slice(7, L_)), wB.v(slice(2, 4), slice(7, L_)), wB.v(slice(2, 4), slice(3, L_ - 4)), ALU.add)
                tt(E, wB.v(slice(3, 4), slice(15, L_)), wA.v(slice(3, 4), slice(15, L_)), wA.v(slice(3, 4), slice(7, L_ - 8)), ALU.add)

            def P2a2(gc):
                for g in range(4):
                    w = 2 << g
                    cur = wA if g % 2 == 0 else wB
                    stt(pTt.v(g), cur.v(g, slice(16, L_)), 1.0 / w, uT.v(g, slice(16, L_)), ALU.mult, ALU.subtract)
                    if gc == 0:
                        tt("dve", gA.v(slice(0, 16)), cur.v(g, slice(16, 32)), svc(SV_INVCNT + g * 16, 16), ALU.mult)
                        tt("dve", pTt.v(g, slice(0, 16)), gA.v(slice(0, 16)), uT.v(g, slice(16, 32)), ALU.subtract)
                S.op("dve", lambda e: e.tensor_copy(out=uT.full[:, :, 0:16], in_=uT.full[:, :, GC:GC + 16]),
                     reads=[uT.v()], writes=[uT.v()])

            def P2b(gc):
                yT = yT2[gc % 2]
                bk = PB[next_bank()]
                bk2 = PB[next_bank()]
                for g in range(4):
                    dst = (bk if g < 2 else bk2).v(slice((g % 2) * GC, (g % 2 + 1) * GC))
                    mm_group(dst, [(wpl.v(g), pTt.v(g))])
                for g in range(4):
                    dst = (bk if g < 2 else bk2).v(slice((g % 2) * GC, (g % 2 + 1) * GC))
                    ts("dve", yT.v(g), dst, svc(SV_PSC + l * 4 + g), ALU.mult)

            def M1(gc, mrange):
                hTg = hTg2[gc % 2]
                yT = yT2[gc % 2]
                gs = gsl(gc)
                for m in mrange:
                    ms = slice(m * 128, (m + 1) * 128)
                    b1 = PB[next_bank()]
                    b2 = PB[next_bank()]
                    ya = b1.v(slice(0, GC))
                    ga = b1.v(slice(GC, 2 * GC))
                    yb = b2.v(slice(0, GC))
                    gb = b2.v(slice(GC, 2 * GC))
                    mm_group(ya, [(ppl.v(g, ms), yT.v(g)) for g in range(4)])
                    mm_group(ga, [(Wga.v(k, ms), hTg.v(k)) for k in range(8)])
                    mm_group(yb, [(pat.v(j, ms), attn_o.v(j, gs)) for j in range(4)])
                    mm_group(gb, [(Wgb.v(k, ms), hTg.v(k)) for k in range(8)])
                    act(sgA.v(), ga, AF.Sigmoid)
                    act(sgB.v(), gb, AF.Sigmoid)
                    tt("dve", gA.v(), sgA.v(), ya, ALU.mult)
                    tt("dve", gB.v(), sgB.v(), yb, ALU.mult)
                    tt("dve", mrg.v(m), gA.v(), gB.v(), ALU.add)

            def M2(gc):
                gs = gsl(gc)
                for m in range(8):
                    ms = slice(m * 128, (m + 1) * 128)
                    bk = PB[next_bank()]
                    ov = bk.v(slice(0, GC))
                    mm_group(ov, [(wo.v(k, ms), mrg.v(k)) for k in range(8)])
                    stt(xT.v(m, gs), ov, modT.v(l, slice(16 + m, 17 + m)), xT.v(m, gs), ALU.mult, ALU.add)

            def FsA(gc):
                gs = gsl(gc)
                for k in range(8):
                    act(sqe.v(k), xT.v(k, gs), AF.Square)

            def FsB(gc):
                gs = gsl(gc)
                bk = PB[next_bank()]
                st = bk.v(slice(0, GC))
                mm_group(st, [(ones_b.v(), sqe.v(k)) for k in range(8)])
                rstd_from(st, rstd1_all.v(gs), D, sgA.v())

            P0(0)
            P1(0)
            P2a(0)
            P2a2(0)
            P2b(0)
            for gc in range(NG):
                M1(gc, range(0, 2))
                if gc + 1 < NG:
                    P0(gc + 1)
                if gc >= 1:
                    FsA(gc - 1)
                M1(gc, range(2, 4))
                if gc + 1 < NG:
                    P1(gc + 1)
                    P2a(gc + 1)
                M1(gc, range(4, 6))
                if gc >= 1:
                    FsB(gc - 1)
                M1(gc, range(6, 8))
                if gc + 1 < NG:
                    P2a2(gc + 1)
                M2(gc)
                if gc + 1 < NG:
                    P2b(gc + 1)
            FsA(NG - 1)
            FsB(NG - 1)

            h2 = sb(R_AO, [128, 8, NT], BF16)
            o = R_AO + 32768
            W1 = [sb(o + i * 16384, [128, 8, 512], BF16) for i in range(2)]
            W2 = [sb(o + 8192 + i * 16384, [128, 4, D], BF16) for i in range(2)]
            o += 32768
            hid = [sb(o + i * 4096, [128, 4, C], BF16) for i in range(2)]; o += 8192
            assert o <= R_X
            o = R_X + 32768
            sqf = sb(o, [128, 8, C], BF16); o += 8192
            fA = sb(o, [128, C], F32); o += 2048
            fB = sb(o, [128, C], F32); o += 2048
            frs = sb(o, [128, NT], F32); o += 8192
            rl = [sb(o + i * 2048, [128, C], F32) for i in range(2)]; o += 4096
            assert o <= ARENA
            w1v = wff1_d[l].rearrange("(k p) n -> p k n", p=128)
            w2v = wff2_d[l].rearrange("(j p) n -> p j n", p=128)

            def ffload(g):
                wload(W1[g % 2].v(), w1v[:, :, g * 512:(g + 1) * 512], f"W1_{g % 2}")
                wload(W2[g % 2].v(), w2v[:, g * 4:(g + 1) * 4, :], f"W2_{g % 2}")

            for c in range(NCH):
                cs = slice(c * C, (c + 1) * C)
                h_from(l, 1, lambda k: xT.v(k, cs), rstd1_all.v(cs), lambda k: h2.v(k, cs), fA.v(), fB.v())
            ffload(0)
            ri = [0]

            def ff_up(g, c):
                w1 = W1[g % 2]
                cs = slice(c * C, (c + 1) * C)
                hd = hid[(g * NCH + c) % 2]
                for hc in range(4):
                    bk = PB[next_bank()]
                    mm_group(bk.v(), [(w1.v(k, slice(hc * 128, (hc + 1) * 128)), h2.v(k, cs)) for k in range(8)])
                    r = rl[ri[0] % 2]
                    ri[0] += 1
                    act(r.v(), bk.v(), AF.Relu)
                    tt("dve", hd.v(hc), r.v(), r.v(), ALU.mult)

            def ff_down(g, c):
                w2 = W2[g % 2]
                cs = slice(c * C, (c + 1) * C)
                hd = hid[(g * NCH + c) % 2]
                for m in range(8):
                    bk = PB[next_bank()]
                    mm_group(bk.v(), [(w2.v(hc, slice(m * 128, (m + 1) * 128)), hd.v(hc)) for hc in range(4)])
                    stt(xT.v(m, cs), bk.v(), modT.v(l, slice(40 + m, 41 + m)), xT.v(m, cs), ALU.mult, ALU.add)

            def early_stats(c):
                cs = slice(c * C, (c + 1) * C)
                for k in range(8):
                    act(sqf.v(k), xT.v(k, cs), AF.Square)
                bk = PB[next_bank()]
                mm_group(bk.v(), [(ones_b.v(), sqf.v(k)) for k in range(8)])
                rstd_from(bk.v(), rstd1_all.v(cs), D, fA.v())

            seq = [(g, c) for g in range(FFG) for c in range(NCH)]
            loaded = {0}
            ff_up(*seq[0])
            for i, (g, c) in enumerate(seq):
                if c == 0 and g + 1 < FFG and (g + 1) not in loaded:
                    ffload(g + 1)
                    loaded.add(g + 1)
                if i + 1 < len(seq):
                    ff_up(*seq[i + 1])
                ff_down(g, c)
                if g == FFG - 1 and c >= 1:
                    early_stats(c - 1)
            early_stats(NCH - 1)

        o = R_AO
        osq = sb(o, [128, 8, C], BF16); o += 8192
        ors = sb(o, [128, C], F32); o += 2048
        otmp = sb(o, [128, C], F32); o += 2048
        obuf = [sb(o + i * 16384, [128, 8, C], F32) for i in range(2)]; o += 32768
        outv = out_d.rearrange("(k p) t -> p k t", p=128)
        out_tok = None
        out_toks = []
        for c in range(NCH):
            cs = slice(c * C, (c + 1) * C)
            ob_ = obuf[c % 2]
            for k in range(8):
                stt(ob_.v(k), xT.v(k, cs), svc(SV_FING + k), rstd1_all.v(cs), ALU.mult, ALU.mult)
            out_tok = S.dma("sp", [(lambda c=c, ob_=ob_: lambda e: e.dma_start(out=outv[:, :, c * C:(c + 1) * C], in_=ob_.v().ap))()],
                            f"outst{c}", reads=[ob_.v()], writes=[View(None, ("dram", "out"), (0, 1, c, c + 1))])
            out_toks.append(out_tok)
        for tk in out_toks:
            S.wait_tok("sp", tk)
        S.emit(block)
    return nc


_NC_CACHE = {}


def _inv_freq():
    return (np.float32(10000.0) ** (-np.arange(0, 32, 2, dtype=np.float32) / np.float32(32))).astype(np.float32)


def kernel(x, c, positions, ln1_g, ln2_g, w_ada, b_ada, w_in, q_norm_g, w_uq, kv_norm_g, w_uk, w_uv,
           w_pool, pool_scale, p_pool, p_attn, w_out, w_ff1, w_ff2, final_g):
    f32 = np.float32
    x = np.asarray(x, f32)
    c = np.asarray(c, f32)
    positions = np.asarray(positions, np.int32)
    B, S_, _ = x.shape
    dbg = _NC_CACHE.get("debug", False)
    if "nc" not in _NC_CACHE:
        _NC_CACHE["nc"] = build_nc(debug=dbg)
    nc = _NC_CACHE["nc"]

    def fm(v, n):
        return np.asarray(v, f32).reshape(n, 128).T

    tri = (np.arange(128)[None, :] >= np.arange(128)[:, None]).astype(ml_dtypes.bfloat16)
    invf = _inv_freq()
    shared = {
        "tri": tri,
        "w_ada": np.ascontiguousarray(np.asarray(w_ada, f32)),
        "w_in": np.ascontiguousarray(np.asarray(w_in, f32)),
        "w_uq": np.ascontiguousarray(np.asarray(w_uq, f32).reshape(DEPTH, 384, 768)),
        "w_uk": np.ascontiguousarray(np.asarray(w_uk, f32).reshape(DEPTH, 256, 512)),
        "w_uv": np.ascontiguousarray(np.asarray(w_uv, f32).reshape(DEPTH, 256, 512)),
        "w_pool": np.ascontiguousarray(np.asarray(w_pool, f32)),
        "p_pool": np.ascontiguousarray(np.asarray(p_pool, f32)),
        "p_attn": np.ascontiguousarray(np.asarray(p_attn, f32)),
        "w_out": np.ascontiguousarray(np.asarray(w_out, f32)),
        "w_ff1": np.ascontiguousarray(np.asarray(w_ff1, f32)),
        "w_ff2": np.ascontiguousarray(np.asarray(w_ff2, f32)),
    }
    in_maps = []
    for core in range(8):
        b, half = core // 2, core % 2
        t0 = half * NT
        sv = np.zeros((128, NV), f32)
        for l in range(DEPTH):
            sv[:, SV_LN1 + l * 8:SV_LN1 + l * 8 + 8] = fm(ln1_g[l], 8)
            sv[:, SV_LN2 + l * 8:SV_LN2 + l * 8 + 8] = fm(ln2_g[l], 8)
            sv[:, SV_BADA + l * 48:SV_BADA + l * 48 + 48] = fm(b_ada[l], 48)
            sv[:, SV_QNG + l * 3:SV_QNG + l * 3 + 3] = fm(q_norm_g[l], 3)
            sv[:, SV_KVG + l * 2:SV_KVG + l * 2 + 2] = fm(kv_norm_g[l], 2)
            sv[:, SV_PSC + l * 4:SV_PSC + l * 4 + 4] = fm(pool_scale[l], 4)
        sv[:, SV_FING:SV_FING + 8] = fm(final_g, 8)
        sv[64:96, SV_INVF] = np.tile(invf, 2)
        sv[:, SV_FLAG] = float(half)
        for g in range(4):
            w = 2 << g
            tpos = t0 + np.arange(16)
            cnt = np.minimum(tpos + 1, w).astype(f32)
            sv[:, SV_INVCNT + g * 16:SV_INVCNT + g * 16 + 16] = (f32(1.0) / cnt)[None, :]
        sv[:, SV_C:SV_C + 8] = fm(c[b], 8)
        m = dict(shared)
        m["xT"] = np.ascontiguousarray(x[b, t0:t0 + NT, :].T)
        m["pos"] = np.ascontiguousarray(positions[b, t0:t0 + NT].reshape(1, NT))
        m["smallv"] = sv
        in_maps.append(m)
    res = run_bass_kernel_spmd(nc, in_maps, core_ids=list(range(8)))
    _NC_CACHE["res"] = res.results if dbg else None
    out = np.empty((B, S_, D), f32)
    for core in range(8):
        b, half = core // 2, core % 2
        t0 = half * NT
        out[b, t0:t0 + NT, :] = np.asarray(res.results[core]["outT"], f32).T
    return out
```

```python
import math
import numpy as np
import ml_dtypes
import concourse.bass as bass
import concourse.mybir as mybir
from concourse.bass_utils import run_bass_kernel_spmd

F32 = mybir.dt.float32
BF16 = mybir.dt.bfloat16
I32 = mybir.dt.int32
U8 = mybir.dt.uint8
ALU = mybir.AluOpType
AF = mybir.ActivationFunctionType

D = 1024
NT = 2048
CTX = 4096
C = 512
NCH = NT // C
GC = 256
DEPTH = 2
NH = 8
EPS = 1e-6
SCALE = 1.0 / math.sqrt(96.0)
FFG = 8
ESZ = {F32: 4, BF16: 2, I32: 4, U8: 1}

SV_LN1, SV_LN2, SV_BADA, SV_QNG, SV_KVG, SV_PSC, SV_FING = 0, 16, 32, 128, 134, 138, 146
SV_INVF, SV_FLAG, SV_INVCNT, SV_C = 154, 155, 156, 220
NV = 228
ARENA = 207 * 1024


class View:
    __slots__ = ("ap", "space", "iv")

    def __init__(self, ap, space, iv):
        self.ap = ap
        self.space = space
        self.iv = iv


class T:
    def __init__(self, base_ap, space, off, shape, dt):
        self.shape = tuple(shape)
        self.dt = dt
        self.space = space
        self.off = off
        es = ESZ[dt]
        self.es = es
        free = self.shape[1:]
        n = int(np.prod(free))
        self.nbytes = n * es
        strides = []
        s = 1
        for d in reversed(free):
            strides.append(s)
            s *= d
        self.strides = list(reversed(strides))
        if space == "sb":
            ap = base_ap[:, off:off + n * es]
            if dt != U8:
                ap = ap.bitcast(dt)
        else:
            ap = base_ap[:, :]
        if len(free) == 2:
            ap = ap.rearrange("p (a b) -> p a b", a=free[0])
        elif len(free) == 3:
            ap = ap.rearrange("p (a b c) -> p a b c", a=free[0], b=free[1])
        self.full = ap

    def v(self, *idx, p=None):
        free = self.shape[1:]
        idx = list(idx) + [slice(None)] * (len(free) - len(idx))
        lo = 0
        hi = 0
        key = []
        for i, d, st in zip(idx, free, self.strides):
            if isinstance(i, int):
                a, b = i, i + 1
                key.append(i)
            else:
                a = 0 if i.start is None else i.start
                b = d if i.stop is None else i.stop
                key.append(slice(a, b))
            assert 0 <= a < b <= d, (idx, self.shape)
            lo += a * st
            hi += (b - 1) * st
        p0, p1 = (0, self.shape[0]) if p is None else p
        assert p1 <= self.shape[0]
        ap = self.full[(slice(p0, p1),) + tuple(key)]
        return View(ap, self.space, (p0, p1, self.off + lo * self.es, self.off + (hi + 1) * self.es))


class Sched:
    EPOCH = 12000

    def __init__(self, nc, es):
        self.nc = nc
        self.es = es
        self.eng = {"pe": nc.tensor, "act": nc.scalar, "dve": nc.vector, "pool": nc.gpsimd, "sp": nc.sync}
        self.streams = {e: [] for e in self.eng}
        self.cnt = {e: 0 for e in self.eng}
        self.psems = {e: [] for e in self.eng}
        self.waited = {e: {} for e in self.eng}
        self.recs = {}
        self.dsem = {}
        self.pbank = {}
        self.nsem = 0

    def new_sem(self, name):
        self.nsem += 1
        return self.es.enter_context(self.nc.semaphore(name))

    def _ptok(self, e):
        self.cnt[e] += 1
        n = self.cnt[e]
        ei = (n - 1) // self.EPOCH
        while len(self.psems[e]) <= ei:
            self.psems[e].append(self.new_sem(f"p_{e}_{len(self.psems[e])}"))
        return ("p", e, n)

    def _tok_wait(self, tok):
        if tok[0] == "p":
            _, e, n = tok
            ei = (n - 1) // self.EPOCH
            return self.psems[e][ei], n - ei * self.EPOCH
        _, key, val = tok
        return self.dsem[key][0], val

    @staticmethod
    def _ov(a, b):
        return a[0] < b[1] and b[0] < a[1] and a[2] < b[3] and b[2] < a[3]

    @staticmethod
    def _cov(a, b):
        return a[0] <= b[0] and a[1] >= b[1] and a[2] <= b[2] and a[3] >= b[3]

    def _deps(self, reads, writes, e=None):
        deps = []
        for v in reads:
            if v.space[0] == "ps":
                st = self.pbank.get(v.space)
                if st is not None and st[0] != e:
                    deps.append(st[1])
                continue
            for r in self.recs.get(v.space, ()):
                if r[1] == "W" and self._ov(r[0], v.iv):
                    deps.append(r[2])
        for v in writes:
            if v.space[0] == "ps":
                st = self.pbank.get(v.space)
                if st is not None and st[0] != e:
                    deps.append(st[1])
                continue
            for r in self.recs.get(v.space, ()):
                if self._ov(r[0], v.iv):
                    deps.append(r[2])
        return deps

    def _record(self, reads, writes, tok):
        ps = [v for v in list(reads) + list(writes) if v.space[0] == "ps"]
        for v in ps:
            self.pbank[v.space] = (tok[1], tok)
        reads = [v for v in reads if v.space[0] != "ps"]
        writes = [v for v in writes if v.space[0] != "ps"]
        for v in writes:
            lst = self.recs.setdefault(v.space, [])
            lst[:] = [r for r in lst if not self._cov(v.iv, r[0])]
            lst.append([v.iv, "W", tok])
        for v in reads:
            lst = self.recs.setdefault(v.space, [])
            lst[:] = [r for r in lst if not (r[1] == "R" and r[2][0] == "p" and tok[0] == "p"
                                             and r[2][1] == tok[1] and self._cov(v.iv, r[0]))]
            lst.append([v.iv, "R", tok])

    def _waits(self, e, deps):
        best = {}
        for tok in deps:
            if tok[0] == "p":
                if tok[1] == e and e in ("pe", "sp"):
                    continue
                k = ("p", tok[1])
                val = tok[2]
            else:
                k = ("d", tok[1])
                val = tok[2]
            if self.waited[e].get(k, 0) >= val:
                continue
            if best.get(k, (0, None))[0] < val:
                best[k] = (val, tok)
        out = []
        for k, (val, tok) in best.items():
            self.waited[e][k] = val
            out.append(self._tok_wait(tok))
        return out

    def op(self, e, fn, reads=(), writes=(), sig=True):
        deps = self._deps(reads, writes, e)
        waits = self._waits(e, deps)
        if sig:
            tok = self._ptok(e)
            sem, val = self._tok_wait(tok)
            self._record(reads, writes, tok)
        else:
            tok = None
            sem = None
        self.streams[e].append((waits, fn, sem, False))
        return tok

    def dma(self, q, fns, key, reads=(), writes=()):
        deps = self._deps(reads, writes)
        waits = self._waits(q, deps)
        if key not in self.dsem:
            self.dsem[key] = [self.new_sem("d_" + key), 0]
        ds = self.dsem[key]
        ds[1] += 16 * len(fns)
        tok = ("d", key, ds[1])
        self._record(reads, writes, tok)
        first = True
        for fn in fns:
            self.streams[q].append((waits if first else [], fn, ds[0], True))
            first = False
        return tok

    def custom(self, e, fn, reads=(), writes=(), key=None, inc=1):
        deps = self._deps(reads, writes)
        waits = self._waits(e, deps)
        if key not in self.dsem:
            self.dsem[key] = [self.new_sem("c_" + key), 0]
        ds = self.dsem[key]
        ds[1] += inc
        tok = ("d", key, ds[1])
        self._record(reads, writes, tok)
        self.streams[e].append((waits, fn, ds[0], "cc"))
        return tok

    def wait_tok(self, e, tok):
        ws = self._waits(e, [tok])
        if ws:
            self.streams[e].append((ws, None, None, False))

    def emit(self, block):
        def run(e, eng):
            for waits, fn, sem, kind in self.streams[e]:
                for s, v in waits:
                    eng.wait_ge(s, v)
                if fn is None:
                    continue
                inst = fn(eng)
                if sem is not None:
                    if kind is True:
                        inst.then_inc(sem, 16)
                    elif kind == "cc":
                        inst.then_inc(sem)
                    else:
                        inst.then_inc(sem, 1)

        @block.tensor
        def _(eng):
            run("pe", eng)

        @block.scalar
        def _(eng):
            run("act", eng)

        @block.vector
        def _(eng):
            run("dve", eng)

        @block.gpsimd
        def _(eng):
            run("pool", eng)

        @block.sync
        def _(eng):
            run("sp", eng)


def build_nc(debug=False):
    from contextlib import ExitStack
    nc = bass.Bass("TRN2", target_bir_lowering=False)
    dr = {}

    def din(name, shape, dt):
        dr[name] = nc.dram_tensor(name, list(shape), dt, kind="ExternalInput").ap()
        return dr[name]

    xT_d = din("xT", [D, NT], F32)
    pos_d = din("pos", [1, NT], I32)
    sv_d = din("smallv", [128, NV], F32)
    tri_d = din("tri", [128, 128], BF16)
    wada_d = din("w_ada", [DEPTH, D, 6 * D], F32)
    win_d = din("w_in", [DEPTH, D, 3232], F32)
    wuq_d = din("w_uq", [DEPTH, 384, 768], F32)
    wuk_d = din("w_uk", [DEPTH, 256, 512], F32)
    wuv_d = din("w_uv", [DEPTH, 256, 512], F32)
    wpool_d = din("w_pool", [DEPTH, 4, 128, 128], F32)
    ppool_d = din("p_pool", [DEPTH, 512, D], F32)
    pattn_d = din("p_attn", [DEPTH, 512, D], F32)
    wout_d = din("w_out", [DEPTH, D, D], F32)
    wff1_d = din("w_ff1", [DEPTH, D, 4 * D], F32)
    wff2_d = din("w_ff2", [DEPTH, 4 * D, D], F32)
    out_d = nc.dram_tensor("outT", [D, NT], F32, kind="ExternalOutput").ap()

    exs = [nc.dram_tensor(f"exs{l}", [288, NT], BF16) for l in range(DEPTH)]
    exd = [nc.dram_tensor(f"exd{l}", [576, NT], BF16) for l in range(DEPTH)]
    hls = [nc.dram_tensor(f"hls{l}", [128, 64], F32) for l in range(DEPTH)]
    hld = [nc.dram_tensor(f"hld{l}", [256, 64], F32) for l in range(DEPTH)]
    tabs = nc.dram_tensor("tabs", [256, NT], F32)
    if debug:
        dbg_q = nc.dram_tensor("dbg_q", [128, NH * NT], BF16, kind="ExternalOutput").ap()
        dbg_ckv = nc.dram_tensor("dbg_ckv", [128, 2 * CTX], BF16, kind="ExternalOutput").ap()
        dbg_kr = nc.dram_tensor("dbg_kr", [128, CTX], BF16, kind="ExternalOutput").ap()
        dbg_ao = nc.dram_tensor("dbg_ao", [128, 4 * NT], BF16, kind="ExternalOutput").ap()
        dbg_cos = nc.dram_tensor("dbg_cos", [128, 2 * NT], F32, kind="ExternalOutput").ap()
        dbg_pat = nc.dram_tensor("dbg_pat", [128, 4 * D], BF16, kind="ExternalOutput").ap()
        dbg_mrg = nc.dram_tensor("dbg_mrg", [128, 8 * GC], BF16, kind="ExternalOutput").ap()

    with ExitStack() as es:
        arena = es.enter_context(nc.sbuf_tensor("arena", [128, ARENA], U8))
        banks = [es.enter_context(nc.psum_tensor(f"bank{i}", [128, 512], F32)) for i in range(8)]
        S = Sched(nc, es)
        block = es.enter_context(nc.Block())

        def sb(off, shape, dt):
            t = T(arena, "sb", off, shape, dt)
            assert off + t.nbytes <= ARENA, (off, shape)
            return t

        PB = [T(banks[i], ("ps", i), 0, [128, 512], F32) for i in range(8)]

        def dview(handle, name):
            return View(None, ("dram", name), (0, 1, 0, 1))

        xT = sb(0, [128, 8, NT], F32)
        o = 65536
        smallv = sb(o, [128, NV], F32); o += NV * 4
        modT = sb(o, [128, DEPTH, 48], F32); o += DEPTH * 48 * 4
        avec = sb(o, [128, DEPTH, 2, 8], F32); o += DEPTH * 16 * 4
        cact = sb(o, [128, 8], BF16); o += 16
        ones_b = sb(o, [128, 128], BF16); o += 256
        tri = sb(o, [128, 128], BF16); o += 256
        epsT = sb(o, [128, 1], F32); o += 4
        zeroT = sb(o, [128, 1], F32); o += 4
        ones_f = sb(o, [128, 128], F32); o += 512
        sel_f = sb(o, [128, 128], F32); o += 512
        flag16 = sb(o, [128, 16], BF16); o += 32
        halo_sb = sb(o, [128, 4, 16], F32); o += 256
        halo_in = sb(o, [128, 4, 16], F32); o += 256
        assert o <= 69632, o
        P0 = 69632
        R_AO = P0
        R_CTX = R_AO + 16384
        R_Q = R_CTX + 24576
        R_X = R_Q + 32768
        assert R_X == 143360

        cosT = sb(R_AO, [128, NT], F32)
        sinT = sb(R_AO + 8192, [128, NT], F32)
        attn_o = sb(R_AO, [128, 4, NT], BF16)
        ctx_ckv = sb(R_CTX, [128, 2, CTX], BF16)
        ctx_kr = sb(R_CTX + 16384, [128, CTX], BF16)
        qT_all = sb(R_Q, [128, NH, NT], BF16)

        def svc(col, n=1):
            return smallv.v(slice(col, col + n))

        bank_rr = [0]

        def next_bank(pool=(0, 1, 2, 3, 4, 5, 6, 7)):
            b = pool[bank_rr[0] % len(pool)]
            bank_rr[0] += 1
            return b

        def mm_group(out, pairs, extra_reads=()):
            n = len(pairs)
            allreads = [v for pr in pairs for v in pr] + list(extra_reads)
            for i, (l, r) in enumerate(pairs):
                last = i == n - 1
                fn = (lambda l=l, r=r, i=i, last=last: lambda e: e.matmul(out.ap, l.ap, r.ap, start=(i == 0), stop=last))()
                if last:
                    S.op("pe", fn, reads=allreads, writes=[out])
                else:
                    if i == 0:
                        S.op("pe", fn, reads=allreads, writes=[out], sig=False)
                    else:
                        S.op("pe", fn, sig=False)

        def act(out, in_, func, scale=1.0, bias=None, extra_reads=()):
            rd = [in_] + list(extra_reads)
            kw = {}
            if bias is not None:
                kw["bias"] = bias.ap
                rd.append(bias)
            if isinstance(scale, View):
                rd.append(scale)
                sc = scale.ap
            else:
                sc = scale
            S.op("act", lambda e: e.activation(out=out.ap, in_=in_.ap, func=func, scale=sc, **kw),
                 reads=rd, writes=[out])

        def tt(eng, out, a, b, op):
            S.op(eng, lambda e: e.tensor_tensor(out=out.ap, in0=a.ap, in1=b.ap, op=op), reads=[a, b], writes=[out])

        def ts(eng, out, a, s1, op0, s2=None, op1=None):
            rd = [a]
            s1v = s1.ap if isinstance(s1, View) else s1
            s2v = s2.ap if isinstance(s2, View) else s2
            if isinstance(s1, View):
                rd.append(s1)
            if isinstance(s2, View):
                rd.append(s2)
            if op1 is None:
                S.op(eng, lambda e: e.tensor_scalar(out=out.ap, in0=a.ap, scalar1=s1v, scalar2=None, op0=op0),
                     reads=rd, writes=[out])
            else:
                S.op(eng, lambda e: e.tensor_scalar(out=out.ap, in0=a.ap, scalar1=s1v, scalar2=s2v, op0=op0, op1=op1),
                     reads=rd, writes=[out])

        def stt(out, a, s, b, op0, op1):
            rd = [a, b]
            sv = s.ap if isinstance(s, View) else s
            if isinstance(s, View):
                rd.append(s)
            S.op("dve", lambda e: e.scalar_tensor_tensor(out=out.ap, in0=a.ap, scalar=sv, in1=b.ap, op0=op0, op1=op1),
                 reads=rd, writes=[out])

        def copy(eng, out, in_):
            if eng == "act":
                S.op("act", lambda e: e.copy(out=out.ap, in_=in_.ap), reads=[in_], writes=[out])
            else:
                S.op(eng, lambda e: e.tensor_copy(out=out.ap, in_=in_.ap), reads=[in_], writes=[out])

        def memset(eng, out, val):
            S.op(eng, lambda e: e.memset(out.ap, val), writes=[out])

        def wload(dst, src_ap, key):
            S.dma("pool", [lambda e: e.dma_start(out=dst.ap, in_=src_ap)], key, writes=[dst])

        def rstd_from(stat_ps, out, n, tmp):
            act(tmp, stat_ps, AF.Ln, scale=1.0 / n, bias=epsT.v())
            act(out, tmp, AF.Exp, scale=-0.5)

        S.dma("sp", [lambda e: e.dma_start(out=smallv.v().ap, in_=sv_d)], "smallv", writes=[smallv.v()])
        S.dma("sp", [lambda e: e.dma_start(out=tri.v().ap, in_=tri_d)], "tri", writes=[tri.v()])
        posi = sb(R_CTX, [128, NT], I32)
        S.dma("sp", [lambda e: e.dma_start(out=posi.v().ap, in_=pos_d.partition_broadcast(128)[:, 0, :])],
              "pos", writes=[posi.v()])
        xTv = xT_d.rearrange("(k p) t -> p k t", p=128)
        for c in range(NCH):
            dst = xT.v(slice(None), slice(c * C, (c + 1) * C))
            S.dma("sp", [(lambda c=c, dst=dst: lambda e: e.dma_start(out=dst.ap, in_=xTv[:, :, c * C:(c + 1) * C]))()],
                  f"x{c}", writes=[dst])

        Wkr0 = sb(R_X + 4096, [128, 8, 96], BF16)
        Wkrr0 = sb(R_X + 4096 + 1536, [128, 8, 96], BF16)
        memset("dve", Wkr0.v(), 0.0)
        memset("dve", Wkrr0.v(), 0.0)
        memset("dve", ones_b.v(), 1.0)
        memset("dve", epsT.v(), EPS)
        memset("dve", zeroT.v(), 0.0)
        memset("dve", ones_f.v(), 1.0)
        memset("dve", sel_f.v(), 0.0)
        memset("dve", sel_f.v(slice(64, 128)), 1.0)
        memset("dve", flag16.v(), 1.0)
        ts("dve", flag16.v(), flag16.v(), svc(SV_FLAG), ALU.mult)

        act(cact.v(), svc(SV_C, 8), AF.Silu)

        PI = math.pi
        ang = sb(R_Q, [128, NT], F32)
        t1 = sb(R_Q + 8192, [128, NT], F32)
        t2 = sb(R_Q + 16384, [128, NT], F32)
        ki = sb(R_Q + 24576, [128, NT], I32)
        copy("dve", ang.v(), posi.v())
        ts("dve", ang.v(), ang.v(), svc(SV_INVF), ALU.mult)
        ts("dve", t1.v(), ang.v(), 1.0 / (2 * PI), ALU.mult)
        copy("dve", ki.v(), t1.v())
        copy("dve", t1.v(), ki.v())
        C1 = 6.28125
        C2 = 2 * PI - C1
        stt(t2.v(), t1.v(), -C1, ang.v(), ALU.mult, ALU.add)
        stt(t2.v(), t1.v(), -C2, t2.v(), ALU.mult, ALU.add)

        def wrap(r, tmp):
            ts("dve", tmp, r, PI, ALU.is_gt)
            stt(r, tmp, -2 * PI, r, ALU.mult, ALU.add)
            ts("dve", tmp, r, -PI, ALU.is_lt)
            stt(r, tmp, 2 * PI, r, ALU.mult, ALU.add)

        wrap(t2.v(), t1.v())
        act(sinT.v(), t2.v(), AF.Sin)
        ts("dve", t2.v(), t2.v(), PI / 2, ALU.add)
        wrap(t2.v(), t1.v())
        act(cosT.v(), t2.v(), AF.Sin)
        tabs_v = dview(tabs, "tabs")
        if debug:
            S.dma("sp", [lambda e: e.dma_start(out=dbg_cos[:, 0:NT], in_=cosT.v().ap),
                         lambda e: e.dma_start(out=dbg_cos[:, NT:2 * NT], in_=sinT.v().ap)],
                  "dbg0", reads=[cosT.v(), sinT.v()], writes=[View(None, ("dram", "dbg0"), (0, 1, 0, 1))])
        S.dma("sp", [lambda e: e.dma_start(out=tabs.ap()[0:128, :], in_=cosT.v().ap),
                     lambda e: e.dma_start(out=tabs.ap()[128:256, :], in_=sinT.v().ap)],
              "tabs_st", reads=[cosT.v(), sinT.v()], writes=[tabs_v])

        wada_b = [sb(R_CTX + 8192, [128, 8, D], BF16), sb(R_X + 30720, [128, 8, D], BF16)]
        for j in range(2):
            wb = wada_b[j]
            src = wada_d[0].rearrange("(k p) n -> p k n", p=128)[:, :, j * D:(j + 1) * D]
            wload(wb.v(), src, f"wada{j}")
            bk = PB[next_bank()]
            for m in range(8):
                mm_group(bk.v(slice(m, m + 1)),
                         [(wb.v(k, slice(m * 128, (m + 1) * 128)), cact.v(slice(k, k + 1))) for k in range(8)])
            tt("dve", modT.v(0, slice(j * 8, (j + 1) * 8)), bk.v(slice(0, 8)), svc(SV_BADA + j * 8, 8), ALU.add)
        stt(avec.v(0, 0), modT.v(0, slice(8, 16)), 1.0, svc(SV_LN1, 8), ALU.add, ALU.mult)
        wpc = sb(R_X + 51200, [128, 8, 512], BF16)
        mod_pieces = [(0, q) for q in range(4, 12)] + [(1, q) for q in range(12)]

        def mod_piece_step(lq, banks):
            ll, q = lq

            def f():
                src = wada_d[ll].rearrange("(k p) n -> p k n", p=128)[:, :, q * 512:(q + 1) * 512]
                wload(wpc.v(), src, "wpc")
                bk = PB[next_bank(banks)]
                for m in range(4):
                    mm_group(bk.v(slice(m, m + 1)),
                             [(wpc.v(k, slice(m * 128, (m + 1) * 128)), cact.v(slice(k, k + 1))) for k in range(8)])
                tt("dve", modT.v(ll, slice(q * 4, (q + 1) * 4)), bk.v(slice(0, 4)),
                   svc(SV_BADA + ll * 48 + q * 4, 4), ALU.add)
                if q == 3:
                    stt(avec.v(ll, 0), modT.v(ll, slice(8, 16)), 1.0, svc(SV_LN1 + ll * 8, 8), ALU.add, ALU.mult)
                if q == 9:
                    stt(avec.v(ll, 1), modT.v(ll, slice(32, 40)), 1.0, svc(SV_LN2 + ll * 8, 8), ALU.add, ALU.mult)
            return f

        rstd1_all = sb(ARENA - 8192, [128, NT], F32)
        sq0 = sb(R_X + 49152, [128, 8, C], BF16)
        ln0 = sb(R_X + 57344, [128, C], F32)
        for c in range(NCH):
            cs = slice(c * C, (c + 1) * C)
            for k in range(8):
                act(sq0.v(k), xT.v(k, cs), AF.Square)
            bk = PB[next_bank()]
            mm_group(bk.v(), [(ones_b.v(), sq0.v(k)) for k in range(8)])
            act(ln0.v(), bk.v(), AF.Ln, scale=1.0 / D, bias=epsT.v())
            act(rstd1_all.v(cs), ln0.v(), AF.Exp, scale=-0.5)
        def stats_all(rstd_all, sq, lntmp):
            for c in range(NCH):
                cs = slice(c * C, (c + 1) * C)
                for k in range(8):
                    act(sq.v(k), xT.v(k, cs), AF.Square)
                bk = PB[next_bank()]
                mm_group(bk.v(), [(ones_b.v(), sq.v(k)) for k in range(8)])
                rstd_from(bk.v(), rstd_all.v(cs), D, lntmp)

        def h_from(l, which, xv_k, rstd, hv_k, tmpA, tmpB):
            boff = 0 if which == 0 else 24
            for k in range(8):
                tmp = tmpA if k % 2 == 0 else tmpB
                tt("dve", tmp, xv_k(k), rstd, ALU.mult)
                act(hv_k(k), tmp, AF.Identity, scale=avec.v(l, which, slice(k, k + 1)),
                    bias=modT.v(l, slice(boff + k, boff + k + 1)))

        for l in range(DEPTH):
            o = R_X
            Wkv = sb(o, [128, 8, 256], BF16); o += 4096
            Wkr = sb(o, [128, 8, 96], BF16); o += 1536
            Wkrr = sb(o, [128, 8, 96], BF16); o += 1536
            Wq = sb(o, [128, 8, 384], BF16); o += 6144
            wuq = sb(o, [128, 3, 768], BF16); o += 4608
            wuqr = sb(o, [128, 3, 768], BF16); o += 4608
            Wu = sb(o, [128, 8, 512], BF16); o += 8192
            hT = sb(o, [128, 8, C], BF16); o += 8192
            sq = sb(o, [128, 8, C], BF16); o += 8192
            tmpA = sb(o, [128, C], F32); o += 2048
            tmpB = sb(o, [128, C], F32); o += 2048
            rstdq = sb(o, [128, C], F32); o += 2048
            rstdkv = rstdq
            cqn = sb(o, [128, 3, C], BF16); o += 3072
            sql = sb(o, [128, 3, C], BF16); o += 3072
            assert o <= ARENA - 8192, o
            hT2 = [hT, sq]
            rstd1_all = sb(ARENA - 8192, [128, NT], F32)

            winv = win_d[l].rearrange("(k p) n -> p k n", p=128)
            wload(Wkv.v(), winv[:, :, 896:1152], "Wkv")
            if l == 0:
                wload(Wkr.v(slice(None), slice(64, 96)), winv[:, :, 1152:1184], "Wkr")
            wload(Wq.v(), winv[:, :, 512:896], "Wq")
            wload(wuq.v(), wuq_d[l].rearrange("(j p) n -> p j n", p=128), "wuq")
            wload(Wu.v(), winv[:, :, 0:512], "Wu")
            if l > 0:
                memset("dve", Wkr.v(), 0.0)
                memset("dve", Wkrr.v(), 0.0)
                wload(Wkr.v(slice(None), slice(64, 96)), winv[:, :, 1152:1184], "Wkr")
            ts("dve", Wkrr.v(slice(None), slice(64, 80)), Wkr.v(slice(None), slice(80, 96)), -1.0, ALU.mult)
            copy("dve", Wkrr.v(slice(None), slice(80, 96)), Wkr.v(slice(None), slice(64, 80)))
            wuq4 = wuq.full.rearrange("p j (h d) -> p j h d", h=NH)
            wuqr4 = wuqr.full.rearrange("p j (h d) -> p j h d", h=NH)
            memset("dve", wuqr.v(), 0.0)
            S.op("dve", lambda e: e.tensor_scalar(out=wuqr4[:, :, :, 64:80], in0=wuq4[:, :, :, 80:96], scalar1=-1.0,
                                                    scalar2=None, op0=ALU.mult),
                 reads=[wuq.v()], writes=[wuqr.v()])
            S.op("dve", lambda e: e.tensor_copy(out=wuqr4[:, :, :, 80:96], in_=wuq4[:, :, :, 64:80]),
                 reads=[wuq.v()], writes=[wuqr.v()])
            if l > 0:
                S.dma("sp", [lambda e: e.dma_start(out=cosT.v().ap, in_=tabs.ap()[0:128, :]),
                             lambda e: e.dma_start(out=sinT.v().ap, in_=tabs.ap()[128:256, :])],
                      "tabs_ld", reads=[tabs_v], writes=[cosT.v(), sinT.v()])

            for c in range(NCH):
                cs = slice(c * C, (c + 1) * C)
                ls = slice(NT + c * C, NT + (c + 1) * C)
                hT = hT2[c % 2]
                if c == 0:
                    h_from(l, 0, lambda k: xT.v(k, cs), rstd1_all.v(cs), lambda k: hT.v(k), tmpA.v(), tmpB.v())
                bkv = [PB[next_bank()] for _ in range(2)]
                for j in range(2):
                    mm_group(bkv[j].v(), [(Wkv.v(k, slice(j * 128, (j + 1) * 128)), hT.v(k)) for k in range(8)])
                    act(cqn.v(j), bkv[j].v(), AF.Square)
                bq = [PB[next_bank()] for _ in range(3)]
                for j in range(3):
                    mm_group(bq[j].v(), [(Wq.v(k, slice(j * 128, (j + 1) * 128)), hT.v(k)) for k in range(8)])
                    act(sql.v(j), bq[j].v(), AF.Square)
                bs = PB[next_bank()]
                mm_group(bs.v(), [(ones_b.v(), cqn.v(j)) for j in range(2)])
                rstd_from(bs.v(), rstdkv.v(), 256, tmpA.v())
                for j in range(2):
                    stt(ctx_ckv.v(j, ls), bkv[j].v(), svc(SV_KVG + l * 2 + j), rstdkv.v(), ALU.mult, ALU.mult)
                bkr = PB[next_bank()]
                bkrr = PB[next_bank()]
                mm_group(bkr.v(p=(0, 96)), [(Wkr.v(k), hT.v(k)) for k in range(8)])
                mm_group(bkrr.v(p=(0, 96)), [(Wkrr.v(k), hT.v(k)) for k in range(8)])
                R = (64, 96)
                tt("dve", tmpA.v(p=R), bkr.v(p=R), cosT.v(cs, p=R), ALU.mult)
                tt("dve", tmpB.v(p=R), bkrr.v(p=R), sinT.v(cs, p=R), ALU.mult)
                tt("dve", ctx_kr.v(ls, p=R), tmpA.v(p=R), tmpB.v(p=R), ALU.add)
                bs = PB[next_bank()]
                mm_group(bs.v(), [(ones_b.v(), sql.v(j)) for j in range(3)])
                rstd_from(bs.v(), rstdq.v(), 384, tmpA.v())
                ts("dve", rstdq.v(), rstdq.v(), SCALE, ALU.mult)
                for j in range(3):
                    stt(cqn.v(j), bq[j].v(), svc(SV_QNG + l * 3 + j), rstdq.v(), ALU.mult, ALU.mult)
                Q = (0, 96)
                if c == NCH - 1:
                    bh = PB[next_bank()]
                    for g in range(4):
                        mm_group(bh.v(slice(g * 16, (g + 1) * 16)),
                                 [(Wu.v(k, slice(g * 128, (g + 1) * 128)), hT.v(k, slice(C - 16, C))) for k in range(8)])
                    S.op("dve", lambda e, bh=bh: e.tensor_copy(
                        out=halo_sb.full, in_=bh.full[:, 0:64].rearrange("p (g t) -> p g t", g=4)),
                        reads=[bh.v(slice(0, 64))], writes=[halo_sb.v()])

                if c + 1 < NCH:
                    cs2 = slice((c + 1) * C, (c + 2) * C)
                    hTn = hT2[(c + 1) % 2]
                    h_from(l, 0, lambda k: xT.v(k, cs2), rstd1_all.v(cs2), lambda k: hTn.v(k), tmpA.v(), tmpB.v())
                for h in range(NH):
                    ba = PB[next_bank()]
                    bb = PB[next_bank()]
                    hs = slice(h * 96, (h + 1) * 96)
                    mm_group(ba.v(p=Q), [(wuq.v(j, hs), cqn.v(j)) for j in range(3)])
                    mm_group(bb.v(p=Q), [(wuqr.v(j, hs), cqn.v(j)) for j in range(3)])
                    tt("dve", tmpA.v(p=Q), ba.v(p=Q), cosT.v(cs, p=Q), ALU.mult)
                    tt("dve", tmpB.v(p=Q), bb.v(p=Q), sinT.v(cs, p=Q), ALU.mult)
                    tt("dve", qT_all.v(h, cs, p=Q), tmpA.v(p=Q), tmpB.v(p=Q), ALU.add)
            wuk = sb(R_X, [128, 2, 512], BF16)
            wuv = sb(R_X + 2048, [128, 2, 512], BF16)
            wload(wuk.v(), wuk_d[l].rearrange("(j p) n -> p j n", p=128), "wuk")
            wload(wuv.v(), wuv_d[l].rearrange("(j p) n -> p j n", p=128), "wuv")
            exs_v = dview(exs[l], f"exs{l}")
            exd_v = dview(exd[l], f"exd{l}")
            hls_v = dview(hls[l], f"hls{l}")
            hld_v = dview(hld[l], f"hld{l}")
            own = slice(NT, CTX)
            S.dma("sp", [lambda e, l=l: e.dma_start(out=exs[l].ap()[0:128, :], in_=ctx_ckv.v(0, own).ap),
                         lambda e, l=l: e.dma_start(out=exs[l].ap()[128:256, :], in_=ctx_ckv.v(1, own).ap),
                         lambda e, l=l: e.dma_start(out=exs[l].ap()[256:288, :], in_=ctx_kr.v(own, p=(64, 96)).ap)],
                  f"exst{l}", reads=[ctx_ckv.v(0, own), ctx_ckv.v(1, own), ctx_kr.v(own, p=(64, 96))],
                  writes=[exs_v])
            S.dma("sp", [lambda e, l=l: e.dma_start(out=hls[l].ap(), in_=halo_sb.full.rearrange("p g t -> p (g t)"))],
                  f"hlst{l}", reads=[halo_sb.v()], writes=[hls_v])
            RG = [[0, 1], [2, 3], [4, 5], [6, 7]]
            cc_tok = S.custom("pool", lambda e, l=l: e.collective_compute(
                "AllGather", ALU.bypass, replica_groups=RG, ins=[exs[l].ap().opt()], outs=[exd[l].ap().opt()]),
                reads=[exs_v], writes=[exd_v], key=f"cc{l}")
            S.wait_tok("pool", cc_tok)
            S.custom("pool", lambda e, l=l: e.collective_compute(
                "AllGather", ALU.bypass, replica_groups=RG, ins=[hls[l].ap().opt()], outs=[hld[l].ap().opt()]),
                reads=[hls_v], writes=[hld_v], key=f"cch{l}")
            rem = slice(0, NT)
            for j in range(2):
                S.dma("sp", [(lambda l=l, j=j: lambda e: e.dma_start(out=ctx_ckv.v(j, rem).ap,
                                                                      in_=exd[l].ap()[j * 128:(j + 1) * 128, :]))()],
                      f"exld{l}_{j}", reads=[exd_v], writes=[ctx_ckv.v(j, rem)])
            S.dma("sp", [lambda e, l=l: e.dma_start(out=ctx_kr.v(rem, p=(64, 96)).ap, in_=exd[l].ap()[256:288, :])],
                  f"exldk{l}", reads=[exd_v], writes=[ctx_kr.v(rem, p=(64, 96))])
            S.dma("sp", [lambda e, l=l: e.dma_start(out=halo_in.full.rearrange("p g t -> p (g t)"), in_=hld[l].ap()[0:128, :])],
                  f"exldh{l}", reads=[hld_v], writes=[halo_in.v()])

            if debug and l == 0:
                S.dma("sp", [lambda e: e.dma_start(out=dbg_q, in_=qT_all.full.rearrange("p h t -> p (h t)")),
                             lambda e: e.dma_start(out=dbg_ckv, in_=ctx_ckv.full.rearrange("p j t -> p (j t)")),
                             lambda e: e.dma_start(out=dbg_kr, in_=ctx_kr.full)],
                      "dbg1", reads=[qT_all.v(), ctx_ckv.v(), ctx_kr.v()], writes=[View(None, ("dram", "dbg1"), (0, 1, 0, 1))])
            o = R_X
            o += 4096
            KT = [sb(o + i * 8192, [128, CTX], BF16) for i in range(2)]; o += 16384
            VV = [sb(o + i * 8192, [128, 32, 128], BF16) for i in range(2)]; o += 16384
            NPT = 6
            LA = 4
            PT = [sb(o + i * 1024, [128, C], BF16) for i in range(NPT)]; o += NPT * 1024
            Osb = [sb(o + i * 2048, [128, C], F32) for i in range(2)]; o += 4096
            rdn = [sb(o + i * 2048, [128, C], F32) for i in range(2)]; o += 4096
            assert o <= ARENA
            o = R_CTX
            Wu2 = sb(o, [128, 8, 512], BF16); o += 8192
            Wga = sb(o, [128, 8, D], BF16); o += 16384
            Wgb = sb(o, [128, 8, D], BF16); o += 16384
            wpl = sb(o, [128, 4, 128], BF16); o += 1024
            ppl = sb(o, [128, 4, D], BF16); o += 8192
            pat = sb(o, [128, 4, D], BF16); o += 8192
            wo = sb(o, [128, 8, D], BF16); o += 16384
            G_o = o
            S_BANKS = (0, 1, 2, 3, 4, 7)

            def build_steps(h):
                lower = h < 4
                kt_t = KT[h % 2]
                v_t = VV[h % 2]
                voff = 0 if lower else 64
                vcol = 64 if lower else 0

                def kstep(cc):
                    def f():
                        bk = PB[next_bank(S_BANKS)]
                        ks = slice(cc * C, (cc + 1) * C)
                        mm_group(bk.v(p=(0, 64)),
                                 [(wuk.v(j, slice(h * 64, (h + 1) * 64)), ctx_ckv.v(j, ks)) for j in range(2)])
                        copy("dve", kt_t.v(ks, p=(0, 64)), bk.v(p=(0, 64)))
                    return f

                def krope(part):
                    def f():
                        copy("dve", kt_t.v(part, p=(64, 96)), ctx_kr.v(part, p=(64, 96)))
                    return f

                def vinit():
                    memset("dve", v_t.v(), 0.0)
                    S.op("dve", lambda e: e.tensor_copy(out=v_t.full[:, 0:16, vcol], in_=flag16.full),
                         reads=[flag16.v()], writes=[v_t.v(slice(0, 16))])
                    S.op("dve", lambda e: e.memset(v_t.full[:, 16:32, vcol], 1.0), writes=[v_t.v(slice(16, 32))])

                def vstep(kb):
                    def f():
                        bk = PB[next_bank(S_BANKS)]
                        for i in range(8):
                            kt = kb * 8 + i
                            mm_group(bk.v(slice(i * 64, (i + 1) * 64)),
                                     [(ctx_ckv.v(j, slice(kt * 128, (kt + 1) * 128)), wuv.v(j, slice(h * 64, (h + 1) * 64)))
                                      for j in range(2)])
                        src = bk.full.rearrange("p (i d) -> p i d", i=8)
                        dstap = v_t.full[:, kb * 8:(kb + 1) * 8, voff:voff + 64]
                        if kb < 2:
                            S.op("dve", lambda e: e.tensor_scalar(out=dstap, in0=src, scalar1=svc(SV_FLAG).ap,
                                                                    scalar2=None, op0=ALU.mult),
                                 reads=[bk.v(), svc(SV_FLAG)], writes=[v_t.v(slice(kb * 8, (kb + 1) * 8))])
                        else:
                            S.op("dve", lambda e: e.tensor_copy(out=dstap, in_=src),
                                 reads=[bk.v()], writes=[v_t.v(slice(kb * 8, (kb + 1) * 8))])
                    return f

                return ([vinit] + [kstep(cc) for cc in (4, 5, 6, 7)] + [krope(slice(NT, CTX)), vstep(2), vstep(3)]
                        + [kstep(cc) for cc in (0, 1, 2, 3)] + [krope(slice(0, NT)), vstep(0), vstep(1)])

            for st in build_steps(0):
                st()
            pti = 0
            oi = 0
            items = []
            step_at = {}
            for h in range(NH):
                base = len(items)
                for c in range(NCH):
                    tiles = [(kt, 0) for kt in range(16 + 4 * c)] + [(16 + 4 * c + i, i) for i in range(4)]
                    for ti, (kt, i0) in enumerate(tiles):
                        items.append((h, c, kt, i0, ti, len(tiles)))
                n_h = len(items) - base
                nxt = build_steps(h + 1) if h + 1 < NH else []
                if nxt:
                    gap = n_h // (len(nxt) + 1)
                    for si, st in enumerate(nxt):
                        step_at.setdefault(base + (si + 1) * gap, []).append(st)
                if l == 0:
                    for at in (3, 38, 73):
                        if mod_pieces:
                            step_at.setdefault(base + at, []).append(mod_piece_step(mod_pieces.pop(0), S_BANKS))
                if h == 4:
                    step_at.setdefault(base + 12, []).append(
                        lambda: wload(Wgb.v(), winv[:, :, 2208:3232], "Wgb"))
                if h == 5:
                    step_at.setdefault(base + 12, []).append(
                        lambda: wload(wpl.v(), wpool_d[l].rearrange("g c d -> c g d"), "wpl"))
                if h == 7:
                    step_at.setdefault(base + 12, []).append(
                        lambda: (wload(Wu2.v(), winv[:, :, 0:512], "Wu2"),
                                 wload(ppl.v(), ppool_d[l].rearrange("(g p) n -> p g n", p=128), "ppl"),
                                 wload(Wga.v(), winv[:, :, 1184:2208], "Wga")))
            n_items = len(items)
            deferred = []
            pend = {}
            obs = {}
            for idx in range(n_items + LA):
                if idx < n_items:
                    h, c, kt, i0, ti, nt_ = items[idx]
                    q0 = i0 * 128
                    n = C - q0
                    sbk = PB[next_bank(S_BANKS)]
                    sv_ = sbk.v(slice(0, n))
                    qv = qT_all.v(h, slice(c * C + q0, (c + 1) * C), p=(0, 96))
                    kv = KT[h % 2].v(slice(kt * 128, (kt + 1) * 128), p=(0, 96))
                    mm_group(sv_, [(kv, qv)])
                    pt = PT[pti % NPT]
                    pti += 1
                    pv = pt.v(slice(0, n))
                    act(pv, sv_, AF.Exp)
                    if kt >= 16 + 4 * c:
                        tt("dve", pt.v(slice(0, 128)), pt.v(slice(0, 128)), tri.v(), ALU.mult)
                    pend[idx] = pv
                    if ti == 0:
                        obs[(h, c)] = (PB[5 + oi % 2], Osb[oi % 2], rdn[oi % 2])
                        oi += 1
                j = idx - LA
                if j >= 0:
                    h, c, kt, i0, ti, nt_ = items[j]
                    lower = h < 4
                    q0 = i0 * 128
                    ob, osb, rden = obs.pop((h, c)) if ti == nt_ - 1 else obs[(h, c)]
                    ov = ob.v(slice(q0, C))
                    vv = VV[h % 2].v(kt)
                    pv = pend.pop(j)
                    S.op("pe", (lambda ov=ov, vv=vv, pv=pv, ti=ti, nt_=nt_: lambda e: e.matmul(
                        ov.ap, vv.ap, pv.ap, start=(ti == 0), stop=(ti == nt_ - 1)))(),
                        reads=[vv, pv], writes=[ov])
                    if ti == nt_ - 1:
                        copy("dve", osb.v(), ob.v())
                        rp = (64, 65) if lower else (0, 1)
                        S.op("dve", (lambda osb=osb, rden=rden, rp=rp: lambda e: e.reciprocal(
                            out=rden.v(p=rp).ap, in_=osb.v(p=rp).ap))(),
                            reads=[osb.v(p=rp)], writes=[rden.v(p=rp)])

                        def fin(h=h, c=c, osb=osb, rden=rden, rp=rp, lower=lower):
                            bcb = PB[next_bank(S_BANKS)]
                            mm_group(bcb.v(), [((ones_f if lower else sel_f).v(p=rp), rden.v(p=rp))])
                            rows = (0, 64) if lower else (64, 128)
                            tt("dve", attn_o.v(h % 4, slice(c * C, (c + 1) * C), p=rows), osb.v(p=rows),
                               bcb.v(p=rows), ALU.mult)
                        deferred.append((idx + 16, fin))
                for st in step_at.get(idx, []):
                    st()
                while deferred and deferred[0][0] <= idx:
                    deferred.pop(0)[1]()
            for _, fn in deferred:
                fn()

            if debug and l == 0:
                S.dma("sp", [lambda e: e.dma_start(out=dbg_ao, in_=attn_o.full.rearrange("p j t -> p (j t)"))],
                      "dbg2", reads=[attn_o.v()], writes=[View(None, ("dram", "dbg2"), (0, 1, 0, 1))])
            o = G_o
            hTg2 = [sb(o + i * 4096, [128, 8, GC], BF16) for i in range(2)]; o += 8192
            gA = sb(o, [128, GC], F32); o += 1024
            gB = sb(o, [128, GC], F32); o += 1024
            hA = sb(o, [128, GC], F32); o += 1024
            hB = sb(o, [128, GC], F32); o += 1024
            uT = sb(o, [128, 4, 16 + GC], F32); o += 4 * (16 + GC) * 4
            wA = sb(o, [128, 4, 16 + GC], F32); o += 4 * (16 + GC) * 4
            wB = sb(o, [128, 4, 16 + GC], F32); o += 4 * (16 + GC) * 4
            pTt = sb(o, [128, 4, GC], BF16); o += 4 * GC * 2
            yTt = sb(o, [128, 4, GC], BF16); o += 4 * GC * 2
            mrg = sb(o, [128, 8, GC], BF16); o += 8 * GC * 2
            sgA = sb(o, [128, GC], F32); o += 1024
            sgB = sb(o, [128, GC], F32); o += 1024
            pav = pattn_d[l].rearrange("(hh j v) n -> hh v j n", hh=2, j=4)
            S.dma("pool", [lambda e, pav=pav: e.dma_start(out=pat.v(p=(0, 64)).ap, in_=pav[0]),
                           lambda e, pav=pav: e.dma_start(out=pat.v(p=(64, 128)).ap, in_=pav[1])], "pat", writes=[pat.v()])
            wload(wo.v(), wout_d[l].rearrange("(k p) n -> p k n", p=128), "wo")
            yT2 = [yTt, sb(o, [128, 4, GC], BF16)]; o += 4 * GC * 2
            sqe = sb(o, [128, 8, GC], BF16); o += 8 * GC * 2
            assert o <= ARENA - 8192, o
            ts("dve", uT.v(slice(None), slice(0, 16)), halo_in.v(), svc(SV_FLAG), ALU.mult)
            NG = NT // GC
            L_ = 16 + GC

            def gsl(gc):
                return slice(gc * GC, (gc + 1) * GC)

            def P0(gc):
                hTg = hTg2[gc % 2]
                gs = gsl(gc)
                h_from(l, 0, lambda k: xT.v(k, gs), rstd1_all.v(gs), lambda k: hTg.v(k), hA.v(), hB.v())

            def P1(gc):
                hTg = hTg2[gc % 2]
                for half in range(2):
                    bk = PB[next_bank()]
                    for gg in range(2):
                        g = half * 2 + gg
                        mm_group(bk.v(slice(gg * GC, (gg + 1) * GC)),
                                 [(Wu2.v(k, slice(g * 128, (g + 1) * 128)), hTg.v(k)) for k in range(8)])
                    S.op("act", lambda e, bk=bk, half=half: e.copy(
                        out=uT.full[:, half * 2:half * 2 + 2, 16:16 + GC],
                        in_=bk.full.rearrange("p (g t) -> p g t", g=2)),
                        reads=[bk.v()], writes=[uT.v(slice(half * 2, half * 2 + 2))])

            def P2a(gc):
                E = "pool"
                tt(E, wA.v(slice(0, 4), slice(1, L_)), uT.v(slice(0, 4), slice(1, L_)), uT.v(slice(0, 4), slice(0, L_ - 1)), ALU.add)
                tt(E, wB.v(slice(1, 4), slice(3, L_)), wA.v(slice(1, 4), slice(3, L_)), wA.v(slice(1, 4), slice(1, L_ - 2)), ALU.add)
                tt(E, wA.v(slice(2, 4), slice(7, L_)), wB.v(slice(2, 4), slice(7, L_)), wB.v(slice(2, 4), slice(3, L_ - 4)), ALU.add)
                tt(E, wB.v(slice(3, 4), slice(15, L_)), wA.v(slice(3, 4), slice(15, L_)), wA.v(slice(3, 4), slice(7, L_ - 8)), ALU.add)

            def P2a2(gc):
                for g in range(4):
                    w = 2 << g
                    cur = wA if g % 2 == 0 else wB
                    stt(pTt.v(g), cur.v(g, slice(16, L_)), 1.0 / w, uT.v(g, slice(16, L_)), ALU.mult, ALU.subtract)
                    if gc == 0:
                        tt("dve", gA.v(slice(0, 16)), cur.v(g, slice(16, 32)), svc(SV_INVCNT + g * 16, 16), ALU.mult)
                        tt("dve", pTt.v(g, slice(0, 16)), gA.v(slice(0, 16)), uT.v(g, slice(16, 32)), ALU.subtract)
                S.op("dve", lambda e: e.tensor_copy(out=uT.full[:, :, 0:16], in_=uT.full[:, :, GC:GC + 16]),
                     reads=[uT.v()], writes=[uT.v()])

            def P2b(gc):
                yT = yT2[gc % 2]
                bk = PB[next_bank()]
                bk2 = PB[next_bank()]
                for g in range(4):
                    dst = (bk if g < 2 else bk2).v(slice((g % 2) * GC, (g % 2 + 1) * GC))
                    mm_group(dst, [(wpl.v(g), pTt.v(g))])
                for g in range(4):
                    dst = (bk if g < 2 else bk2).v(slice((g % 2) * GC, (g % 2 + 1) * GC))
                    ts("dve", yT.v(g), dst, svc(SV_PSC + l * 4 + g), ALU.mult)

            def M1(gc, mrange):
                hTg = hTg2[gc % 2]
                yT = yT2[gc % 2]
                gs = gsl(gc)
                for m in mrange:
                    ms = slice(m * 128, (m + 1) * 128)
                    b1 = PB[next_bank()]
                    b2 = PB[next_bank()]
                    ya = b1.v(slice(0, GC))
                    ga = b1.v(slice(GC, 2 * GC))
                    yb = b2.v(slice(0, GC))
                    gb = b2.v(slice(GC, 2 * GC))
                    mm_group(ya, [(ppl.v(g, ms), yT.v(g)) for g in range(4)])
                    mm_group(ga, [(Wga.v(k, ms), hTg.v(k)) for k in range(8)])
                    mm_group(yb, [(pat.v(j, ms), attn_o.v(j, gs)) for j in range(4)])
                    mm_group(gb, [(Wgb.v(k, ms), hTg.v(k)) for k in range(8)])
                    act(sgA.v(), ga, AF.Sigmoid)
                    act(sgB.v(), gb, AF.Sigmoid)
                    tt("dve", gA.v(), sgA.v(), ya, ALU.mult)
                    tt("dve", gB.v(), sgB.v(), yb, ALU.mult)
                    tt("dve", mrg.v(m), gA.v(), gB.v(), ALU.add)

            def M2(gc):
                gs = gsl(gc)
                for m in range(8):
                    ms = slice(m * 128, (m + 1) * 128)
                    bk = PB[next_bank()]
                    ov = bk.v(slice(0, GC))
                    mm_group(ov, [(wo.v(k, ms), mrg.v(k)) for k in range(8)])
                    stt(xT.v(m, gs), ov, modT.v(l, slice(16 + m, 17 + m)), xT.v(m, gs), ALU.mult, ALU.add)

            def FsA(gc):
                gs = gsl(gc)
                for k in range(8):
                    act(sqe.v(k), xT.v(k, gs), AF.Square)

            def FsB(gc):
                gs = gsl(gc)
                bk = PB[next_bank()]
                st = bk.v(slice(0, GC))
                mm_group(st, [(ones_b.v(), sqe.v(k)) for k in range(8)])
                rstd_from(st, rstd1_all.v(gs), D, sgA.v())

            P0(0)
            P1(0)
            P2a(0)
            P2a2(0)
            P2b(0)
            for gc in range(NG):
                M1(gc, range(0, 2))
                if gc + 1 < NG:
                    P0(gc + 1)
                if gc >= 1:
                    FsA(gc - 1)
                M1(gc, range(2, 4))
                if gc + 1 < NG:
                    P1(gc + 1)
                    P2a(gc + 1)
                M1(gc, range(4, 6))
                if gc >= 1:
                    FsB(gc - 1)
                M1(gc, range(6, 8))
                if gc + 1 < NG:
                    P2a2(gc + 1)
                M2(gc)
                if gc + 1 < NG:
                    P2b(gc + 1)
            FsA(NG - 1)
            FsB(NG - 1)

            h2 = sb(R_AO, [128, 8, NT], BF16)
            o = R_AO + 32768
            W1 = [sb(o + i * 16384, [128, 8, 512], BF16) for i in range(2)]
            W2 = [sb(o + 8192 + i * 16384, [128, 4, D], BF16) for i in range(2)]
            o += 32768
            hid = [sb(o + i * 4096, [128, 4, C], BF16) for i in range(2)]; o += 8192
            assert o <= R_X
            o = R_X + 32768
            sqf = sb(o, [128, 8, C], BF16); o += 8192
            fA = sb(o, [128, C], F32); o += 2048
            fB = sb(o, [128, C], F32); o += 2048
            frs = sb(o, [128, NT], F32); o += 8192
            rl = [sb(o + i * 2048, [128, C], F32) for i in range(2)]; o += 4096
            assert o <= ARENA
            w1v = wff1_d[l].rearrange("(k p) n -> p k n", p=128)
            w2v = wff2_d[l].rearrange("(j p) n -> p j n", p=128)

            def ffload(g):
                wload(W1[g % 2].v(), w1v[:, :, g * 512:(g + 1) * 512], f"W1_{g % 2}")
                wload(W2[g % 2].v(), w2v[:, g * 4:(g + 1) * 4, :], f"W2_{g % 2}")

            for c in range(NCH):
                cs = slice(c * C, (c + 1) * C)
                h_from(l, 1, lambda k: xT.v(k, cs), rstd1_all.v(cs), lambda k: h2.v(k, cs), fA.v(), fB.v())
            ffload(0)
            ri = [0]

            def ff_up(g, c):
                w1 = W1[g % 2]
                cs = slice(c * C, (c + 1) * C)
                hd = hid[(g * NCH + c) % 2]
                for hc in range(4):
                    bk = PB[next_bank()]
                    mm_group(bk.v(), [(w1.v(k, slice(hc * 128, (hc + 1) * 128)), h2.v(k, cs)) for k in range(8)])
                    r = rl[ri[0] % 2]
                    ri[0] += 1
                    act(r.v(), bk.v(), AF.Relu)
                    tt("dve", hd.v(hc), r.v(), r.v(), ALU.mult)

            def ff_down(g, c):
                w2 = W2[g % 2]
                cs = slice(c * C, (c + 1) * C)
                hd = hid[(g * NCH + c) % 2]
                for m in range(8):
                    bk = PB[next_bank()]
                    mm_group(bk.v(), [(w2.v(hc, slice(m * 128, (m + 1) * 128)), hd.v(hc)) for hc in range(4)])
                    stt(xT.v(m, cs), bk.v(), modT.v(l, slice(40 + m, 41 + m)), xT.v(m, cs), ALU.mult, ALU.add)

            def early_stats(c):
                cs = slice(c * C, (c + 1) * C)
                for k in range(8):
                    act(sqf.v(k), xT.v(k, cs), AF.Square)
                bk = PB[next_bank()]
                mm_group(bk.v(), [(ones_b.v(), sqf.v(k)) for k in range(8)])
                rstd_from(bk.v(), rstd1_all.v(cs), D, fA.v())

            seq = [(g, c) for g in range(FFG) for c in range(NCH)]
            loaded = {0}
            ff_up(*seq[0])
            for i, (g, c) in enumerate(seq):
                if c == 0 and g + 1 < FFG and (g + 1) not in loaded:
                    ffload(g + 1)
                    loaded.add(g + 1)
                if i + 1 < len(seq):
                    ff_up(*seq[i + 1])
                ff_down(g, c)
                if g == FFG - 1 and c >= 1:
                    early_stats(c - 1)
            early_stats(NCH - 1)

        o = R_AO
        osq = sb(o, [128, 8, C], BF16); o += 8192
        ors = sb(o, [128, C], F32); o += 2048
        otmp = sb(o, [128, C], F32); o += 2048
        obuf = [sb(o + i * 16384, [128, 8, C], F32) for i in range(2)]; o += 32768
        outv = out_d.rearrange("(k p) t -> p k t", p=128)
        out_tok = None
        out_toks = []
        for c in range(NCH):
            cs = slice(c * C, (c + 1) * C)
            ob_ = obuf[c % 2]
            for k in range(8):
                stt(ob_.v(k), xT.v(k, cs), svc(SV_FING + k), rstd1_all.v(cs), ALU.mult, ALU.mult)
            out_tok = S.dma("sp", [(lambda c=c, ob_=ob_: lambda e: e.dma_start(out=outv[:, :, c * C:(c + 1) * C], in_=ob_.v().ap))()],
                            f"outst{c}", reads=[ob_.v()], writes=[View(None, ("dram", "out"), (0, 1, c, c + 1))])
            out_toks.append(out_tok)
        for tk in out_toks:
            S.wait_tok("sp", tk)
        S.emit(block)
    return nc


_NC_CACHE = {}


def _inv_freq():
    return (np.float32(10000.0) ** (-np.arange(0, 32, 2, dtype=np.float32) / np.float32(32))).astype(np.float32)


def kernel(x, c, positions, ln1_g, ln2_g, w_ada, b_ada, w_in, q_norm_g, w_uq, kv_norm_g, w_uk, w_uv,
           w_pool, pool_scale, p_pool, p_attn, w_out, w_ff1, w_ff2, final_g):
    f32 = np.float32
    x = np.asarray(x, f32)
    c = np.asarray(c, f32)
    positions = np.asarray(positions, np.int32)
    B, S_, _ = x.shape
    dbg = _NC_CACHE.get("debug", False)
    if "nc" not in _NC_CACHE:
        _NC_CACHE["nc"] = build_nc(debug=dbg)
    nc = _NC_CACHE["nc"]

    def fm(v, n):
        return np.asarray(v, f32).reshape(n, 128).T

    tri = (np.arange(128)[None, :] >= np.arange(128)[:, None]).astype(ml_dtypes.bfloat16)
    invf = _inv_freq()
    shared = {
        "tri": tri,
        "w_ada": np.ascontiguousarray(np.asarray(w_ada, f32)),
        "w_in": np.ascontiguousarray(np.asarray(w_in, f32)),
        "w_uq": np.ascontiguousarray(np.asarray(w_uq, f32).reshape(DEPTH, 384, 768)),
        "w_uk": np.ascontiguousarray(np.asarray(w_uk, f32).reshape(DEPTH, 256, 512)),
        "w_uv": np.ascontiguousarray(np.asarray(w_uv, f32).reshape(DEPTH, 256, 512)),
        "w_pool": np.ascontiguousarray(np.asarray(w_pool, f32)),
        "p_pool": np.ascontiguousarray(np.asarray(p_pool, f32)),
        "p_attn": np.ascontiguousarray(np.asarray(p_attn, f32)),
        "w_out": np.ascontiguousarray(np.asarray(w_out, f32)),
        "w_ff1": np.ascontiguousarray(np.asarray(w_ff1, f32)),
        "w_ff2": np.ascontiguousarray(np.asarray(w_ff2, f32)),
    }
    in_maps = []
    for core in range(8):
        b, half = core // 2, core % 2
        t0 = half * NT
        sv = np.zeros((128, NV), f32)
        for l in range(DEPTH):
            sv[:, SV_LN1 + l * 8:SV_LN1 + l * 8 + 8] = fm(ln1_g[l], 8)
            sv[:, SV_LN2 + l * 8:SV_LN2 + l * 8 + 8] = fm(ln2_g[l], 8)
            sv[:, SV_BADA + l * 48:SV_BADA + l * 48 + 48] = fm(b_ada[l], 48)
            sv[:, SV_QNG + l * 3:SV_QNG + l * 3 + 3] = fm(q_norm_g[l], 3)
            sv[:, SV_KVG + l * 2:SV_KVG + l * 2 + 2] = fm(kv_norm_g[l], 2)
            sv[:, SV_PSC + l * 4:SV_PSC + l * 4 + 4] = fm(pool_scale[l], 4)
        sv[:, SV_FING:SV_FING + 8] = fm(final_g, 8)
        sv[64:96, SV_INVF] = np.tile(invf, 2)
        sv[:, SV_FLAG] = float(half)
        for g in range(4):
            w = 2 << g
            tpos = t0 + np.arange(16)
            cnt = np.minimum(tpos + 1, w).astype(f32)
            sv[:, SV_INVCNT + g * 16:SV_INVCNT + g * 16 + 16] = (f32(1.0) / cnt)[None, :]
        sv[:, SV_C:SV_C + 8] = fm(c[b], 8)
        m = dict(shared)
        m["xT"] = np.ascontiguousarray(x[b, t0:t0 + NT, :].T)
        m["pos"] = np.ascontiguousarray(positions[b, t0:t0 + NT].reshape(1, NT))
        m["smallv"] = sv
        in_maps.append(m)
    res = run_bass_kernel_spmd(nc, in_maps, core_ids=list(range(8)))
    _NC_CACHE["res"] = res.results if dbg else None
    out = np.empty((B, S_, D), f32)
    for core in range(8):
        b, half = core // 2, core % 2
        t0 = half * NT
        out[b, t0:t0 + NT, :] = np.asarray(res.results[core]["outT"], f32).T
    return out
```

```python
import math
import numpy as np
import ml_dtypes
import concourse.bass as bass
import concourse.mybir as mybir
from concourse.bass_utils import run_bass_kernel_spmd

F32 = mybir.dt.float32
BF16 = mybir.dt.bfloat16
I32 = mybir.dt.int32
U8 = mybir.dt.uint8
ALU = mybir.AluOpType
AF = mybir.ActivationFunctionType

D = 1024
NT = 2048
CTX = 4096
C = 512
NCH = NT // C
GC = 256
DEPTH = 2
NH = 8
EPS = 1e-6
SCALE = 1.0 / math.sqrt(96.0)
FFG = 8
ESZ = {F32: 4, BF16: 2, I32: 4, U8: 1}

SV_LN1, SV_LN2, SV_BADA, SV_QNG, SV_KVG, SV_PSC, SV_FING = 0, 16, 32, 128, 134, 138, 146
SV_INVF, SV_FLAG, SV_INVCNT, SV_C = 154, 155, 156, 220
NV = 228
ARENA = 207 * 1024


class View:
    __slots__ = ("ap", "space", "iv")

    def __init__(self, ap, space, iv):
        self.ap = ap
        self.space = space
        self.iv = iv


class T:
    def __init__(self, base_ap, space, off, shape, dt):
        self.shape = tuple(shape)
        self.dt = dt
        self.space = space
        self.off = off
        es = ESZ[dt]
        self.es = es
        free = self.shape[1:]
        n = int(np.prod(free))
        self.nbytes = n * es
        strides = []
        s = 1
        for d in reversed(free):
            strides.append(s)
            s *= d
        self.strides = list(reversed(strides))
        if space == "sb":
            ap = base_ap[:, off:off + n * es]
            if dt != U8:
                ap = ap.bitcast(dt)
        else:
            ap = base_ap[:, :]
        if len(free) == 2:
            ap = ap.rearrange("p (a b) -> p a b", a=free[0])
        elif len(free) == 3:
            ap = ap.rearrange("p (a b c) -> p a b c", a=free[0], b=free[1])
        self.full = ap

    def v(self, *idx, p=None):
        free = self.shape[1:]
        idx = list(idx) + [slice(None)] * (len(free) - len(idx))
        lo = 0
        hi = 0
        key = []
        for i, d, st in zip(idx, free, self.strides):
            if isinstance(i, int):
                a, b = i, i + 1
                key.append(i)
            else:
                a = 0 if i.start is None else i.start
                b = d if i.stop is None else i.stop
                key.append(slice(a, b))
            assert 0 <= a < b <= d, (idx, self.shape)
            lo += a * st
            hi += (b - 1) * st
        p0, p1 = (0, self.shape[0]) if p is None else p
        assert p1 <= self.shape[0]
        ap = self.full[(slice(p0, p1),) + tuple(key)]
        return View(ap, self.space, (p0, p1, self.off + lo * self.es, self.off + (hi + 1) * self.es))


class Sched:
    EPOCH = 12000

    def __init__(self, nc, es):
        self.nc = nc
        self.es = es
        self.eng = {"pe": nc.tensor, "act": nc.scalar, "dve": nc.vector, "pool": nc.gpsimd, "sp": nc.sync}
        self.streams = {e: [] for e in self.eng}
        self.cnt = {e: 0 for e in self.eng}
        self.psems = {e: [] for e in self.eng}
        self.waited = {e: {} for e in self.eng}
        self.recs = {}
        self.dsem = {}
        self.pbank = {}
        self.nsem = 0

    def new_sem(self, name):
        self.nsem += 1
        return self.es.enter_context(self.nc.semaphore(name))

    def _ptok(self, e):
        self.cnt[e] += 1
        n = self.cnt[e]
        ei = (n - 1) // self.EPOCH
        while len(self.psems[e]) <= ei:
            self.psems[e].append(self.new_sem(f"p_{e}_{len(self.psems[e])}"))
        return ("p", e, n)

    def _tok_wait(self, tok):
        if tok[0] == "p":
            _, e, n = tok
            ei = (n - 1) // self.EPOCH
            return self.psems[e][ei], n - ei * self.EPOCH
        _, key, val = tok
        return self.dsem[key][0], val

    @staticmethod
    def _ov(a, b):
        return a[0] < b[1] and b[0] < a[1] and a[2] < b[3] and b[2] < a[3]

    @staticmethod
    def _cov(a, b):
        return a[0] <= b[0] and a[1] >= b[1] and a[2] <= b[2] and a[3] >= b[3]

    def _deps(self, reads, writes, e=None):
        deps = []
        for v in reads:
            if v.space[0] == "ps":
                st = self.pbank.get(v.space)
                if st is not None and st[0] != e:
                    deps.append(st[1])
                continue
            for r in self.recs.get(v.space, ()):
                if r[1] == "W" and self._ov(r[0], v.iv):
                    deps.append(r[2])
        for v in writes:
            if v.space[0] == "ps":
                st = self.pbank.get(v.space)
                if st is not None and st[0] != e:
                    deps.append(st[1])
                continue
            for r in self.recs.get(v.space, ()):
                if self._ov(r[0], v.iv):
                    deps.append(r[2])
        return deps

    def _record(self, reads, writes, tok):
        ps = [v for v in list(reads) + list(writes) if v.space[0] == "ps"]
        for v in ps:
            self.pbank[v.space] = (tok[1], tok)
        reads = [v for v in reads if v.space[0] != "ps"]
        writes = [v for v in writes if v.space[0] != "ps"]
        for v in writes:
            lst = self.recs.setdefault(v.space, [])
            lst[:] = [r for r in lst if not self._cov(v.iv, r[0])]
            lst.append([v.iv, "W", tok])
        for v in reads:
            lst = self.recs.setdefault(v.space, [])
            lst[:] = [r for r in lst if not (r[1] == "R" and r[2][0] == "p" and tok[0] == "p"
                                             and r[2][1] == tok[1] and self._cov(v.iv, r[0]))]
            lst.append([v.iv, "R", tok])

    def _waits(self, e, deps):
        best = {}
        for tok in deps:
            if tok[0] == "p":
                if tok[1] == e and e in ("pe", "sp"):
                    continue
                k = ("p", tok[1])
                val = tok[2]
            else:
                k = ("d", tok[1])
                val = tok[2]
            if self.waited[e].get(k, 0) >= val:
                continue
            if best.get(k, (0, None))[0] < val:
                best[k] = (val, tok)
        out = []
        for k, (val, tok) in best.items():
            self.waited[e][k] = val
            out.append(self._tok_wait(tok))
        return out

    def op(self, e, fn, reads=(), writes=(), sig=True):
        deps = self._deps(reads, writes, e)
        waits = self._waits(e, deps)
        if sig:
            tok = self._ptok(e)
            sem, val = self._tok_wait(tok)
            self._record(reads, writes, tok)
        else:
            tok = None
            sem = None
        self.streams[e].append((waits, fn, sem, False))
        return tok

    def dma(self, q, fns, key, reads=(), writes=()):
        deps = self._deps(reads, writes)
        waits = self._waits(q, deps)
        if key not in self.dsem:
            self.dsem[key] = [self.new_sem("d_" + key), 0]
        ds = self.dsem[key]
        ds[1] += 16 * len(fns)
        tok = ("d", key, ds[1])
        self._record(reads, writes, tok)
        first = True
        for fn in fns:
            self.streams[q].append((waits if first else [], fn, ds[0], True))
            first = False
        return tok

    def custom(self, e, fn, reads=(), writes=(), key=None, inc=1):
        deps = self._deps(reads, writes)
        waits = self._waits(e, deps)
        if key not in self.dsem:
            self.dsem[key] = [self.new_sem("c_" + key), 0]
        ds = self.dsem[key]
        ds[1] += inc
        tok = ("d", key, ds[1])
        self._record(reads, writes, tok)
        self.streams[e].append((waits, fn, ds[0], "cc"))
        return tok

    def wait_tok(self, e, tok):
        ws = self._waits(e, [tok])
        if ws:
            self.streams[e].append((ws, None, None, False))

    def emit(self, block):
        def run(e, eng):
            for waits, fn, sem, kind in self.streams[e]:
                for s, v in waits:
                    eng.wait_ge(s, v)
                if fn is None:
                    continue
                inst = fn(eng)
                if sem is not None:
                    if kind is True:
                        inst.then_inc(sem, 16)
                    elif kind == "cc":
                        inst.then_inc(sem)
                    else:
                        inst.then_inc(sem, 1)

        @block.tensor
        def _(eng):
            run("pe", eng)

        @block.scalar
        def _(eng):
            run("act", eng)

        @block.vector
        def _(eng):
            run("dve", eng)

        @block.gpsimd
        def _(eng):
            run("pool", eng)

        @block.sync
        def _(eng):
            run("sp", eng)


def build_nc(debug=False):
    from contextlib import ExitStack
    nc = bass.Bass("TRN2", target_bir_lowering=False)
    dr = {}

    def din(name, shape, dt):
        dr[name] = nc.dram_tensor(name, list(shape), dt, kind="ExternalInput").ap()
        return dr[name]

    xT_d = din("xT", [D, NT], F32)
    pos_d = din("pos", [1, NT], I32)
    sv_d = din("smallv", [128, NV], F32)
    tri_d = din("tri", [128, 128], BF16)
    wada_d = din("w_ada", [DEPTH, D, 6 * D], F32)
    win_d = din("w_in", [DEPTH, D, 3232], F32)
    wuq_d = din("w_uq", [DEPTH, 384, 768], F32)
    wuk_d = din("w_uk", [DEPTH, 256, 512], F32)
    wuv_d = din("w_uv", [DEPTH, 256, 512], F32)
    wpool_d = din("w_pool", [DEPTH, 4, 128, 128], F32)
    ppool_d = din("p_pool", [DEPTH, 512, D], F32)
    pattn_d = din("p_attn", [DEPTH, 512, D], F32)
    wout_d = din("w_out", [DEPTH, D, D], F32)
    wff1_d = din("w_ff1", [DEPTH, D, 4 * D], F32)
    wff2_d = din("w_ff2", [DEPTH, 4 * D, D], F32)
    out_d = nc.dram_tensor("outT", [D, NT], F32, kind="ExternalOutput").ap()

    exs = [nc.dram_tensor(f"exs{l}", [288, NT], BF16) for l in range(DEPTH)]
    exd = [nc.dram_tensor(f"exd{l}", [576, NT], BF16) for l in range(DEPTH)]
    hls = [nc.dram_tensor(f"hls{l}", [128, 64], F32) for l in range(DEPTH)]
    hld = [nc.dram_tensor(f"hld{l}", [256, 64], F32) for l in range(DEPTH)]
    tabs = nc.dram_tensor("tabs", [256, NT], F32)
    if debug:
        dbg_q = nc.dram_tensor("dbg_q", [128, NH * NT], BF16, kind="ExternalOutput").ap()
        dbg_ckv = nc.dram_tensor("dbg_ckv", [128, 2 * CTX], BF16, kind="ExternalOutput").ap()
        dbg_kr = nc.dram_tensor("dbg_kr", [128, CTX], BF16, kind="ExternalOutput").ap()
        dbg_ao = nc.dram_tensor("dbg_ao", [128, 4 * NT], BF16, kind="ExternalOutput").ap()
        dbg_cos = nc.dram_tensor("dbg_cos", [128, 2 * NT], F32, kind="ExternalOutput").ap()
        dbg_pat = nc.dram_tensor("dbg_pat", [128, 4 * D], BF16, kind="ExternalOutput").ap()
        dbg_mrg = nc.dram_tensor("dbg_mrg", [128, 8 * GC], BF16, kind="ExternalOutput").ap()

    with ExitStack() as es:
        arena = es.enter_context(nc.sbuf_tensor("arena", [128, ARENA], U8))
        banks = [es.enter_context(nc.psum_tensor(f"bank{i}", [128, 512], F32)) for i in range(8)]
        S = Sched(nc, es)
        block = es.enter_context(nc.Block())

        def sb(off, shape, dt):
            t = T(arena, "sb", off, shape, dt)
            assert off + t.nbytes <= ARENA, (off, shape)
            return t

        PB = [T(banks[i], ("ps", i), 0, [128, 512], F32) for i in range(8)]

        def dview(handle, name):
            return View(None, ("dram", name), (0, 1, 0, 1))

        xT = sb(0, [128, 8, NT], F32)
        o = 65536
        smallv = sb(o, [128, NV], F32); o += NV * 4
        modT = sb(o, [128, DEPTH, 48], F32); o += DEPTH * 48 * 4
        avec = sb(o, [128, DEPTH, 2, 8], F32); o += DEPTH * 16 * 4
        cact = sb(o, [128, 8], BF16); o += 16
        ones_b = sb(o, [128, 128], BF16); o += 256
        tri = sb(o, [128, 128], BF16); o += 256
        epsT = sb(o, [128, 1], F32); o += 4
        zeroT = sb(o, [128, 1], F32); o += 4
        ones_f = sb(o, [128, 128], F32); o += 512
        sel_f = sb(o, [128, 128], F32); o += 512
        flag16 = sb(o, [128, 16], BF16); o += 32
        halo_sb = sb(o, [128, 4, 16], F32); o += 256
        halo_in = sb(o, [128, 4, 16], F32); o += 256
        assert o <= 69632, o
        P0 = 69632
        R_AO = P0
        R_CTX = R_AO + 16384
        R_Q = R_CTX + 24576
        R_X = R_Q + 32768
        assert R_X == 143360

        cosT = sb(R_AO, [128, NT], F32)
        sinT = sb(R_AO + 8192, [128, NT], F32)
        attn_o = sb(R_AO, [128, 4, NT], BF16)
        ctx_ckv = sb(R_CTX, [128, 2, CTX], BF16)
        ctx_kr = sb(R_CTX + 16384, [128, CTX], BF16)
        qT_all = sb(R_Q, [128, NH, NT], BF16)

        def svc(col, n=1):
            return smallv.v(slice(col, col + n))

        bank_rr = [0]

        def next_bank(pool=(0, 1, 2, 3, 4, 5, 6, 7)):
            b = pool[bank_rr[0] % len(pool)]
            bank_rr[0] += 1
            return b

        def mm_group(out, pairs, extra_reads=()):
            n = len(pairs)
            allreads = [v for pr in pairs for v in pr] + list(extra_reads)
            for i, (l, r) in enumerate(pairs):
                last = i == n - 1
                fn = (lambda l=l, r=r, i=i, last=last: lambda e: e.matmul(out.ap, l.ap, r.ap, start=(i == 0), stop=last))()
                if last:
                    S.op("pe", fn, reads=allreads, writes=[out])
                else:
                    if i == 0:
                        S.op("pe", fn, reads=allreads, writes=[out], sig=False)
                    else:
                        S.op("pe", fn, sig=False)

        def act(out, in_, func, scale=1.0, bias=None, extra_reads=()):
            rd = [in_] + list(extra_reads)
            kw = {}
            if bias is not None:
                kw["bias"] = bias.ap
                rd.append(bias)
            if isinstance(scale, View):
                rd.append(scale)
                sc = scale.ap
            else:
                sc = scale
            S.op("act", lambda e: e.activation(out=out.ap, in_=in_.ap, func=func, scale=sc, **kw),
                 reads=rd, writes=[out])

        def tt(eng, out, a, b, op):
            S.op(eng, lambda e: e.tensor_tensor(out=out.ap, in0=a.ap, in1=b.ap, op=op), reads=[a, b], writes=[out])

        def ts(eng, out, a, s1, op0, s2=None, op1=None):
            rd = [a]
            s1v = s1.ap if isinstance(s1, View) else s1
            s2v = s2.ap if isinstance(s2, View) else s2
            if isinstance(s1, View):
                rd.append(s1)
            if isinstance(s2, View):
                rd.append(s2)
            if op1 is None:
                S.op(eng, lambda e: e.tensor_scalar(out=out.ap, in0=a.ap, scalar1=s1v, scalar2=None, op0=op0),
                     reads=rd, writes=[out])
            else:
                S.op(eng, lambda e: e.tensor_scalar(out=out.ap, in0=a.ap, scalar1=s1v, scalar2=s2v, op0=op0, op1=op1),
                     reads=rd, writes=[out])

        def stt(out, a, s, b, op0, op1):
            rd = [a, b]
            sv = s.ap if isinstance(s, View) else s
            if isinstance(s, View):
                rd.append(s)
            S.op("dve", lambda e: e.scalar_tensor_tensor(out=out.ap, in0=a.ap, scalar=sv, in1=b.ap, op0=op0, op1=op1),
                 reads=rd, writes=[out])

        def copy(eng, out, in_):
            if eng == "act":
                S.op("act", lambda e: e.copy(out=out.ap, in_=in_.ap), reads=[in_], writes=[out])
            else:
                S.op(eng, lambda e: e.tensor_copy(out=out.ap, in_=in_.ap), reads=[in_], writes=[out])

        def memset(eng, out, val):
            S.op(eng, lambda e: e.memset(out.ap, val), writes=[out])

        def wload(dst, src_ap, key):
            S.dma("pool", [lambda e: e.dma_start(out=dst.ap, in_=src_ap)], key, writes=[dst])

        def rstd_from(stat_ps, out, n, tmp):
            act(tmp, stat_ps, AF.Ln, scale=1.0 / n, bias=epsT.v())
            act(out, tmp, AF.Exp, scale=-0.5)

        S.dma("sp", [lambda e: e.dma_start(out=smallv.v().ap, in_=sv_d)], "smallv", writes=[smallv.v()])
        S.dma("sp", [lambda e: e.dma_start(out=tri.v().ap, in_=tri_d)], "tri", writes=[tri.v()])
        posi = sb(R_CTX, [128, NT], I32)
        S.dma("sp", [lambda e: e.dma_start(out=posi.v().ap, in_=pos_d.partition_broadcast(128)[:, 0, :])],
              "pos", writes=[posi.v()])
        xTv = xT_d.rearrange("(k p) t -> p k t", p=128)
        for k in range(8):
            dst = xT.v(k)
            S.dma("sp", [(lambda k=k, dst=dst: lambda e: e.dma_start(out=dst.ap, in_=xTv[:, k, :]))()],
                  f"x{k}", writes=[dst])

        Wkr0 = sb(R_X + 4096, [128, 8, 96], BF16)
        Wkrr0 = sb(R_X + 4096 + 1536, [128, 8, 96], BF16)
        memset("dve", Wkr0.v(), 0.0)
        memset("dve", Wkrr0.v(), 0.0)
        memset("dve", ones_b.v(), 1.0)
        memset("dve", epsT.v(), EPS)
        memset("dve", zeroT.v(), 0.0)
        memset("dve", ones_f.v(), 1.0)
        memset("dve", sel_f.v(), 0.0)
        memset("dve", sel_f.v(slice(64, 128)), 1.0)
        memset("dve", flag16.v(), 1.0)
        ts("dve", flag16.v(), flag16.v(), svc(SV_FLAG), ALU.mult)

        act(cact.v(), svc(SV_C, 8), AF.Silu)

        PI = math.pi
        ang = sb(R_Q, [128, NT], F32)
        t1 = sb(R_Q + 8192, [128, NT], F32)
        t2 = sb(R_Q + 16384, [128, NT], F32)
        ki = sb(R_Q + 24576, [128, NT], I32)
        copy("dve", ang.v(), posi.v())
        ts("dve", ang.v(), ang.v(), svc(SV_INVF), ALU.mult)
        ts("dve", t1.v(), ang.v(), 1.0 / (2 * PI), ALU.mult)
        copy("dve", ki.v(), t1.v())
        copy("dve", t1.v(), ki.v())
        C1 = 6.28125
        C2 = 2 * PI - C1
        stt(t2.v(), t1.v(), -C1, ang.v(), ALU.mult, ALU.add)
        stt(t2.v(), t1.v(), -C2, t2.v(), ALU.mult, ALU.add)

        def wrap(r, tmp):
            ts("dve", tmp, r, PI, ALU.is_gt)
            stt(r, tmp, -2 * PI, r, ALU.mult, ALU.add)
            ts("dve", tmp, r, -PI, ALU.is_lt)
            stt(r, tmp, 2 * PI, r, ALU.mult, ALU.add)

        wrap(t2.v(), t1.v())
        act(sinT.v(), t2.v(), AF.Sin)
        ts("dve", t2.v(), t2.v(), PI / 2, ALU.add)
        wrap(t2.v(), t1.v())
        act(cosT.v(), t2.v(), AF.Sin)
        tabs_v = dview(tabs, "tabs")
        if debug:
            S.dma("sp", [lambda e: e.dma_start(out=dbg_cos[:, 0:NT], in_=cosT.v().ap),
                         lambda e: e.dma_start(out=dbg_cos[:, NT:2 * NT], in_=sinT.v().ap)],
                  "dbg0", reads=[cosT.v(), sinT.v()], writes=[View(None, ("dram", "dbg0"), (0, 1, 0, 1))])
        S.dma("sp", [lambda e: e.dma_start(out=tabs.ap()[0:128, :], in_=cosT.v().ap),
                     lambda e: e.dma_start(out=tabs.ap()[128:256, :], in_=sinT.v().ap)],
              "tabs_st", reads=[cosT.v(), sinT.v()], writes=[tabs_v])

        wada_b = [sb(R_CTX + 8192, [128, 8, D], BF16), sb(R_X + 30720, [128, 8, D], BF16)]
        for j in range(2):
            wb = wada_b[j]
            src = wada_d[0].rearrange("(k p) n -> p k n", p=128)[:, :, j * D:(j + 1) * D]
            wload(wb.v(), src, f"wada{j}")
            bk = PB[next_bank()]
            for m in range(8):
                mm_group(bk.v(slice(m, m + 1)),
                         [(wb.v(k, slice(m * 128, (m + 1) * 128)), cact.v(slice(k, k + 1))) for k in range(8)])
            tt("dve", modT.v(0, slice(j * 8, (j + 1) * 8)), bk.v(slice(0, 8)), svc(SV_BADA + j * 8, 8), ALU.add)
        stt(avec.v(0, 0), modT.v(0, slice(8, 16)), 1.0, svc(SV_LN1, 8), ALU.add, ALU.mult)
        wpc = sb(R_X + 51200, [128, 8, 512], BF16)
        mod_pieces = [(0, q) for q in range(4, 12)] + [(1, q) for q in range(12)]

        def mod_piece_step(lq, banks):
            ll, q = lq

            def f():
                src = wada_d[ll].rearrange("(k p) n -> p k n", p=128)[:, :, q * 512:(q + 1) * 512]
                wload(wpc.v(), src, "wpc")
                bk = PB[next_bank(banks)]
                for m in range(4):
                    mm_group(bk.v(slice(m, m + 1)),
                             [(wpc.v(k, slice(m * 128, (m + 1) * 128)), cact.v(slice(k, k + 1))) for k in range(8)])
                tt("dve", modT.v(ll, slice(q * 4, (q + 1) * 4)), bk.v(slice(0, 4)),
                   svc(SV_BADA + ll * 48 + q * 4, 4), ALU.add)
                if q == 3:
                    stt(avec.v(ll, 0), modT.v(ll, slice(8, 16)), 1.0, svc(SV_LN1 + ll * 8, 8), ALU.add, ALU.mult)
                if q == 9:
                    stt(avec.v(ll, 1), modT.v(ll, slice(32, 40)), 1.0, svc(SV_LN2 + ll * 8, 8), ALU.add, ALU.mult)
            return f

        rstd1_all = sb(ARENA - 8192, [128, NT], F32)
        sq0 = sb(R_X + 49152, [128, 8, C], BF16)
        ln0 = sb(R_X + 57344, [128, C], F32)
        for c in range(NCH):
            cs = slice(c * C, (c + 1) * C)
            for k in range(8):
                act(sq0.v(k), xT.v(k, cs), AF.Square)
            bk = PB[next_bank()]
            mm_group(bk.v(), [(ones_b.v(), sq0.v(k)) for k in range(8)])
            act(ln0.v(), bk.v(), AF.Ln, scale=1.0 / D, bias=epsT.v())
            act(rstd1_all.v(cs), ln0.v(), AF.Exp, scale=-0.5)
        def stats_all(rstd_all, sq, lntmp):
            for c in range(NCH):
                cs = slice(c * C, (c + 1) * C)
                for k in range(8):
                    act(sq.v(k), xT.v(k, cs), AF.Square)
                bk = PB[next_bank()]
                mm_group(bk.v(), [(ones_b.v(), sq.v(k)) for k in range(8)])
                rstd_from(bk.v(), rstd_all.v(cs), D, lntmp)

        def h_from(l, which, xv_k, rstd, hv_k, tmpA, tmpB):
            boff = 0 if which == 0 else 24
            for k in range(8):
                tmp = tmpA if k % 2 == 0 else tmpB
                tt("dve", tmp, xv_k(k), rstd, ALU.mult)
                act(hv_k(k), tmp, AF.Identity, scale=avec.v(l, which, slice(k, k + 1)),
                    bias=modT.v(l, slice(boff + k, boff + k + 1)))

        for l in range(DEPTH):
            o = R_X
            Wkv = sb(o, [128, 8, 256], BF16); o += 4096
            Wkr = sb(o, [128, 8, 96], BF16); o += 1536
            Wkrr = sb(o, [128, 8, 96], BF16); o += 1536
            Wq = sb(o, [128, 8, 384], BF16); o += 6144
            wuq = sb(o, [128, 3, 768], BF16); o += 4608
            wuqr = sb(o, [128, 3, 768], BF16); o += 4608
            Wu = sb(o, [128, 8, 512], BF16); o += 8192
            hT = sb(o, [128, 8, C], BF16); o += 8192
            sq = sb(o, [128, 8, C], BF16); o += 8192
            tmpA = sb(o, [128, C], F32); o += 2048
            tmpB = sb(o, [128, C], F32); o += 2048
            rstdq = sb(o, [128, C], F32); o += 2048
            rstdkv = rstdq
            cqn = sb(o, [128, 3, C], BF16); o += 3072
            sql = sb(o, [128, 3, C], BF16); o += 3072
            assert o <= ARENA - 8192, o
            hT2 = [hT, sq]
            rstd1_all = sb(ARENA - 8192, [128, NT], F32)

            winv = win_d[l].rearrange("(k p) n -> p k n", p=128)
            wload(Wkv.v(), winv[:, :, 896:1152], "Wkv")
            if l == 0:
                wload(Wkr.v(slice(None), slice(64, 96)), winv[:, :, 1152:1184], "Wkr")
            wload(Wq.v(), winv[:, :, 512:896], "Wq")
            wload(wuq.v(), wuq_d[l].rearrange("(j p) n -> p j n", p=128), "wuq")
            wload(Wu.v(), winv[:, :, 0:512], "Wu")
            if l > 0:
                memset("dve", Wkr.v(), 0.0)
                memset("dve", Wkrr.v(), 0.0)
                wload(Wkr.v(slice(None), slice(64, 96)), winv[:, :, 1152:1184], "Wkr")
            ts("dve", Wkrr.v(slice(None), slice(64, 80)), Wkr.v(slice(None), slice(80, 96)), -1.0, ALU.mult)
            copy("dve", Wkrr.v(slice(None), slice(80, 96)), Wkr.v(slice(None), slice(64, 80)))
            wuq4 = wuq.full.rearrange("p j (h d) -> p j h d", h=NH)
            wuqr4 = wuqr.full.rearrange("p j (h d) -> p j h d", h=NH)
            memset("dve", wuqr.v(), 0.0)
            S.op("dve", lambda e: e.tensor_scalar(out=wuqr4[:, :, :, 64:80], in0=wuq4[:, :, :, 80:96], scalar1=-1.0,
                                                    scalar2=None, op0=ALU.mult),
                 reads=[wuq.v()], writes=[wuqr.v()])
            S.op("dve", lambda e: e.tensor_copy(out=wuqr4[:, :, :, 80:96], in_=wuq4[:, :, :, 64:80]),
                 reads=[wuq.v()], writes=[wuqr.v()])
            if l > 0:
                S.dma("sp", [lambda e: e.dma_start(out=cosT.v().ap, in_=tabs.ap()[0:128, :]),
                             lambda e: e.dma_start(out=sinT.v().ap, in_=tabs.ap()[128:256, :])],
                      "tabs_ld", reads=[tabs_v], writes=[cosT.v(), sinT.v()])

            for c in range(NCH):
                cs = slice(c * C, (c + 1) * C)
                ls = slice(NT + c * C, NT + (c + 1) * C)
                hT = hT2[c % 2]
                if c == 0:
                    h_from(l, 0, lambda k: xT.v(k, cs), rstd1_all.v(cs), lambda k: hT.v(k), tmpA.v(), tmpB.v())
                bkv = [PB[next_bank()] for _ in range(2)]
                for j in range(2):
                    mm_group(bkv[j].v(), [(Wkv.v(k, slice(j * 128, (j + 1) * 128)), hT.v(k)) for k in range(8)])
                    act(cqn.v(j), bkv[j].v(), AF.Square)
                bq = [PB[next_bank()] for _ in range(3)]
                for j in range(3):
                    mm_group(bq[j].v(), [(Wq.v(k, slice(j * 128, (j + 1) * 128)), hT.v(k)) for k in range(8)])
                    act(sql.v(j), bq[j].v(), AF.Square)
                bs = PB[next_bank()]
                mm_group(bs.v(), [(ones_b.v(), cqn.v(j)) for j in range(2)])
                rstd_from(bs.v(), rstdkv.v(), 256, tmpA.v())
                for j in range(2):
                    stt(ctx_ckv.v(j, ls), bkv[j].v(), svc(SV_KVG + l * 2 + j), rstdkv.v(), ALU.mult, ALU.mult)
                bkr = PB[next_bank()]
                bkrr = PB[next_bank()]
                mm_group(bkr.v(p=(0, 96)), [(Wkr.v(k), hT.v(k)) for k in range(8)])
                mm_group(bkrr.v(p=(0, 96)), [(Wkrr.v(k), hT.v(k)) for k in range(8)])
                R = (64, 96)
                tt("dve", tmpA.v(p=R), bkr.v(p=R), cosT.v(cs, p=R), ALU.mult)
                tt("dve", tmpB.v(p=R), bkrr.v(p=R), sinT.v(cs, p=R), ALU.mult)
                tt("dve", ctx_kr.v(ls, p=R), tmpA.v(p=R), tmpB.v(p=R), ALU.add)
                bs = PB[next_bank()]
                mm_group(bs.v(), [(ones_b.v(), sql.v(j)) for j in range(3)])
                rstd_from(bs.v(), rstdq.v(), 384, tmpA.v())
                ts("dve", rstdq.v(), rstdq.v(), SCALE, ALU.mult)
                for j in range(3):
                    stt(cqn.v(j), bq[j].v(), svc(SV_QNG + l * 3 + j), rstdq.v(), ALU.mult, ALU.mult)
                Q = (0, 96)
                if c == NCH - 1:
                    bh = PB[next_bank()]
                    for g in range(4):
                        mm_group(bh.v(slice(g * 16, (g + 1) * 16)),
                                 [(Wu.v(k, slice(g * 128, (g + 1) * 128)), hT.v(k, slice(C - 16, C))) for k in range(8)])
                    S.op("dve", lambda e, bh=bh: e.tensor_copy(
                        out=halo_sb.full, in_=bh.full[:, 0:64].rearrange("p (g t) -> p g t", g=4)),
                        reads=[bh.v(slice(0, 64))], writes=[halo_sb.v()])

                if c + 1 < NCH:
                    cs2 = slice((c + 1) * C, (c + 2) * C)
                    hTn = hT2[(c + 1) % 2]
                    h_from(l, 0, lambda k: xT.v(k, cs2), rstd1_all.v(cs2), lambda k: hTn.v(k), tmpA.v(), tmpB.v())
                for h in range(NH):
                    ba = PB[next_bank()]
                    bb = PB[next_bank()]
                    hs = slice(h * 96, (h + 1) * 96)
                    mm_group(ba.v(p=Q), [(wuq.v(j, hs), cqn.v(j)) for j in range(3)])
                    mm_group(bb.v(p=Q), [(wuqr.v(j, hs), cqn.v(j)) for j in range(3)])
                    tt("dve", tmpA.v(p=Q), ba.v(p=Q), cosT.v(cs, p=Q), ALU.mult)
                    tt("dve", tmpB.v(p=Q), bb.v(p=Q), sinT.v(cs, p=Q), ALU.mult)
                    tt("dve", qT_all.v(h, cs, p=Q), tmpA.v(p=Q), tmpB.v(p=Q), ALU.add)
            wuk = sb(R_X, [128, 2, 512], BF16)
            wuv = sb(R_X + 2048, [128, 2, 512], BF16)
            wload(wuk.v(), wuk_d[l].rearrange("(j p) n -> p j n", p=128), "wuk")
            wload(wuv.v(), wuv_d[l].rearrange("(j p) n -> p j n", p=128), "wuv")
            exs_v = dview(exs[l], f"exs{l}")
            exd_v = dview(exd[l], f"exd{l}")
            hls_v = dview(hls[l], f"hls{l}")
            hld_v = dview(hld[l], f"hld{l}")
            own = slice(NT, CTX)
            S.dma("sp", [lambda e, l=l: e.dma_start(out=exs[l].ap()[0:128, :], in_=ctx_ckv.v(0, own).ap),
                         lambda e, l=l: e.dma_start(out=exs[l].ap()[128:256, :], in_=ctx_ckv.v(1, own).ap),
                         lambda e, l=l: e.dma_start(out=exs[l].ap()[256:288, :], in_=ctx_kr.v(own, p=(64, 96)).ap)],
                  f"exst{l}", reads=[ctx_ckv.v(0, own), ctx_ckv.v(1, own), ctx_kr.v(own, p=(64, 96))],
                  writes=[exs_v])
            S.dma("sp", [lambda e, l=l: e.dma_start(out=hls[l].ap(), in_=halo_sb.full.rearrange("p g t -> p (g t)"))],
                  f"hlst{l}", reads=[halo_sb.v()], writes=[hls_v])
            RG = [[0, 1], [2, 3], [4, 5], [6, 7]]
            cc_tok = S.custom("pool", lambda e, l=l: e.collective_compute(
                "AllGather", ALU.bypass, replica_groups=RG, ins=[exs[l].ap().opt()], outs=[exd[l].ap().opt()]),
                reads=[exs_v], writes=[exd_v], key=f"cc{l}")
            S.wait_tok("pool", cc_tok)
            S.custom("pool", lambda e, l=l: e.collective_compute(
                "AllGather", ALU.bypass, replica_groups=RG, ins=[hls[l].ap().opt()], outs=[hld[l].ap().opt()]),
                reads=[hls_v], writes=[hld_v], key=f"cch{l}")
            rem = slice(0, NT)
            for j in range(2):
                S.dma("sp", [(lambda l=l, j=j: lambda e: e.dma_start(out=ctx_ckv.v(j, rem).ap,
                                                                      in_=exd[l].ap()[j * 128:(j + 1) * 128, :]))()],
                      f"exld{l}_{j}", reads=[exd_v], writes=[ctx_ckv.v(j, rem)])
            S.dma("sp", [lambda e, l=l: e.dma_start(out=ctx_kr.v(rem, p=(64, 96)).ap, in_=exd[l].ap()[256:288, :])],
                  f"exldk{l}", reads=[exd_v], writes=[ctx_kr.v(rem, p=(64, 96))])
            S.dma("sp", [lambda e, l=l: e.dma_start(out=halo_in.full.rearrange("p g t -> p (g t)"), in_=hld[l].ap()[0:128, :])],
                  f"exldh{l}", reads=[hld_v], writes=[halo_in.v()])

            if debug and l == 0:
                S.dma("sp", [lambda e: e.dma_start(out=dbg_q, in_=qT_all.full.rearrange("p h t -> p (h t)")),
                             lambda e: e.dma_start(out=dbg_ckv, in_=ctx_ckv.full.rearrange("p j t -> p (j t)")),
                             lambda e: e.dma_start(out=dbg_kr, in_=ctx_kr.full)],
                      "dbg1", reads=[qT_all.v(), ctx_ckv.v(), ctx_kr.v()], writes=[View(None, ("dram", "dbg1"), (0, 1, 0, 1))])
            o = R_X
            o += 4096
            KT = [sb(o + i * 8192, [128, CTX], BF16) for i in range(2)]; o += 16384
            VV = [sb(o + i * 8192, [128, 32, 128], BF16) for i in range(2)]; o += 16384
            NPT = 6
            LA = 4
            PT = [sb(o + i * 1024, [128, C], BF16) for i in range(NPT)]; o += NPT * 1024
            Osb = [sb(o + i * 2048, [128, C], F32) for i in range(2)]; o += 4096
            rdn = [sb(o + i * 2048, [128, C], F32) for i in range(2)]; o += 4096
            assert o <= ARENA
            o = R_CTX
            Wu2 = sb(o, [128, 8, 512], BF16); o += 8192
            Wga = sb(o, [128, 8, D], BF16); o += 16384
            Wgb = sb(o, [128, 8, D], BF16); o += 16384
            wpl = sb(o, [128, 4, 128], BF16); o += 1024
            ppl = sb(o, [128, 4, D], BF16); o += 8192
            pat = sb(o, [128, 4, D], BF16); o += 8192
            wo = sb(o, [128, 8, D], BF16); o += 16384
            G_o = o
            S_BANKS = (0, 1, 2, 3, 4, 7)

            def build_steps(h):
                lower = h < 4
                kt_t = KT[h % 2]
                v_t = VV[h % 2]
                voff = 0 if lower else 64
                vcol = 64 if lower else 0

                def kstep(cc):
                    def f():
                        bk = PB[next_bank(S_BANKS)]
                        ks = slice(cc * C, (cc + 1) * C)
                        mm_group(bk.v(p=(0, 64)),
                                 [(wuk.v(j, slice(h * 64, (h + 1) * 64)), ctx_ckv.v(j, ks)) for j in range(2)])
                        copy("dve", kt_t.v(ks, p=(0, 64)), bk.v(p=(0, 64)))
                    return f

                def krope(part):
                    def f():
                        copy("dve", kt_t.v(part, p=(64, 96)), ctx_kr.v(part, p=(64, 96)))
                    return f

                def vinit():
                    memset("dve", v_t.v(), 0.0)
                    S.op("dve", lambda e: e.tensor_copy(out=v_t.full[:, 0:16, vcol], in_=flag16.full),
                         reads=[flag16.v()], writes=[v_t.v(slice(0, 16))])
                    S.op("dve", lambda e: e.memset(v_t.full[:, 16:32, vcol], 1.0), writes=[v_t.v(slice(16, 32))])

                def vstep(kb):
                    def f():
                        bk = PB[next_bank(S_BANKS)]
                        for i in range(8):
                            kt = kb * 8 + i
                            mm_group(bk.v(slice(i * 64, (i + 1) * 64)),
                                     [(ctx_ckv.v(j, slice(kt * 128, (kt + 1) * 128)), wuv.v(j, slice(h * 64, (h + 1) * 64)))
                                      for j in range(2)])
                        src = bk.full.rearrange("p (i d) -> p i d", i=8)
                        dstap = v_t.full[:, kb * 8:(kb + 1) * 8, voff:voff + 64]
                        if kb < 2:
                            S.op("dve", lambda e: e.tensor_scalar(out=dstap, in0=src, scalar1=svc(SV_FLAG).ap,
                                                                    scalar2=None, op0=ALU.mult),
                                 reads=[bk.v(), svc(SV_FLAG)], writes=[v_t.v(slice(kb * 8, (kb + 1) * 8))])
                        else:
                            S.op("dve", lambda e: e.tensor_copy(out=dstap, in_=src),
                                 reads=[bk.v()], writes=[v_t.v(slice(kb * 8, (kb + 1) * 8))])
                    return f

                return ([vinit] + [kstep(cc) for cc in (4, 5, 6, 7)] + [krope(slice(NT, CTX)), vstep(2), vstep(3)]
                        + [kstep(cc) for cc in (0, 1, 2, 3)] + [krope(slice(0, NT)), vstep(0), vstep(1)])

            for st in build_steps(0):
                st()
            pti = 0
            oi = 0
            items = []
            step_at = {}
            for h in range(NH):
                base = len(items)
                for c in range(NCH):
                    tiles = [(kt, 0) for kt in range(16 + 4 * c)] + [(16 + 4 * c + i, i) for i in range(4)]
                    for ti, (kt, i0) in enumerate(tiles):
                        items.append((h, c, kt, i0, ti, len(tiles)))
                n_h = len(items) - base
                nxt = build_steps(h + 1) if h + 1 < NH else []
                if nxt:
                    gap = n_h // (len(nxt) + 1)
                    for si, st in enumerate(nxt):
                        step_at.setdefault(base + (si + 1) * gap, []).append(st)
                if l == 0:
                    for at in (3, 38, 73):
                        if mod_pieces:
                            step_at.setdefault(base + at, []).append(mod_piece_step(mod_pieces.pop(0), S_BANKS))
                if h == 4:
                    step_at.setdefault(base + 12, []).append(
                        lambda: wload(Wgb.v(), winv[:, :, 2208:3232], "Wgb"))
                if h == 5:
                    step_at.setdefault(base + 12, []).append(
                        lambda: wload(wpl.v(), wpool_d[l].rearrange("g c d -> c g d"), "wpl"))
                if h == 7:
                    step_at.setdefault(base + 12, []).append(
                        lambda: (wload(Wu2.v(), winv[:, :, 0:512], "Wu2"),
                                 wload(ppl.v(), ppool_d[l].rearrange("(g p) n -> p g n", p=128), "ppl"),
                                 wload(Wga.v(), winv[:, :, 1184:2208], "Wga")))
            n_items = len(items)
            deferred = []
            pend = {}
            obs = {}
            for idx in range(n_items + LA):
                if idx < n_items:
                    h, c, kt, i0, ti, nt_ = items[idx]
                    q0 = i0 * 128
                    n = C - q0
                    sbk = PB[next_bank(S_BANKS)]
                    sv_ = sbk.v(slice(0, n))
                    qv = qT_all.v(h, slice(c * C + q0, (c + 1) * C), p=(0, 96))
                    kv = KT[h % 2].v(slice(kt * 128, (kt + 1) * 128), p=(0, 96))
                    mm_group(sv_, [(kv, qv)])
                    pt = PT[pti % NPT]
                    pti += 1
                    pv = pt.v(slice(0, n))
                    act(pv, sv_, AF.Exp)
                    if kt >= 16 + 4 * c:
                        tt("dve", pt.v(slice(0, 128)), pt.v(slice(0, 128)), tri.v(), ALU.mult)
                    pend[idx] = pv
                    if ti == 0:
                        obs[(h, c)] = (PB[5 + oi % 2], Osb[oi % 2], rdn[oi % 2])
                        oi += 1
                j = idx - LA
                if j >= 0:
                    h, c, kt, i0, ti, nt_ = items[j]
                    lower = h < 4
                    q0 = i0 * 128
                    ob, osb, rden = obs.pop((h, c)) if ti == nt_ - 1 else obs[(h, c)]
                    ov = ob.v(slice(q0, C))
                    vv = VV[h % 2].v(kt)
                    pv = pend.pop(j)
                    S.op("pe", (lambda ov=ov, vv=vv, pv=pv, ti=ti, nt_=nt_: lambda e: e.matmul(
                        ov.ap, vv.ap, pv.ap, start=(ti == 0), stop=(ti == nt_ - 1)))(),
                        reads=[vv, pv], writes=[ov])
                    if ti == nt_ - 1:
                        copy("dve", osb.v(), ob.v())
                        rp = (64, 65) if lower else (0, 1)
                        S.op("dve", (lambda osb=osb, rden=rden, rp=rp: lambda e: e.reciprocal(
                            out=rden.v(p=rp).ap, in_=osb.v(p=rp).ap))(),
                            reads=[osb.v(p=rp)], writes=[rden.v(p=rp)])

                        def fin(h=h, c=c, osb=osb, rden=rden, rp=rp, lower=lower):
                            bcb = PB[next_bank(S_BANKS)]
                            mm_group(bcb.v(), [((ones_f if lower else sel_f).v(p=rp), rden.v(p=rp))])
                            rows = (0, 64) if lower else (64, 128)
                            tt("dve", attn_o.v(h % 4, slice(c * C, (c + 1) * C), p=rows), osb.v(p=rows),
                               bcb.v(p=rows), ALU.mult)
                        deferred.append((idx + 16, fin))
                for st in step_at.get(idx, []):
                    st()
                while deferred and deferred[0][0] <= idx:
                    deferred.pop(0)[1]()
            for _, fn in deferred:
                fn()

            if debug and l == 0:
                S.dma("sp", [lambda e: e.dma_start(out=dbg_ao, in_=attn_o.full.rearrange("p j t -> p (j t)"))],
                      "dbg2", reads=[attn_o.v()], writes=[View(None, ("dram", "dbg2"), (0, 1, 0, 1))])
            o = G_o
            hTg2 = [sb(o + i * 4096, [128, 8, GC], BF16) for i in range(2)]; o += 8192
            gA = sb(o, [128, GC], F32); o += 1024
            gB = sb(o, [128, GC], F32); o += 1024
            hA = sb(o, [128, GC], F32); o += 1024
            hB = sb(o, [128, GC], F32); o += 1024
            uT = sb(o, [128, 4, 16 + GC], F32); o += 4 * (16 + GC) * 4
            wA = sb(o, [128, 4, 16 + GC], F32); o += 4 * (16 + GC) * 4
            wB = sb(o, [128, 4, 16 + GC], F32); o += 4 * (16 + GC) * 4
            pTt = sb(o, [128, 4, GC], BF16); o += 4 * GC * 2
            yTt = sb(o, [128, 4, GC], BF16); o += 4 * GC * 2
            mrg = sb(o, [128, 8, GC], BF16); o += 8 * GC * 2
            sgA = sb(o, [128, GC], F32); o += 1024
            sgB = sb(o, [128, GC], F32); o += 1024
            pav = pattn_d[l].rearrange("(hh j v) n -> hh v j n", hh=2, j=4)
            S.dma("pool", [lambda e, pav=pav: e.dma_start(out=pat.v(p=(0, 64)).ap, in_=pav[0]),
                           lambda e, pav=pav: e.dma_start(out=pat.v(p=(64, 128)).ap, in_=pav[1])], "pat", writes=[pat.v()])
            wload(wo.v(), wout_d[l].rearrange("(k p) n -> p k n", p=128), "wo")
            yT2 = [yTt, sb(o, [128, 4, GC], BF16)]; o += 4 * GC * 2
            sqe = sb(o, [128, 8, GC], BF16); o += 8 * GC * 2
            assert o <= ARENA - 8192, o
            ts("dve", uT.v(slice(None), slice(0, 16)), halo_in.v(), svc(SV_FLAG), ALU.mult)
            NG = NT // GC
            L_ = 16 + GC

            def gsl(gc):
                return slice(gc * GC, (gc + 1) * GC)

            def P0(gc):
                hTg = hTg2[gc % 2]
                gs = gsl(gc)
                h_from(l, 0, lambda k: xT.v(k, gs), rstd1_all.v(gs), lambda k: hTg.v(k), hA.v(), hB.v())

            def P1(gc):
                hTg = hTg2[gc % 2]
                for half in range(2):
                    bk = PB[next_bank()]
                    for gg in range(2):
                        g = half * 2 + gg
                        mm_group(bk.v(slice(gg * GC, (gg + 1) * GC)),
                                 [(Wu2.v(k, slice(g * 128, (g + 1) * 128)), hTg.v(k)) for k in range(8)])
                    S.op("act", lambda e, bk=bk, half=half: e.copy(
                        out=uT.full[:, half * 2:half * 2 + 2, 16:16 + GC],
                        in_=bk.full.rearrange("p (g t) -> p g t", g=2)),
                        reads=[bk.v()], writes=[uT.v(slice(half * 2, half * 2 + 2))])

            def P2a(gc):
                E = "pool"
                tt(E, wA.v(slice(0, 4), slice(1, L_)), uT.v(slice(0, 4), slice(1, L_)), uT.v(slice(0, 4), slice(0, L_ - 1)), ALU.add)
                tt(E, wB.v(slice(1, 4), slice(3, L_)), wA.v(slice(1, 4), slice(3, L_)), wA.v(slice(1, 4), slice(1, L_ - 2)), ALU.add)
                tt(E, wA.v(slice(2, 4), slice(7, L_)), wB.v(slice(2, 4), slice(7, L_)), wB.v(slice(2, 4), slice(3, L_ - 4)), ALU.add)
                tt(E, wB.v(slice(3, 4), slice(15, L_)), wA.v(slice(3, 4), slice(15, L_)), wA.v(slice(3, 4), slice(7, L_ - 8)), ALU.add)

            def P2a2(gc):
                for g in range(4):
                    w = 2 << g
                    cur = wA if g % 2 == 0 else wB
                    stt(pTt.v(g), cur.v(g, slice(16, L_)), 1.0 / w, uT.v(g, slice(16, L_)), ALU.mult, ALU.subtract)
                    if gc == 0:
                        tt("dve", gA.v(slice(0, 16)), cur.v(g, slice(16, 32)), svc(SV_INVCNT + g * 16, 16), ALU.mult)
                        tt("dve", pTt.v(g, slice(0, 16)), gA.v(slice(0, 16)), uT.v(g, slice(16, 32)), ALU.subtract)
                S.op("dve", lambda e: e.tensor_copy(out=uT.full[:, :, 0:16], in_=uT.full[:, :, GC:GC + 16]),
                     reads=[uT.v()], writes=[uT.v()])

            def P2b(gc):
                yT = yT2[gc % 2]
                bk = PB[next_bank()]
                bk2 = PB[next_bank()]
                for g in range(4):
                    dst = (bk if g < 2 else bk2).v(slice((g % 2) * GC, (g % 2 + 1) * GC))
                    mm_group(dst, [(wpl.v(g), pTt.v(g))])
                for g in range(4):
                    dst = (bk if g < 2 else bk2).v(slice((g % 2) * GC, (g % 2 + 1) * GC))
                    ts("dve", yT.v(g), dst, svc(SV_PSC + l * 4 + g), ALU.mult)

            def M1(gc, mrange):
                hTg = hTg2[gc % 2]
                yT = yT2[gc % 2]
                gs = gsl(gc)
                for m in mrange:
                    ms = slice(m * 128, (m + 1) * 128)
                    b1 = PB[next_bank()]
                    b2 = PB[next_bank()]
                    ya = b1.v(slice(0, GC))
                    ga = b1.v(slice(GC, 2 * GC))
                    yb = b2.v(slice(0, GC))
                    gb = b2.v(slice(GC, 2 * GC))
                    mm_group(ya, [(ppl.v(g, ms), yT.v(g)) for g in range(4)])
                    mm_group(ga, [(Wga.v(k, ms), hTg.v(k)) for k in range(8)])
                    mm_group(yb, [(pat.v(j, ms), attn_o.v(j, gs)) for j in range(4)])
                    mm_group(gb, [(Wgb.v(k, ms), hTg.v(k)) for k in range(8)])
                    act(sgA.v(), ga, AF.Sigmoid)
                    act(sgB.v(), gb, AF.Sigmoid)
                    tt("dve", gA.v(), sgA.v(), ya, ALU.mult)
                    tt("dve", gB.v(), sgB.v(), yb, ALU.mult)
                    tt("dve", mrg.v(m), gA.v(), gB.v(), ALU.add)

            def M2(gc):
                gs = gsl(gc)
                for m in range(8):
                    ms = slice(m * 128, (m + 1) * 128)
                    bk = PB[next_bank()]
                    ov = bk.v(slice(0, GC))
                    mm_group(ov, [(wo.v(k, ms), mrg.v(k)) for k in range(8)])
                    stt(xT.v(m, gs), ov, modT.v(l, slice(16 + m, 17 + m)), xT.v(m, gs), ALU.mult, ALU.add)

            def FsA(gc):
                gs = gsl(gc)
                for k in range(8):
                    act(sqe.v(k), xT.v(k, gs), AF.Square)

            def FsB(gc):
                gs = gsl(gc)
                bk = PB[next_bank()]
                st = bk.v(slice(0, GC))
                mm_group(st, [(ones_b.v(), sqe.v(k)) for k in range(8)])
                rstd_from(st, rstd1_all.v(gs), D, sgA.v())

            P0(0)
            P1(0)
            P2a(0)
            P2a2(0)
            P2b(0)
            for gc in range(NG):
                M1(gc, range(0, 2))
                if gc + 1 < NG:
                    P0(gc + 1)
                if gc >= 1:
                    FsA(gc - 1)
                M1(gc, range(2, 4))
                if gc + 1 < NG:
                    P1(gc + 1)
                    P2a(gc + 1)
                M1(gc, range(4, 6))
                if gc >= 1:
                    FsB(gc - 1)
                M1(gc, range(6, 8))
                if gc + 1 < NG:
                    P2a2(gc + 1)
                M2(gc)
                if gc + 1 < NG:
                    P2b(gc + 1)
            FsA(NG - 1)
            FsB(NG - 1)

            h2 = sb(R_AO, [128, 8, NT], BF16)
            o = R_AO + 32768
            W1 = [sb(o + i * 16384, [128, 8, 512], BF16) for i in range(2)]
            W2 = [sb(o + 8192 + i * 16384, [128, 4, D], BF16) for i in range(2)]
            o += 32768
            hid = [sb(o + i * 4096, [128, 4, C], BF16) for i in range(2)]; o += 8192
            assert o <= R_X
            o = R_X + 32768
            sqf = sb(o, [128, 8, C], BF16); o += 8192
            fA = sb(o, [128, C], F32); o += 2048
            fB = sb(o, [128, C], F32); o += 2048
            frs = sb(o, [128, NT], F32); o += 8192
            rl = [sb(o + i * 2048, [128, C], F32) for i in range(2)]; o += 4096
            assert o <= ARENA
            w1v = wff1_d[l].rearrange("(k p) n -> p k n", p=128)
            w2v = wff2_d[l].rearrange("(j p) n -> p j n", p=128)

            def ffload(g):
                wload(W1[g % 2].v(), w1v[:, :, g * 512:(g + 1) * 512], f"W1_{g % 2}")
                wload(W2[g % 2].v(), w2v[:, g * 4:(g + 1) * 4, :], f"W2_{g % 2}")

            for c in range(NCH):
                cs = slice(c * C, (c + 1) * C)
                h_from(l, 1, lambda k: xT.v(k, cs), rstd1_all.v(cs), lambda k: h2.v(k, cs), fA.v(), fB.v())
            ffload(0)
            ri = [0]

            def ff_up(g, c):
                w1 = W1[g % 2]
                cs = slice(c * C, (c + 1) * C)
                hd = hid[(g * NCH + c) % 2]
                for hc in range(4):
                    bk = PB[next_bank()]
                    mm_group(bk.v(), [(w1.v(k, slice(hc * 128, (hc + 1) * 128)), h2.v(k, cs)) for k in range(8)])
                    r = rl[ri[0] % 2]
                    ri[0] += 1
                    act(r.v(), bk.v(), AF.Relu)
                    tt("dve", hd.v(hc), r.v(), r.v(), ALU.mult)

            def ff_down(g, c):
                w2 = W2[g % 2]
                cs = slice(c * C, (c + 1) * C)
                hd = hid[(g * NCH + c) % 2]
                for m in range(8):
                    bk = PB[next_bank()]
                    mm_group(bk.v(), [(w2.v(hc, slice(m * 128, (m + 1) * 128)), hd.v(hc)) for hc in range(4)])
                    stt(xT.v(m, cs), bk.v(), modT.v(l, slice(40 + m, 41 + m)), xT.v(m, cs), ALU.mult, ALU.add)

            def early_stats(c):
                cs = slice(c * C, (c + 1) * C)
                for k in range(8):
                    act(sqf.v(k), xT.v(k, cs), AF.Square)
                bk = PB[next_bank()]
                mm_group(bk.v(), [(ones_b.v(), sqf.v(k)) for k in range(8)])
                rstd_from(bk.v(), rstd1_all.v(cs), D, fA.v())

            seq = [(g, c) for g in range(FFG) for c in range(NCH)]
            loaded = {0}
            ff_up(*seq[0])
            for i, (g, c) in enumerate(seq):
                if c == 0 and g + 1 < FFG and (g + 1) not in loaded:
                    ffload(g + 1)
                    loaded.add(g + 1)
                if i + 1 < len(seq):
                    ff_up(*seq[i + 1])
                ff_down(g, c)
                if g == FFG - 1 and c >= 1:
                    early_stats(c - 1)
            early_stats(NCH - 1)

        o = R_AO
        osq = sb(o, [128, 8, C], BF16); o += 8192
        ors = sb(o, [128, C], F32); o += 2048
        otmp = sb(o, [128, C], F32); o += 2048
        obuf = [sb(o + i * 16384, [128, 8, C], F32) for i in range(2)]; o += 32768
        outv = out_d.rearrange("(k p) t -> p k t", p=128)
        out_tok = None
        out_toks = []
        for c in range(NCH):
            cs = slice(c * C, (c + 1) * C)
            ob_ = obuf[c % 2]
            for k in range(8):
                stt(ob_.v(k), xT.v(k, cs), svc(SV_FING + k), rstd1_all.v(cs), ALU.mult, ALU.mult)
            out_tok = S.dma("sp", [(lambda c=c, ob_=ob_: lambda e: e.dma_start(out=outv[:, :, c * C:(c + 1) * C], in_=ob_.v().ap))()],
                            f"outst{c}", reads=[ob_.v()], writes=[View(None, ("dram", "out"), (0, 1, c, c + 1))])
            out_toks.append(out_tok)
        for tk in out_toks:
            S.wait_tok("sp", tk)
        S.emit(block)
    return nc


_NC_CACHE = {}


def _inv_freq():
    return (np.float32(10000.0) ** (-np.arange(0, 32, 2, dtype=np.float32) / np.float32(32))).astype(np.float32)


def kernel(x, c, positions, ln1_g, ln2_g, w_ada, b_ada, w_in, q_norm_g, w_uq, kv_norm_g, w_uk, w_uv,
           w_pool, pool_scale, p_pool, p_attn, w_out, w_ff1, w_ff2, final_g):
    f32 = np.float32
    x = np.asarray(x, f32)
    c = np.asarray(c, f32)
    positions = np.asarray(positions, np.int32)
    B, S_, _ = x.shape
    dbg = _NC_CACHE.get("debug", False)
    if "nc" not in _NC_CACHE:
        _NC_CACHE["nc"] = build_nc(debug=dbg)
    nc = _NC_CACHE["nc"]

    def fm(v, n):
        return np.asarray(v, f32).reshape(n, 128).T

    tri = (np.arange(128)[None, :] >= np.arange(128)[:, None]).astype(ml_dtypes.bfloat16)
    invf = _inv_freq()
    shared = {
        "tri": tri,
        "w_ada": np.ascontiguousarray(np.asarray(w_ada, f32)),
        "w_in": np.ascontiguousarray(np.asarray(w_in, f32)),
        "w_uq": np.ascontiguousarray(np.asarray(w_uq, f32).reshape(DEPTH, 384, 768)),
        "w_uk": np.ascontiguousarray(np.asarray(w_uk, f32).reshape(DEPTH, 256, 512)),
        "w_uv": np.ascontiguousarray(np.asarray(w_uv, f32).reshape(DEPTH, 256, 512)),
        "w_pool": np.ascontiguousarray(np.asarray(w_pool, f32)),
        "p_pool": np.ascontiguousarray(np.asarray(p_pool, f32)),
        "p_attn": np.ascontiguousarray(np.asarray(p_attn, f32)),
        "w_out": np.ascontiguousarray(np.asarray(w_out, f32)),
        "w_ff1": np.ascontiguousarray(np.asarray(w_ff1, f32)),
        "w_ff2": np.ascontiguousarray(np.asarray(w_ff2, f32)),
    }
    in_maps = []
    for core in range(8):
        b, half = core // 2, core % 2
        t0 = half * NT
        sv = np.zeros((128, NV), f32)
        for l in range(DEPTH):
            sv[:, SV_LN1 + l * 8:SV_LN1 + l * 8 + 8] = fm(ln1_g[l], 8)
            sv[:, SV_LN2 + l * 8:SV_LN2 + l * 8 + 8] = fm(ln2_g[l], 8)
            sv[:, SV_BADA + l * 48:SV_BADA + l * 48 + 48] = fm(b_ada[l], 48)
            sv[:, SV_QNG + l * 3:SV_QNG + l * 3 + 3] = fm(q_norm_g[l], 3)
            sv[:, SV_KVG + l * 2:SV_KVG + l * 2 + 2] = fm(kv_norm_g[l], 2)
            sv[:, SV_PSC + l * 4:SV_PSC + l * 4 + 4] = fm(pool_scale[l], 4)
        sv[:, SV_FING:SV_FING + 8] = fm(final_g, 8)
        sv[64:96, SV_INVF] = np.tile(invf, 2)
        sv[:, SV_FLAG] = float(half)
        for g in range(4):
            w = 2 << g
            tpos = t0 + np.arange(16)
            cnt = np.minimum(tpos + 1, w).astype(f32)
            sv[:, SV_INVCNT + g * 16:SV_INVCNT + g * 16 + 16] = (f32(1.0) / cnt)[None, :]
        sv[:, SV_C:SV_C + 8] = fm(c[b], 8)
        m = dict(shared)
        m["xT"] = np.ascontiguousarray(x[b, t0:t0 + NT, :].T)
        m["pos"] = np.ascontiguousarray(positions[b, t0:t0 + NT].reshape(1, NT))
        m["smallv"] = sv
        in_maps.append(m)
    res = run_bass_kernel_spmd(nc, in_maps, core_ids=list(range(8)))
    _NC_CACHE["res"] = res.results if dbg else None
    out = np.empty((B, S_, D), f32)
    for core in range(8):
        b, half = core // 2, core % 2
        t0 = half * NT
        out[b, t0:t0 + NT, :] = np.asarray(res.results[core]["outT"], f32).T
    return out
```

```python
import math
import numpy as np
import ml_dtypes
import concourse.bass as bass
import concourse.mybir as mybir
from concourse.bass_utils import run_bass_kernel_spmd

F32 = mybir.dt.float32
BF16 = mybir.dt.bfloat16
I32 = mybir.dt.int32
U8 = mybir.dt.uint8
ALU = mybir.AluOpType
AF = mybir.ActivationFunctionType

D = 1024
NT = 2048
CTX = 4096
C = 512
NCH = NT // C
GC = 256
DEPTH = 2
NH = 8
EPS = 1e-6
SCALE = 1.0 / math.sqrt(96.0)
FFG = 8
ESZ = {F32: 4, BF16: 2, I32: 4, U8: 1}

SV_LN1, SV_LN2, SV_BADA, SV_QNG, SV_KVG, SV_PSC, SV_FING = 0, 16, 32, 128, 134, 138, 146
SV_INVF, SV_FLAG, SV_INVCNT, SV_C = 154, 155, 156, 220
NV = 228
ARENA = 207 * 1024


class View:
    __slots__ = ("ap", "space", "iv")

    def __init__(self, ap, space, iv):
        self.ap = ap
        self.space = space
        self.iv = iv


class T:
    def __init__(self, base_ap, space, off, shape, dt):
        self.shape = tuple(shape)
        self.dt = dt
        self.space = space
        self.off = off
        es = ESZ[dt]
        self.es = es
        free = self.shape[1:]
        n = int(np.prod(free))
        self.nbytes = n * es
        strides = []
        s = 1
        for d in reversed(free):
            strides.append(s)
            s *= d
        self.strides = list(reversed(strides))
        if space == "sb":
            ap = base_ap[:, off:off + n * es]
            if dt != U8:
                ap = ap.bitcast(dt)
        else:
            ap = base_ap[:, :]
        if len(free) == 2:
            ap = ap.rearrange("p (a b) -> p a b", a=free[0])
        elif len(free) == 3:
            ap = ap.rearrange("p (a b c) -> p a b c", a=free[0], b=free[1])
        self.full = ap

    def v(self, *idx, p=None):
        free = self.shape[1:]
        idx = list(idx) + [slice(None)] * (len(free) - len(idx))
        lo = 0
        hi = 0
        key = []
        for i, d, st in zip(idx, free, self.strides):
            if isinstance(i, int):
                a, b = i, i + 1
                key.append(i)
            else:
                a = 0 if i.start is None else i.start
                b = d if i.stop is None else i.stop
                key.append(slice(a, b))
            assert 0 <= a < b <= d, (idx, self.shape)
            lo += a * st
            hi += (b - 1) * st
        p0, p1 = (0, self.shape[0]) if p is None else p
        assert p1 <= self.shape[0]
        ap = self.full[(slice(p0, p1),) + tuple(key)]
        return View(ap, self.space, (p0, p1, self.off + lo * self.es, self.off + (hi + 1) * self.es))


class Sched:
    EPOCH = 12000

    def __init__(self, nc, es):
        self.nc = nc
        self.es = es
        self.eng = {"pe": nc.tensor, "act": nc.scalar, "dve": nc.vector, "pool": nc.gpsimd, "sp": nc.sync}
        self.streams = {e: [] for e in self.eng}
        self.cnt = {e: 0 for e in self.eng}
        self.psems = {e: [] for e in self.eng}
        self.waited = {e: {} for e in self.eng}
        self.recs = {}
        self.dsem = {}
        self.pbank = {}
        self.nsem = 0

    def new_sem(self, name):
        self.nsem += 1
        return self.es.enter_context(self.nc.semaphore(name))

    def _ptok(self, e):
        self.cnt[e] += 1
        n = self.cnt[e]
        ei = (n - 1) // self.EPOCH
        while len(self.psems[e]) <= ei:
            self.psems[e].append(self.new_sem(f"p_{e}_{len(self.psems[e])}"))
        return ("p", e, n)

    def _tok_wait(self, tok):
        if tok[0] == "p":
            _, e, n = tok
            ei = (n - 1) // self.EPOCH
            return self.psems[e][ei], n - ei * self.EPOCH
        _, key, val = tok
        return self.dsem[key][0], val

    @staticmethod
    def _ov(a, b):
        return a[0] < b[1] and b[0] < a[1] and a[2] < b[3] and b[2] < a[3]

    @staticmethod
    def _cov(a, b):
        return a[0] <= b[0] and a[1] >= b[1] and a[2] <= b[2] and a[3] >= b[3]

    def _deps(self, reads, writes, e=None):
        deps = []
        for v in reads:
            if v.space[0] == "ps":
                st = self.pbank.get(v.space)
                if st is not None and st[0] != e:
                    deps.append(st[1])
                continue
            for r in self.recs.get(v.space, ()):
                if r[1] == "W" and self._ov(r[0], v.iv):
                    deps.append(r[2])
        for v in writes:
            if v.space[0] == "ps":
                st = self.pbank.get(v.space)
                if st is not None and st[0] != e:
                    deps.append(st[1])
                continue
            for r in self.recs.get(v.space, ()):
                if self._ov(r[0], v.iv):
                    deps.append(r[2])
        return deps

    def _record(self, reads, writes, tok):
        ps = [v for v in list(reads) + list(writes) if v.space[0] == "ps"]
        for v in ps:
            self.pbank[v.space] = (tok[1], tok)
        reads = [v for v in reads if v.space[0] != "ps"]
        writes = [v for v in writes if v.space[0] != "ps"]
        for v in writes:
            lst = self.recs.setdefault(v.space, [])
            lst[:] = [r for r in lst if not self._cov(v.iv, r[0])]
            lst.append([v.iv, "W", tok])
        for v in reads:
            lst = self.recs.setdefault(v.space, [])
            lst[:] = [r for r in lst if not (r[1] == "R" and r[2][0] == "p" and tok[0] == "p"
                                             and r[2][1] == tok[1] and self._cov(v.iv, r[0]))]
            lst.append([v.iv, "R", tok])

    def _waits(self, e, deps):
        best = {}
        for tok in deps:
            if tok[0] == "p":
                if tok[1] == e and e in ("pe", "sp"):
                    continue
                k = ("p", tok[1])
                val = tok[2]
            else:
                k = ("d", tok[1])
                val = tok[2]
            if self.waited[e].get(k, 0) >= val:
                continue
            if best.get(k, (0, None))[0] < val:
                best[k] = (val, tok)
        out = []
        for k, (val, tok) in best.items():
            self.waited[e][k] = val
            out.append(self._tok_wait(tok))
        return out

    def op(self, e, fn, reads=(), writes=(), sig=True):
        deps = self._deps(reads, writes, e)
        waits = self._waits(e, deps)
        if sig:
            tok = self._ptok(e)
            sem, val = self._tok_wait(tok)
            self._record(reads, writes, tok)
        else:
            tok = None
            sem = None
        self.streams[e].append((waits, fn, sem, False))
        return tok

    def dma(self, q, fns, key, reads=(), writes=()):
        deps = self._deps(reads, writes)
        waits = self._waits(q, deps)
        if key not in self.dsem:
            self.dsem[key] = [self.new_sem("d_" + key), 0]
        ds = self.dsem[key]
        ds[1] += 16 * len(fns)
        tok = ("d", key, ds[1])
        self._record(reads, writes, tok)
        first = True
        for fn in fns:
            self.streams[q].append((waits if first else [], fn, ds[0], True))
            first = False
        return tok

    def custom(self, e, fn, reads=(), writes=(), key=None, inc=1):
        deps = self._deps(reads, writes)
        waits = self._waits(e, deps)
        if key not in self.dsem:
            self.dsem[key] = [self.new_sem("c_" + key), 0]
        ds = self.dsem[key]
        ds[1] += inc
        tok = ("d", key, ds[1])
        self._record(reads, writes, tok)
        self.streams[e].append((waits, fn, ds[0], "cc"))
        return tok

    def wait_tok(self, e, tok):
        ws = self._waits(e, [tok])
        if ws:
            self.streams[e].append((ws, None, None, False))

    def emit(self, block):
        def run(e, eng):
            for waits, fn, sem, kind in self.streams[e]:
                for s, v in waits:
                    eng.wait_ge(s, v)
                if fn is None:
                    continue
                inst = fn(eng)
                if sem is not None:
                    if kind is True:
                        inst.then_inc(sem, 16)
                    elif kind == "cc":
                        inst.then_inc(sem)
                    else:
                        inst.then_inc(sem, 1)

        @block.tensor
        def _(eng):
            run("pe", eng)

        @block.scalar
        def _(eng):
            run("act", eng)

        @block.vector
        def _(eng):
            run("dve", eng)

        @block.gpsimd
        def _(eng):
            run("pool", eng)

        @block.sync
        def _(eng):
            run("sp", eng)


def build_nc(debug=False):
    from contextlib import ExitStack
    nc = bass.Bass("TRN2", target_bir_lowering=False)
    dr = {}

    def din(name, shape, dt):
        dr[name] = nc.dram_tensor(name, list(shape), dt, kind="ExternalInput").ap()
        return dr[name]

    xT_d = din("xT", [D, NT], F32)
    pos_d = din("pos", [1, NT], I32)
    sv_d = din("smallv", [128, NV], F32)
    tri_d = din("tri", [128, 128], BF16)
    wada_d = din("w_ada", [DEPTH, D, 6 * D], F32)
    win_d = din("w_in", [DEPTH, D, 3232], F32)
    wuq_d = din("w_uq", [DEPTH, 384, 768], F32)
    wuk_d = din("w_uk", [DEPTH, 256, 512], F32)
    wuv_d = din("w_uv", [DEPTH, 256, 512], F32)
    wpool_d = din("w_pool", [DEPTH, 4, 128, 128], F32)
    ppool_d = din("p_pool", [DEPTH, 512, D], F32)
    pattn_d = din("p_attn", [DEPTH, 512, D], F32)
    wout_d = din("w_out", [DEPTH, D, D], F32)
    wff1_d = din("w_ff1", [DEPTH, D, 4 * D], F32)
    wff2_d = din("w_ff2", [DEPTH, 4 * D, D], F32)
    out_d = nc.dram_tensor("outT", [D, NT], F32, kind="ExternalOutput").ap()

    exs = [nc.dram_tensor(f"exs{l}", [288, NT], BF16) for l in range(DEPTH)]
    exd = [nc.dram_tensor(f"exd{l}", [576, NT], BF16) for l in range(DEPTH)]
    hls = [nc.dram_tensor(f"hls{l}", [128, 64], F32) for l in range(DEPTH)]
    hld = [nc.dram_tensor(f"hld{l}", [256, 64], F32) for l in range(DEPTH)]
    tabs = nc.dram_tensor("tabs", [256, NT], F32)
    if debug:
        dbg_q = nc.dram_tensor("dbg_q", [128, NH * NT], BF16, kind="ExternalOutput").ap()
        dbg_ckv = nc.dram_tensor("dbg_ckv", [128, 2 * CTX], BF16, kind="ExternalOutput").ap()
        dbg_kr = nc.dram_tensor("dbg_kr", [128, CTX], BF16, kind="ExternalOutput").ap()
        dbg_ao = nc.dram_tensor("dbg_ao", [128, 4 * NT], BF16, kind="ExternalOutput").ap()
        dbg_cos = nc.dram_tensor("dbg_cos", [128, 2 * NT], F32, kind="ExternalOutput").ap()
        dbg_pat = nc.dram_tensor("dbg_pat", [128, 4 * D], BF16, kind="ExternalOutput").ap()
        dbg_mrg = nc.dram_tensor("dbg_mrg", [128, 8 * GC], BF16, kind="ExternalOutput").ap()

    with ExitStack() as es:
        arena = es.enter_context(nc.sbuf_tensor("arena", [128, ARENA], U8))
        banks = [es.enter_context(nc.psum_tensor(f"bank{i}", [128, 512], F32)) for i in range(8)]
        S = Sched(nc, es)
        block = es.enter_context(nc.Block())

        def sb(off, shape, dt):
            t = T(arena, "sb", off, shape, dt)
            assert off + t.nbytes <= ARENA, (off, shape)
            return t

        PB = [T(banks[i], ("ps", i), 0, [128, 512], F32) for i in range(8)]

        def dview(handle, name):
            return View(None, ("dram", name), (0, 1, 0, 1))

        xT = sb(0, [128, 8, NT], F32)
        o = 65536
        smallv = sb(o, [128, NV], F32); o += NV * 4
        modT = sb(o, [128, DEPTH, 48], F32); o += DEPTH * 48 * 4
        avec = sb(o, [128, DEPTH, 2, 8], F32); o += DEPTH * 16 * 4
        cact = sb(o, [128, 8], BF16); o += 16
        ones_b = sb(o, [128, 128], BF16); o += 256
        tri = sb(o, [128, 128], BF16); o += 256
        epsT = sb(o, [128, 1], F32); o += 4
        zeroT = sb(o, [128, 1], F32); o += 4
        ones_f = sb(o, [128, 128], F32); o += 512
        sel_f = sb(o, [128, 128], F32); o += 512
        flag16 = sb(o, [128, 16], BF16); o += 32
        halo_sb = sb(o, [128, 4, 16], F32); o += 256
        halo_in = sb(o, [128, 4, 16], F32); o += 256
        assert o <= 69632, o
        P0 = 69632
        R_AO = P0
        R_CTX = R_AO + 16384
        R_Q = R_CTX + 24576
        R_X = R_Q + 32768
        assert R_X == 143360

        cosT = sb(R_AO, [128, NT], F32)
        sinT = sb(R_AO + 8192, [128, NT], F32)
        attn_o = sb(R_AO, [128, 4, NT], BF16)
        ctx_ckv = sb(R_CTX, [128, 2, CTX], BF16)
        ctx_kr = sb(R_CTX + 16384, [128, CTX], BF16)
        qT_all = sb(R_Q, [128, NH, NT], BF16)

        def svc(col, n=1):
            return smallv.v(slice(col, col + n))

        bank_rr = [0]

        def next_bank(pool=(0, 1, 2, 3, 4, 5, 6, 7)):
            b = pool[bank_rr[0] % len(pool)]
            bank_rr[0] += 1
            return b

        def mm_group(out, pairs, extra_reads=()):
            n = len(pairs)
            allreads = [v for pr in pairs for v in pr] + list(extra_reads)
            for i, (l, r) in enumerate(pairs):
                last = i == n - 1
                fn = (lambda l=l, r=r, i=i, last=last: lambda e: e.matmul(out.ap, l.ap, r.ap, start=(i == 0), stop=last))()
                if last:
                    S.op("pe", fn, reads=allreads, writes=[out])
                else:
                    if i == 0:
                        S.op("pe", fn, reads=allreads, writes=[out], sig=False)
                    else:
                        S.op("pe", fn, sig=False)

        def act(out, in_, func, scale=1.0, bias=None, extra_reads=()):
            rd = [in_] + list(extra_reads)
            kw = {}
            if bias is not None:
                kw["bias"] = bias.ap
                rd.append(bias)
            if isinstance(scale, View):
                rd.append(scale)
                sc = scale.ap
            else:
                sc = scale
            S.op("act", lambda e: e.activation(out=out.ap, in_=in_.ap, func=func, scale=sc, **kw),
                 reads=rd, writes=[out])

        def tt(eng, out, a, b, op):
            S.op(eng, lambda e: e.tensor_tensor(out=out.ap, in0=a.ap, in1=b.ap, op=op), reads=[a, b], writes=[out])

        def ts(eng, out, a, s1, op0, s2=None, op1=None):
            rd = [a]
            s1v = s1.ap if isinstance(s1, View) else s1
            s2v = s2.ap if isinstance(s2, View) else s2
            if isinstance(s1, View):
                rd.append(s1)
            if isinstance(s2, View):
                rd.append(s2)
            if op1 is None:
                S.op(eng, lambda e: e.tensor_scalar(out=out.ap, in0=a.ap, scalar1=s1v, scalar2=None, op0=op0),
                     reads=rd, writes=[out])
            else:
                S.op(eng, lambda e: e.tensor_scalar(out=out.ap, in0=a.ap, scalar1=s1v, scalar2=s2v, op0=op0, op1=op1),
                     reads=rd, writes=[out])

        def stt(out, a, s, b, op0, op1):
            rd = [a, b]
            sv = s.ap if isinstance(s, View) else s
            if isinstance(s, View):
                rd.append(s)
            S.op("dve", lambda e: e.scalar_tensor_tensor(out=out.ap, in0=a.ap, scalar=sv, in1=b.ap, op0=op0, op1=op1),
                 reads=rd, writes=[out])

        def copy(eng, out, in_):
            if eng == "act":
                S.op("act", lambda e: e.copy(out=out.ap, in_=in_.ap), reads=[in_], writes=[out])
            else:
                S.op(eng, lambda e: e.tensor_copy(out=out.ap, in_=in_.ap), reads=[in_], writes=[out])

        def memset(eng, out, val):
            S.op(eng, lambda e: e.memset(out.ap, val), writes=[out])

        def wload(dst, src_ap, key):
            S.dma("pool", [lambda e: e.dma_start(out=dst.ap, in_=src_ap)], key, writes=[dst])

        def rstd_from(stat_ps, out, n, tmp):
            act(tmp, stat_ps, AF.Ln, scale=1.0 / n, bias=epsT.v())
            act(out, tmp, AF.Exp, scale=-0.5)

        S.dma("sp", [lambda e: e.dma_start(out=smallv.v().ap, in_=sv_d)], "smallv", writes=[smallv.v()])
        S.dma("sp", [lambda e: e.dma_start(out=tri.v().ap, in_=tri_d)], "tri", writes=[tri.v()])
        posi = sb(R_CTX, [128, NT], I32)
        S.dma("sp", [lambda e: e.dma_start(out=posi.v().ap, in_=pos_d.partition_broadcast(128)[:, 0, :])],
              "pos", writes=[posi.v()])
        xTv = xT_d.rearrange("(k p) t -> p k t", p=128)
        for c in range(NCH):
            dst = xT.v(slice(None), slice(c * C, (c + 1) * C))
            S.dma("sp", [(lambda c=c, dst=dst: lambda e: e.dma_start(out=dst.ap, in_=xTv[:, :, c * C:(c + 1) * C]))()],
                  f"x{c}", writes=[dst])

        Wkr0 = sb(R_X + 4096, [128, 8, 96], BF16)
        Wkrr0 = sb(R_X + 4096 + 1536, [128, 8, 96], BF16)
        memset("dve", Wkr0.v(), 0.0)
        memset("dve", Wkrr0.v(), 0.0)
        memset("dve", ones_b.v(), 1.0)
        memset("dve", epsT.v(), EPS)
        memset("dve", zeroT.v(), 0.0)
        memset("dve", ones_f.v(), 1.0)
        memset("dve", sel_f.v(), 0.0)
        memset("dve", sel_f.v(slice(64, 128)), 1.0)
        memset("dve", flag16.v(), 1.0)
        ts("dve", flag16.v(), flag16.v(), svc(SV_FLAG), ALU.mult)

        act(cact.v(), svc(SV_C, 8), AF.Silu)

        PI = math.pi
        ang = sb(R_Q, [128, NT], F32)
        t1 = sb(R_Q + 8192, [128, NT], F32)
        t2 = sb(R_Q + 16384, [128, NT], F32)
        ki = sb(R_Q + 24576, [128, NT], I32)
        copy("dve", ang.v(), posi.v())
        ts("dve", ang.v(), ang.v(), svc(SV_INVF), ALU.mult)
        ts("dve", t1.v(), ang.v(), 1.0 / (2 * PI), ALU.mult)
        copy("dve", ki.v(), t1.v())
        copy("dve", t1.v(), ki.v())
        C1 = 6.28125
        C2 = 2 * PI - C1
        stt(t2.v(), t1.v(), -C1, ang.v(), ALU.mult, ALU.add)
        stt(t2.v(), t1.v(), -C2, t2.v(), ALU.mult, ALU.add)

        def wrap(r, tmp):
            ts("dve", tmp, r, PI, ALU.is_gt)
            stt(r, tmp, -2 * PI, r, ALU.mult, ALU.add)
            ts("dve", tmp, r, -PI, ALU.is_lt)
            stt(r, tmp, 2 * PI, r, ALU.mult, ALU.add)

        wrap(t2.v(), t1.v())
        act(sinT.v(), t2.v(), AF.Sin)
        ts("dve", t2.v(), t2.v(), PI / 2, ALU.add)
        wrap(t2.v(), t1.v())
        act(cosT.v(), t2.v(), AF.Sin)
        tabs_v = dview(tabs, "tabs")
        if debug:
            S.dma("sp", [lambda e: e.dma_start(out=dbg_cos[:, 0:NT], in_=cosT.v().ap),
                         lambda e: e.dma_start(out=dbg_cos[:, NT:2 * NT], in_=sinT.v().ap)],
                  "dbg0", reads=[cosT.v(), sinT.v()], writes=[View(None, ("dram", "dbg0"), (0, 1, 0, 1))])
        S.dma("sp", [lambda e: e.dma_start(out=tabs.ap()[0:128, :], in_=cosT.v().ap),
                     lambda e: e.dma_start(out=tabs.ap()[128:256, :], in_=sinT.v().ap)],
              "tabs_st", reads=[cosT.v(), sinT.v()], writes=[tabs_v])

        wada_b = [sb(R_CTX + 8192, [128, 8, D], BF16), sb(R_X + 30720, [128, 8, D], BF16)]
        for j in range(2):
            wb = wada_b[j]
            src = wada_d[0].rearrange("(k p) n -> p k n", p=128)[:, :, j * D:(j + 1) * D]
            wload(wb.v(), src, f"wada{j}")
            bk = PB[next_bank()]
            for m in range(8):
                mm_group(bk.v(slice(m, m + 1)),
                         [(wb.v(k, slice(m * 128, (m + 1) * 128)), cact.v(slice(k, k + 1))) for k in range(8)])
            tt("dve", modT.v(0, slice(j * 8, (j + 1) * 8)), bk.v(slice(0, 8)), svc(SV_BADA + j * 8, 8), ALU.add)
        stt(avec.v(0, 0), modT.v(0, slice(8, 16)), 1.0, svc(SV_LN1, 8), ALU.add, ALU.mult)
        wpc = sb(R_X + 51200, [128, 8, 512], BF16)
        mod_pieces = [(0, q) for q in range(4, 12)] + [(1, q) for q in range(12)]

        def mod_piece_step(lq, banks):
            ll, q = lq

            def f():
                src = wada_d[ll].rearrange("(k p) n -> p k n", p=128)[:, :, q * 512:(q + 1) * 512]
                wload(wpc.v(), src, "wpc")
                bk = PB[next_bank(banks)]
                for m in range(4):
                    mm_group(bk.v(slice(m, m + 1)),
                             [(wpc.v(k, slice(m * 128, (m + 1) * 128)), cact.v(slice(k, k + 1))) for k in range(8)])
                tt("dve", modT.v(ll, slice(q * 4, (q + 1) * 4)), bk.v(slice(0, 4)),
                   svc(SV_BADA + ll * 48 + q * 4, 4), ALU.add)
                if q == 3:
                    stt(avec.v(ll, 0), modT.v(ll, slice(8, 16)), 1.0, svc(SV_LN1 + ll * 8, 8), ALU.add, ALU.mult)
                if q == 9:
                    stt(avec.v(ll, 1), modT.v(ll, slice(32, 40)), 1.0, svc(SV_LN2 + ll * 8, 8), ALU.add, ALU.mult)
            return f

        rstd1_all = sb(ARENA - 8192, [128, NT], F32)
        sq0 = sb(R_X + 49152, [128, 8, C], BF16)
        ln0 = sb(R_X + 57344, [128, C], F32)
        for c in range(NCH):
            cs = slice(c * C, (c + 1) * C)
            for k in range(8):
                act(sq0.v(k), xT.v(k, cs), AF.Square)
            bk = PB[next_bank()]
            mm_group(bk.v(), [(ones_b.v(), sq0.v(k)) for k in range(8)])
            act(ln0.v(), bk.v(), AF.Ln, scale=1.0 / D, bias=epsT.v())
            act(rstd1_all.v(cs), ln0.v(), AF.Exp, scale=-0.5)
        def stats_all(rstd_all, sq, lntmp):
            for c in range(NCH):
                cs = slice(c * C, (c + 1) * C)
                for k in range(8):
                    act(sq.v(k), xT.v(k, cs), AF.Square)
                bk = PB[next_bank()]
                mm_group(bk.v(), [(ones_b.v(), sq.v(k)) for k in range(8)])
                rstd_from(bk.v(), rstd_all.v(cs), D, lntmp)

        def h_from(l, which, xv_k, rstd, hv_k, tmpA, tmpB):
            boff = 0 if which == 0 else 24
            for k in range(8):
                tmp = tmpA if k % 2 == 0 else tmpB
                tt("dve", tmp, xv_k(k), rstd, ALU.mult)
                act(hv_k(k), tmp, AF.Identity, scale=avec.v(l, which, slice(k, k + 1)),
                    bias=modT.v(l, slice(boff + k, boff + k + 1)))

        for l in range(DEPTH):
            o = R_X
            Wkv = sb(o, [128, 8, 256], BF16); o += 4096
            Wkr = sb(o, [128, 8, 96], BF16); o += 1536
            Wkrr = sb(o, [128, 8, 96], BF16); o += 1536
            Wq = sb(o, [128, 8, 384], BF16); o += 6144
            wuq = sb(o, [128, 3, 768], BF16); o += 4608
            wuqr = sb(o, [128, 3, 768], BF16); o += 4608
            Wu = sb(o, [128, 8, 512], BF16); o += 8192
            hT = sb(o, [128, 8, C], BF16); o += 8192
            sq = sb(o, [128, 8, C], BF16); o += 8192
            tmpA = sb(o, [128, C], F32); o += 2048
            tmpB = sb(o, [128, C], F32); o += 2048
            rstdq = sb(o, [128, C], F32); o += 2048
            rstdkv = rstdq
            cqn = sb(o, [128, 3, C], BF16); o += 3072
            sql = sb(o, [128, 3, C], BF16); o += 3072
            assert o <= ARENA - 8192, o
            hT2 = [hT, sq]
            rstd1_all = sb(ARENA - 8192, [128, NT], F32)

            winv = win_d[l].rearrange("(k p) n -> p k n", p=128)
            wload(Wkv.v(), winv[:, :, 896:1152], "Wkv")
            if l == 0:
                wload(Wkr.v(slice(None), slice(64, 96)), winv[:, :, 1152:1184], "Wkr")
            wload(Wq.v(), winv[:, :, 512:896], "Wq")
            wload(wuq.v(), wuq_d[l].rearrange("(j p) n -> p j n", p=128), "wuq")
            wload(Wu.v(), winv[:, :, 0:512], "Wu")
            if l > 0:
                memset("dve", Wkr.v(), 0.0)
                memset("dve", Wkrr.v(), 0.0)
                wload(Wkr.v(slice(None), slice(64, 96)), winv[:, :, 1152:1184], "Wkr")
            ts("dve", Wkrr.v(slice(None), slice(64, 80)), Wkr.v(slice(None), slice(80, 96)), -1.0, ALU.mult)
            copy("dve", Wkrr.v(slice(None), slice(80, 96)), Wkr.v(slice(None), slice(64, 80)))
            wuq4 = wuq.full.rearrange("p j (h d) -> p j h d", h=NH)
            wuqr4 = wuqr.full.rearrange("p j (h d) -> p j h d", h=NH)
            memset("dve", wuqr.v(), 0.0)
            S.op("dve", lambda e: e.tensor_scalar(out=wuqr4[:, :, :, 64:80], in0=wuq4[:, :, :, 80:96], scalar1=-1.0,
                                                    scalar2=None, op0=ALU.mult),
                 reads=[wuq.v()], writes=[wuqr.v()])
            S.op("dve", lambda e: e.tensor_copy(out=wuqr4[:, :, :, 80:96], in_=wuq4[:, :, :, 64:80]),
                 reads=[wuq.v()], writes=[wuqr.v()])
            if l > 0:
                S.dma("sp", [lambda e: e.dma_start(out=cosT.v().ap, in_=tabs.ap()[0:128, :]),
                             lambda e: e.dma_start(out=sinT.v().ap, in_=tabs.ap()[128:256, :])],
                      "tabs_ld", reads=[tabs_v], writes=[cosT.v(), sinT.v()])

            for c in range(NCH):
                cs = slice(c * C, (c + 1) * C)
                ls = slice(NT + c * C, NT + (c + 1) * C)
                hT = hT2[c % 2]
                if c == 0:
                    h_from(l, 0, lambda k: xT.v(k, cs), rstd1_all.v(cs), lambda k: hT.v(k), tmpA.v(), tmpB.v())
                bkv = [PB[next_bank()] for _ in range(2)]
                for j in range(2):
                    mm_group(bkv[j].v(), [(Wkv.v(k, slice(j * 128, (j + 1) * 128)), hT.v(k)) for k in range(8)])
                    act(cqn.v(j), bkv[j].v(), AF.Square)
                bq = [PB[next_bank()] for _ in range(3)]
                for j in range(3):
                    mm_group(bq[j].v(), [(Wq.v(k, slice(j * 128, (j + 1) * 128)), hT.v(k)) for k in range(8)])
                    act(sql.v(j), bq[j].v(), AF.Square)
                bs = PB[next_bank()]
                mm_group(bs.v(), [(ones_b.v(), cqn.v(j)) for j in range(2)])
                rstd_from(bs.v(), rstdkv.v(), 256, tmpA.v())
                for j in range(2):
                    stt(ctx_ckv.v(j, ls), bkv[j].v(), svc(SV_KVG + l * 2 + j), rstdkv.v(), ALU.mult, ALU.mult)
                bkr = PB[next_bank()]
                bkrr = PB[next_bank()]
                mm_group(bkr.v(p=(0, 96)), [(Wkr.v(k), hT.v(k)) for k in range(8)])
                mm_group(bkrr.v(p=(0, 96)), [(Wkrr.v(k), hT.v(k)) for k in range(8)])
                R = (64, 96)
                tt("dve", tmpA.v(p=R), bkr.v(p=R), cosT.v(cs, p=R), ALU.mult)
                tt("dve", tmpB.v(p=R), bkrr.v(p=R), sinT.v(cs, p=R), ALU.mult)
                tt("dve", ctx_kr.v(ls, p=R), tmpA.v(p=R), tmpB.v(p=R), ALU.add)
                bs = PB[next_bank()]
                mm_group(bs.v(), [(ones_b.v(), sql.v(j)) for j in range(3)])
                rstd_from(bs.v(), rstdq.v(), 384, tmpA.v())
                ts("dve", rstdq.v(), rstdq.v(), SCALE, ALU.mult)
                for j in range(3):
                    stt(cqn.v(j), bq[j].v(), svc(SV_QNG + l * 3 + j), rstdq.v(), ALU.mult, ALU.mult)
                Q = (0, 96)
                if c == NCH - 1:
                    bh = PB[next_bank()]
                    for g in range(4):
                        mm_group(bh.v(slice(g * 16, (g + 1) * 16)),
                                 [(Wu.v(k, slice(g * 128, (g + 1) * 128)), hT.v(k, slice(C - 16, C))) for k in range(8)])
                    S.op("dve", lambda e, bh=bh: e.tensor_copy(
                        out=halo_sb.full, in_=bh.full[:, 0:64].rearrange("p (g t) -> p g t", g=4)),
                        reads=[bh.v(slice(0, 64))], writes=[halo_sb.v()])

                if c + 1 < NCH:
                    cs2 = slice((c + 1) * C, (c + 2) * C)
                    hTn = hT2[(c + 1) % 2]
                    h_from(l, 0, lambda k: xT.v(k, cs2), rstd1_all.v(cs2), lambda k: hTn.v(k), tmpA.v(), tmpB.v())
                for h in range(NH):
                    ba = PB[next_bank()]
                    bb = PB[next_bank()]
                    hs = slice(h * 96, (h + 1) * 96)
                    mm_group(ba.v(p=Q), [(wuq.v(j, hs), cqn.v(j)) for j in range(3)])
                    mm_group(bb.v(p=Q), [(wuqr.v(j, hs), cqn.v(j)) for j in range(3)])
                    tt("dve", tmpA.v(p=Q), ba.v(p=Q), cosT.v(cs, p=Q), ALU.mult)
                    tt("dve", tmpB.v(p=Q), bb.v(p=Q), sinT.v(cs, p=Q), ALU.mult)
                    tt("dve", qT_all.v(h, cs, p=Q), tmpA.v(p=Q), tmpB.v(p=Q), ALU.add)
            wuk = sb(R_X, [128, 2, 512], BF16)
            wuv = sb(R_X + 2048, [128, 2, 512], BF16)
            wload(wuk.v(), wuk_d[l].rearrange("(j p) n -> p j n", p=128), "wuk")
            wload(wuv.v(), wuv_d[l].rearrange("(j p) n -> p j n", p=128), "wuv")
            exs_v = dview(exs[l], f"exs{l}")
            exd_v = dview(exd[l], f"exd{l}")
            hls_v = dview(hls[l], f"hls{l}")
            hld_v = dview(hld[l], f"hld{l}")
            own = slice(NT, CTX)
            S.dma("sp", [lambda e, l=l: e.dma_start(out=exs[l].ap()[0:128, :], in_=ctx_ckv.v(0, own).ap),
                         lambda e, l=l: e.dma_start(out=exs[l].ap()[128:256, :], in_=ctx_ckv.v(1, own).ap),
                         lambda e, l=l: e.dma_start(out=exs[l].ap()[256:288, :], in_=ctx_kr.v(own, p=(64, 96)).ap)],
                  f"exst{l}", reads=[ctx_ckv.v(0, own), ctx_ckv.v(1, own), ctx_kr.v(own, p=(64, 96))],
                  writes=[exs_v])
            S.dma("sp", [lambda e, l=l: e.dma_start(out=hls[l].ap(), in_=halo_sb.full.rearrange("p g t -> p (g t)"))],
                  f"hlst{l}", reads=[halo_sb.v()], writes=[hls_v])
            RG = [[0, 1], [2, 3], [4, 5], [6, 7]]
            cc_tok = S.custom("pool", lambda e, l=l: e.collective_compute(
                "AllGather", ALU.bypass, replica_groups=RG, ins=[exs[l].ap().opt()], outs=[exd[l].ap().opt()]),
                reads=[exs_v], writes=[exd_v], key=f"cc{l}")
            S.wait_tok("pool", cc_tok)
            S.custom("pool", lambda e, l=l: e.collective_compute(
                "AllGather", ALU.bypass, replica_groups=RG, ins=[hls[l].ap().opt()], outs=[hld[l].ap().opt()]),
                reads=[hls_v], writes=[hld_v], key=f"cch{l}")
            rem = slice(0, NT)
            for j in range(2):
                S.dma("sp", [(lambda l=l, j=j: lambda e: e.dma_start(out=ctx_ckv.v(j, rem).ap,
                                                                      in_=exd[l].ap()[j * 128:(j + 1) * 128, :]))()],
                      f"exld{l}_{j}", reads=[exd_v], writes=[ctx_ckv.v(j, rem)])
            S.dma("sp", [lambda e, l=l: e.dma_start(out=ctx_kr.v(rem, p=(64, 96)).ap, in_=exd[l].ap()[256:288, :])],
                  f"exldk{l}", reads=[exd_v], writes=[ctx_kr.v(rem, p=(64, 96))])
            S.dma("sp", [lambda e, l=l: e.dma_start(out=halo_in.full.rearrange("p g t -> p (g t)"), in_=hld[l].ap()[0:128, :])],
                  f"exldh{l}", reads=[hld_v], writes=[halo_in.v()])

            if debug and l == 0:
                S.dma("sp", [lambda e: e.dma_start(out=dbg_q, in_=qT_all.full.rearrange("p h t -> p (h t)")),
                             lambda e: e.dma_start(out=dbg_ckv, in_=ctx_ckv.full.rearrange("p j t -> p (j t)")),
                             lambda e: e.dma_start(out=dbg_kr, in_=ctx_kr.full)],
                      "dbg1", reads=[qT_all.v(), ctx_ckv.v(), ctx_kr.v()], writes=[View(None, ("dram", "dbg1"), (0, 1, 0, 1))])
            o = R_X
            o += 4096
            KT = [sb(o + i * 8192, [128, CTX], BF16) for i in range(2)]; o += 16384
            VV = [sb(o + i * 8192, [128, 32, 128], BF16) for i in range(2)]; o += 16384
            NPT = 6
            LA = 4
            PT = [sb(o + i * 1024, [128, C], BF16) for i in range(NPT)]; o += NPT * 1024
            Osb = [sb(o + i * 2048, [128, C], F32) for i in range(2)]; o += 4096
            rdn = [sb(o + i * 2048, [128, C], F32) for i in range(2)]; o += 4096
            assert o <= ARENA
            o = R_CTX
            Wu2 = sb(o, [128, 8, 512], BF16); o += 8192
            Wga = sb(o, [128, 8, D], BF16); o += 16384
            Wgb = sb(o, [128, 8, D], BF16); o += 16384
            wpl = sb(o, [128, 4, 128], BF16); o += 1024
            ppl = sb(o, [128, 4, D], BF16); o += 8192
            pat = sb(o, [128, 4, D], BF16); o += 8192
            wo = sb(o, [128, 8, D], BF16); o += 16384
            G_o = o
            S_BANKS = (0, 1, 2, 3, 4, 7)

            def build_steps(h):
                lower = h < 4
                kt_t = KT[h % 2]
                v_t = VV[h % 2]
                voff = 0 if lower else 64
                vcol = 64 if lower else 0

                def kstep(cc):
                    def f():
                        bk = PB[next_bank(S_BANKS)]
                        ks = slice(cc * C, (cc + 1) * C)
                        mm_group(bk.v(p=(0, 64)),
                                 [(wuk.v(j, slice(h * 64, (h + 1) * 64)), ctx_ckv.v(j, ks)) for j in range(2)])
                        copy("dve", kt_t.v(ks, p=(0, 64)), bk.v(p=(0, 64)))
                    return f

                def krope(part):
                    def f():
                        copy("dve", kt_t.v(part, p=(64, 96)), ctx_kr.v(part, p=(64, 96)))
                    return f

                def vinit():
                    memset("dve", v_t.v(), 0.0)
                    S.op("dve", lambda e: e.tensor_copy(out=v_t.full[:, 0:16, vcol], in_=flag16.full),
                         reads=[flag16.v()], writes=[v_t.v(slice(0, 16))])
                    S.op("dve", lambda e: e.memset(v_t.full[:, 16:32, vcol], 1.0), writes=[v_t.v(slice(16, 32))])

                def vstep(kb):
                    def f():
                        bk = PB[next_bank(S_BANKS)]
                        for i in range(8):
                            kt = kb * 8 + i
                            mm_group(bk.v(slice(i * 64, (i + 1) * 64)),
                                     [(ctx_ckv.v(j, slice(kt * 128, (kt + 1) * 128)), wuv.v(j, slice(h * 64, (h + 1) * 64)))
                                      for j in range(2)])
                        src = bk.full.rearrange("p (i d) -> p i d", i=8)
                        dstap = v_t.full[:, kb * 8:(kb + 1) * 8, voff:voff + 64]
                        if kb < 2:
                            S.op("dve", lambda e: e.tensor_scalar(out=dstap, in0=src, scalar1=svc(SV_FLAG).ap,
                                                                    scalar2=None, op0=ALU.mult),
                                 reads=[bk.v(), svc(SV_FLAG)], writes=[v_t.v(slice(kb * 8, (kb + 1) * 8))])
                        else:
                            S.op("dve", lambda e: e.tensor_copy(out=dstap, in_=src),
                                 reads=[bk.v()], writes=[v_t.v(slice(kb * 8, (kb + 1) * 8))])
                    return f

                return ([vinit] + [kstep(cc) for cc in (4, 5, 6, 7)] + [krope(slice(NT, CTX)), vstep(2), vstep(3)]
                        + [kstep(cc) for cc in (0, 1, 2, 3)] + [krope(slice(0, NT)), vstep(0), vstep(1)])

            for st in build_steps(0):
                st()
            pti = 0
            oi = 0
            items = []
            step_at = {}
            for h in range(NH):
                base = len(items)
                for c in range(NCH):
                    tiles = [(kt, 0) for kt in range(16 + 4 * c)] + [(16 + 4 * c + i, i) for i in range(4)]
                    for ti, (kt, i0) in enumerate(tiles):
                        items.append((h, c, kt, i0, ti, len(tiles)))
                n_h = len(items) - base
                nxt = build_steps(h + 1) if h + 1 < NH else []
                if nxt:
                    gap = n_h // (len(nxt) + 1)
                    for si, st in enumerate(nxt):
                        step_at.setdefault(base + (si + 1) * gap, []).append(st)
                if l == 0:
                    for at in (3, 38, 73):
                        if mod_pieces:
                            step_at.setdefault(base + at, []).append(mod_piece_step(mod_pieces.pop(0), S_BANKS))
                if h == 4:
                    step_at.setdefault(base + 12, []).append(
                        lambda: wload(Wgb.v(), winv[:, :, 2208:3232], "Wgb"))
                if h == 5:
                    step_at.setdefault(base + 12, []).append(
                        lambda: wload(wpl.v(), wpool_d[l].rearrange("g c d -> c g d"), "wpl"))
                if h == 7:
                    step_at.setdefault(base + 12, []).append(
                        lambda: (wload(Wu2.v(), winv[:, :, 0:512], "Wu2"),
                                 wload(ppl.v(), ppool_d[l].rearrange("(g p) n -> p g n", p=128), "ppl"),
                                 wload(Wga.v(), winv[:, :, 1184:2208], "Wga")))
            n_items = len(items)
            deferred = []
            pend = {}
            obs = {}
            for idx in range(n_items + LA):
                if idx < n_items:
                    h, c, kt, i0, ti, nt_ = items[idx]
                    q0 = i0 * 128
                    n = C - q0
                    sbk = PB[next_bank(S_BANKS)]
                    sv_ = sbk.v(slice(0, n))
                    qv = qT_all.v(h, slice(c * C + q0, (c + 1) * C), p=(0, 96))
                    kv = KT[h % 2].v(slice(kt * 128, (kt + 1) * 128), p=(0, 96))
                    mm_group(sv_, [(kv, qv)])
                    pt = PT[pti % NPT]
                    pti += 1
                    pv = pt.v(slice(0, n))
                    act(pv, sv_, AF.Exp)
                    if kt >= 16 + 4 * c:
                        tt("dve", pt.v(slice(0, 128)), pt.v(slice(0, 128)), tri.v(), ALU.mult)
                    pend[idx] = pv
                    if ti == 0:
                        obs[(h, c)] = (PB[5 + oi % 2], Osb[oi % 2], rdn[oi % 2])
                        oi += 1
                j = idx - LA
                if j >= 0:
                    h, c, kt, i0, ti, nt_ = items[j]
                    lower = h < 4
                    q0 = i0 * 128
                    ob, osb, rden = obs.pop((h, c)) if ti == nt_ - 1 else obs[(h, c)]
                    ov = ob.v(slice(q0, C))
                    vv = VV[h % 2].v(kt)
                    pv = pend.pop(j)
                    S.op("pe", (lambda ov=ov, vv=vv, pv=pv, ti=ti, nt_=nt_: lambda e: e.matmul(
                        ov.ap, vv.ap, pv.ap, start=(ti == 0), stop=(ti == nt_ - 1)))(),
                        reads=[vv, pv], writes=[ov])
                    if ti == nt_ - 1:
                        copy("dve", osb.v(), ob.v())
                        rp = (64, 65) if lower else (0, 1)
                        S.op("dve", (lambda osb=osb, rden=rden, rp=rp: lambda e: e.reciprocal(
                            out=rden.v(p=rp).ap, in_=osb.v(p=rp).ap))(),
                            reads=[osb.v(p=rp)], writes=[rden.v(p=rp)])

                        def fin(h=h, c=c, osb=osb, rden=rden, rp=rp, lower=lower):
                            bcb = PB[next_bank(S_BANKS)]
                            mm_group(bcb.v(), [((ones_f if lower else sel_f).v(p=rp), rden.v(p=rp))])
                            rows = (0, 64) if lower else (64, 128)
                            tt("dve", attn_o.v(h % 4, slice(c * C, (c + 1) * C), p=rows), osb.v(p=rows),
                               bcb.v(p=rows), ALU.mult)
                        deferred.append((idx + 16, fin))
                for st in step_at.get(idx, []):
                    st()
                while deferred and deferred[0][0] <= idx:
                    deferred.pop(0)[1]()
            for _, fn in deferred:
                fn()

            if debug and l == 0:
                S.dma("sp", [lambda e: e.dma_start(out=dbg_ao, in_=attn_o.full.rearrange("p j t -> p (j t)"))],
                      "dbg2", reads=[attn_o.v()], writes=[View(None, ("dram", "dbg2"), (0, 1, 0, 1))])
            o = G_o
            hTg2 = [sb(o + i * 4096, [128, 8, GC], BF16) for i in range(2)]; o += 8192
            gA = sb(o, [128, GC], F32); o += 1024
            gB = sb(o, [128, GC], F32); o += 1024
            hA = sb(o, [128, GC], F32); o += 1024
            hB = sb(o, [128, GC], F32); o += 1024
            uT = sb(o, [128, 4, 16 + GC], F32); o += 4 * (16 + GC) * 4
            wA = sb(o, [128, 4, 16 + GC], F32); o += 4 * (16 + GC) * 4
            wB = sb(o, [128, 4, 16 + GC], F32); o += 4 * (16 + GC) * 4
            pTt = sb(o, [128, 4, GC], BF16); o += 4 * GC * 2
            yTt = sb(o, [128, 4, GC], BF16); o += 4 * GC * 2
            mrg = sb(o, [128, 8, GC], BF16); o += 8 * GC * 2
            sgA = sb(o, [128, GC], F32); o += 1024
            sgB = sb(o, [128, GC], F32); o += 1024
            pav = pattn_d[l].rearrange("(hh j v) n -> hh v j n", hh=2, j=4)
            S.dma("pool", [lambda e, pav=pav: e.dma_start(out=pat.v(p=(0, 64)).ap, in_=pav[0]),
                           lambda e, pav=pav: e.dma_start(out=pat.v(p=(64, 128)).ap, in_=pav[1])], "pat", writes=[pat.v()])
            wload(wo.v(), wout_d[l].rearrange("(k p) n -> p k n", p=128), "wo")
            yT2 = [yTt, sb(o, [128, 4, GC], BF16)]; o += 4 * GC * 2
            sqe = sb(o, [128, 8, GC], BF16); o += 8 * GC * 2
            assert o <= ARENA - 8192, o
            ts("dve", uT.v(slice(None), slice(0, 16)), halo_in.v(), svc(SV_FLAG), ALU.mult)
            NG = NT // GC
            L_ = 16 + GC

            def gsl(gc):
                return slice(gc * GC, (gc + 1) * GC)

            def P0(gc):
                hTg = hTg2[gc % 2]
                gs = gsl(gc)
                h_from(l, 0, lambda k: xT.v(k, gs), rstd1_all.v(gs), lambda k: hTg.v(k), hA.v(), hB.v())

            def P1(gc):
                hTg = hTg2[gc % 2]
                for half in range(2):
                    bk = PB[next_bank()]
                    for gg in range(2):
                        g = half * 2 + gg
                        mm_group(bk.v(slice(gg * GC, (gg + 1) * GC)),
                                 [(Wu2.v(k, slice(g * 128, (g + 1) * 128)), hTg.v(k)) for k in range(8)])
                    S.op("act", lambda e, bk=bk, half=half: e.copy(
                        out=uT.full[:, half * 2:half * 2 + 2, 16:16 + GC],
                        in_=bk.full.rearrange("p (g t) -> p g t", g=2)),
                        reads=[bk.v()], writes=[uT.v(slice(half * 2, half * 2 + 2))])

            def P2a(gc):
                E = "pool"
                tt(E, wA.v(slice(0, 4), slice(1, L_)), uT.v(slice(0, 4), slice(1, L_)), uT.v(slice(0, 4), slice(0, L_ - 1)), ALU.add)
                tt(E, wB.v(slice(1, 4), slice(3, L_)), wA.v(slice(1, 4), slice(3, L_)), wA.v(slice(1, 4), slice(1, L_ - 2)), ALU.add)
                tt(E, wA.v(slice(2, 4), slice(7, L_)), wB.v(slice(2, 4), slice(7, L_)), wB.v(slice(2, 4), slice(3, L_ - 4)), ALU.add)
                tt(E, wB.v(slice(3, 4), slice(15, L_)), wA.v(slice(3, 4), slice(15, L_)), wA.v(slice(3, 4), slice(7, L_ - 8)), ALU.add)

            def P2a2(gc):
                for g in range(4):
                    w = 2 << g
                    cur = wA if g % 2 == 0 else wB
                    stt(pTt.v(g), cur.v(g, slice(16, L_)), 1.0 / w, uT.v(g, slice(16, L_)), ALU.mult, ALU.subtract)
                    if gc == 0:
                        tt("dve", gA.v(slice(0, 16)), cur.v(g, slice(16, 32)), svc(SV_INVCNT + g * 16, 16), ALU.mult)
                        tt("dve", pTt.v(g, slice(0, 16)), gA.v(slice(0, 16)), uT.v(g, slice(16, 32)), ALU.subtract)
                S.op("dve", lambda e: e.tensor_copy(out=uT.full[:, :, 0:16], in_=uT.full[:, :, GC:GC + 16]),
                     reads=[uT.v()], writes=[uT.v()])

            def P2b(gc):
                yT = yT2[gc % 2]
                bk = PB[next_bank()]
                bk2 = PB[next_bank()]
                for g in range(4):
                    dst = (bk if g < 2 else bk2).v(slice((g % 2) * GC, (g % 2 + 1) * GC))
                    mm_group(dst, [(wpl.v(g), pTt.v(g))])
                for g in range(4):
                    dst = (bk if g < 2 else bk2).v(slice((g % 2) * GC, (g % 2 + 1) * GC))
                    ts("dve", yT.v(g), dst, svc(SV_PSC + l * 4 + g), ALU.mult)

            def M1(gc, mrange):
                hTg = hTg2[gc % 2]
                yT = yT2[gc % 2]
                gs = gsl(gc)
                for m in mrange:
                    ms = slice(m * 128, (m + 1) * 128)
                    b1 = PB[next_bank()]
                    b2 = PB[next_bank()]
                    ga = b1.v(slice(0, GC))
                    gb = b1.v(slice(GC, 2 * GC))
                    ya = b2.v(slice(0, GC))
                    yb = b2.v(slice(GC, 2 * GC))
                    mm_group(ga, [(Wga.v(k, ms), hTg.v(k)) for k in range(8)])
                    mm_group(gb, [(Wgb.v(k, ms), hTg.v(k)) for k in range(8)])
                    mm_group(ya, [(ppl.v(g, ms), yT.v(g)) for g in range(4)])
                    mm_group(yb, [(pat.v(j, ms), attn_o.v(j, gs)) for j in range(4)])
                    act(sg2.v(), b1.v(), AF.Sigmoid)
                    tt("dve", g2.v(), sg2.v(), b2.v(), ALU.mult)
                    tt("dve", mrg.v(m), gA.v(), gB.v(), ALU.add)

            def M2(gc):
                gs = gsl(gc)
                for m in range(8):
                    ms = slice(m * 128, (m + 1) * 128)
                    bk = PB[next_bank()]
                    ov = bk.v(slice(0, GC))
                    mm_group(ov, [(wo.v(k, ms), mrg.v(k)) for k in range(8)])
                    stt(xT.v(m, gs), ov, modT.v(l, slice(16 + m, 17 + m)), xT.v(m, gs), ALU.mult, ALU.add)

            def FsA(gc):
                gs = gsl(gc)
                for k in range(8):
                    act(sqe.v(k), xT.v(k, gs), AF.Square)

            def FsB(gc):
                gs = gsl(gc)
                bk = PB[next_bank()]
                st = bk.v(slice(0, GC))
                mm_group(st, [(ones_b.v(), sqe.v(k)) for k in range(8)])
                rstd_from(st, rstd1_all.v(gs), D, sgA.v())

            assert gB.off == gA.off + GC * 4 and sgB.off == sgA.off + GC * 4
            g2 = sb(gA.off, [128, 2 * GC], F32)
            sg2 = sb(sgA.off, [128, 2 * GC], F32)
            P0(0)
            P1(0)
            P2a(0)
            P2a2(0)
            P2b(0)
            for gc in range(NG):
                M1(gc, range(0, 2))
                if gc + 1 < NG:
                    P0(gc + 1)
                if gc >= 1:
                    FsA(gc - 1)
                M1(gc, range(2, 4))
                if gc + 1 < NG:
                    P1(gc + 1)
                    P2a(gc + 1)
                M1(gc, range(4, 6))
                if gc >= 1:
                    FsB(gc - 1)
                M1(gc, range(6, 8))
                if gc + 1 < NG:
                    P2a2(gc + 1)
                M2(gc)
                if gc + 1 < NG:
                    P2b(gc + 1)
            FsA(NG - 1)
            FsB(NG - 1)

            h2 = sb(R_AO, [128, 8, NT], BF16)
            o = R_AO + 32768
            W1 = [sb(o + i * 16384, [128, 8, 512], BF16) for i in range(2)]
            W2 = [sb(o + 8192 + i * 16384, [128, 4, D], BF16) for i in range(2)]
            o += 32768
            hid = [sb(o + i * 4096, [128, 4, C], BF16) for i in range(2)]; o += 8192
            assert o <= R_X
            o = R_X + 32768
            sqf = sb(o, [128, 8, C], BF16); o += 8192
            fA = sb(o, [128, C], F32); o += 2048
            fB = sb(o, [128, C], F32); o += 2048
            frs = sb(o, [128, NT], F32); o += 8192
            rl = [sb(o + i * 2048, [128, C], F32) for i in range(2)]; o += 4096
            assert o <= ARENA
            w1v = wff1_d[l].rearrange("(k p) n -> p k n", p=128)
            w2v = wff2_d[l].rearrange("(j p) n -> p j n", p=128)

            def ffload(g):
                wload(W1[g % 2].v(), w1v[:, :, g * 512:(g + 1) * 512], f"W1_{g % 2}")
                wload(W2[g % 2].v(), w2v[:, g * 4:(g + 1) * 4, :], f"W2_{g % 2}")

            for c in range(NCH):
                cs = slice(c * C, (c + 1) * C)
                h_from(l, 1, lambda k: xT.v(k, cs), rstd1_all.v(cs), lambda k: h2.v(k, cs), fA.v(), fB.v())
            ffload(0)
            ri = [0]

            def ff_up(g, c):
                w1 = W1[g % 2]
                cs = slice(c * C, (c + 1) * C)
                hd = hid[(g * NCH + c) % 2]
                for hc in range(4):
                    bk = PB[next_bank()]
                    mm_group(bk.v(), [(w1.v(k, slice(hc * 128, (hc + 1) * 128)), h2.v(k, cs)) for k in range(8)])
                    r = rl[ri[0] % 2]
                    ri[0] += 1
                    act(r.v(), bk.v(), AF.Relu)
                    tt("dve", hd.v(hc), r.v(), r.v(), ALU.mult)

            def ff_down(g, c):
                w2 = W2[g % 2]
                cs = slice(c * C, (c + 1) * C)
                hd = hid[(g * NCH + c) % 2]
                for m in range(8):
                    bk = PB[next_bank()]
                    mm_group(bk.v(), [(w2.v(hc, slice(m * 128, (m + 1) * 128)), hd.v(hc)) for hc in range(4)])
                    stt(xT.v(m, cs), bk.v(), modT.v(l, slice(40 + m, 41 + m)), xT.v(m, cs), ALU.mult, ALU.add)

            def early_stats(c):
                cs = slice(c * C, (c + 1) * C)
                for k in range(8):
                    act(sqf.v(k), xT.v(k, cs), AF.Square)
                bk = PB[next_bank()]
                mm_group(bk.v(), [(ones_b.v(), sqf.v(k)) for k in range(8)])
                rstd_from(bk.v(), rstd1_all.v(cs), D, fA.v())

            seq = [(g, c) for g in range(FFG) for c in range(NCH)]
            loaded = {0}
            ff_up(*seq[0])
            for i, (g, c) in enumerate(seq):
                if c == 0 and g + 1 < FFG and (g + 1) not in loaded:
                    ffload(g + 1)
                    loaded.add(g + 1)
                if i + 1 < len(seq):
                    ff_up(*seq[i + 1])
                ff_down(g, c)
                if g == FFG - 1 and c >= 1:
                    early_stats(c - 1)
            early_stats(NCH - 1)

        o = R_AO
        osq = sb(o, [128, 8, C], BF16); o += 8192
        ors = sb(o, [128, C], F32); o += 2048
        otmp = sb(o, [128, C], F32); o += 2048
        obuf = [sb(o + i * 16384, [128, 8, C], F32) for i in range(2)]; o += 32768
        outv = out_d.rearrange("(k p) t -> p k t", p=128)
        out_tok = None
        out_toks = []
        for c in range(NCH):
            cs = slice(c * C, (c + 1) * C)
            ob_ = obuf[c % 2]
            for k in range(8):
                stt(ob_.v(k), xT.v(k, cs), svc(SV_FING + k), rstd1_all.v(cs), ALU.mult, ALU.mult)
            out_tok = S.dma("sp", [(lambda c=c, ob_=ob_: lambda e: e.dma_start(out=outv[:, :, c * C:(c + 1) * C], in_=ob_.v().ap))()],
                            f"outst{c}", reads=[ob_.v()], writes=[View(None, ("dram", "out"), (0, 1, c, c + 1))])
            out_toks.append(out_tok)
        for tk in out_toks:
            S.wait_tok("sp", tk)
        S.emit(block)
    return nc


_NC_CACHE = {}


def _inv_freq():
    return (np.float32(10000.0) ** (-np.arange(0, 32, 2, dtype=np.float32) / np.float32(32))).astype(np.float32)


def kernel(x, c, positions, ln1_g, ln2_g, w_ada, b_ada, w_in, q_norm_g, w_uq, kv_norm_g, w_uk, w_uv,
           w_pool, pool_scale, p_pool, p_attn, w_out, w_ff1, w_ff2, final_g):
    f32 = np.float32
    x = np.asarray(x, f32)
    c = np.asarray(c, f32)
    positions = np.asarray(positions, np.int32)
    B, S_, _ = x.shape
    dbg = _NC_CACHE.get("debug", False)
    if "nc" not in _NC_CACHE:
        _NC_CACHE["nc"] = build_nc(debug=dbg)
    nc = _NC_CACHE["nc"]

    def fm(v, n):
        return np.asarray(v, f32).reshape(n, 128).T

    tri = (np.arange(128)[None, :] >= np.arange(128)[:, None]).astype(ml_dtypes.bfloat16)
    invf = _inv_freq()
    shared = {
        "tri": tri,
        "w_ada": np.ascontiguousarray(np.asarray(w_ada, f32)),
        "w_in": np.ascontiguousarray(np.asarray(w_in, f32)),
        "w_uq": np.ascontiguousarray(np.asarray(w_uq, f32).reshape(DEPTH, 384, 768)),
        "w_uk": np.ascontiguousarray(np.asarray(w_uk, f32).reshape(DEPTH, 256, 512)),
        "w_uv": np.ascontiguousarray(np.asarray(w_uv, f32).reshape(DEPTH, 256, 512)),
        "w_pool": np.ascontiguousarray(np.asarray(w_pool, f32)),
        "p_pool": np.ascontiguousarray(np.asarray(p_pool, f32)),
        "p_attn": np.ascontiguousarray(np.asarray(p_attn, f32)),
        "w_out": np.ascontiguousarray(np.asarray(w_out, f32)),
        "w_ff1": np.ascontiguousarray(np.asarray(w_ff1, f32)),
        "w_ff2": np.ascontiguousarray(np.asarray(w_ff2, f32)),
    }
    in_maps = []
    for core in range(8):
        b, half = core // 2, core % 2
        t0 = half * NT
        sv = np.zeros((128, NV), f32)
        for l in range(DEPTH):
            sv[:, SV_LN1 + l * 8:SV_LN1 + l * 8 + 8] = fm(ln1_g[l], 8)
            sv[:, SV_LN2 + l * 8:SV_LN2 + l * 8 + 8] = fm(ln2_g[l], 8)
            sv[:, SV_BADA + l * 48:SV_BADA + l * 48 + 48] = fm(b_ada[l], 48)
            sv[:, SV_QNG + l * 3:SV_QNG + l * 3 + 3] = fm(q_norm_g[l], 3)
            sv[:, SV_KVG + l * 2:SV_KVG + l * 2 + 2] = fm(kv_norm_g[l], 2)
            sv[:, SV_PSC + l * 4:SV_PSC + l * 4 + 4] = fm(pool_scale[l], 4)
        sv[:, SV_FING:SV_FING + 8] = fm(final_g, 8)
        sv[64:96, SV_INVF] = np.tile(invf, 2)
        sv[:, SV_FLAG] = float(half)
        for g in range(4):
            w = 2 << g
            tpos = t0 + np.arange(16)
            cnt = np.minimum(tpos + 1, w).astype(f32)
            sv[:, SV_INVCNT + g * 16:SV_INVCNT + g * 16 + 16] = (f32(1.0) / cnt)[None, :]
        sv[:, SV_C:SV_C + 8] = fm(c[b], 8)
        m = dict(shared)
        m["xT"] = np.ascontiguousarray(x[b, t0:t0 + NT, :].T)
        m["pos"] = np.ascontiguousarray(positions[b, t0:t0 + NT].reshape(1, NT))
        m["smallv"] = sv
        in_maps.append(m)
    res = run_bass_kernel_spmd(nc, in_maps, core_ids=list(range(8)))
    _NC_CACHE["res"] = res.results if dbg else None
    out = np.empty((B, S_, D), f32)
    for core in range(8):
        b, half = core // 2, core % 2
        t0 = half * NT
        out[b, t0:t0 + NT, :] = np.asarray(res.results[core]["outT"], f32).T
    return out
```
